# Optimizing a Trainium2 kernel written in Bass

```python
import math
import jax, jax.numpy as jnp
from jax import lax
import numpy as np

D_MODEL = 1024
BATCH = 16
SEQ = 2048
DEPTH = 1

HEAD_DIM = 64
NSA_HEADS = 8
NSA_KV_GROUPS = 2
NSA_REP = NSA_HEADS // NSA_KV_GROUPS
SB_HEADS = 8
CMP_BLOCK = 32
CMP_STRIDE = 16
CMP_HIDDEN = 2 * HEAD_DIM
SEL_BLOCK = 64
SEL_TOPK = 8
WINDOW = 512
Q_BLOCK = 128
SEL_Q_CHUNK = 64
D_FF = -(-8 * D_MODEL // (3 * 256)) * 256
RMS_EPS = 1e-6
NEG_INF = -1e30
FORCE_PRIORITY = 1e4
ATTN_SCALE = 1.0 / math.sqrt(HEAD_DIM)

NSA_Q_DIM = NSA_HEADS * HEAD_DIM
NSA_KV_DIM = 6 * NSA_KV_GROUPS * HEAD_DIM
NSA_GATE_DIM = 3 * NSA_HEADS
SB_QKV_DIM = 3 * SB_HEADS * HEAD_DIM
MERGE_DIM = 2 * D_MODEL
D_IN = NSA_Q_DIM + NSA_KV_DIM + NSA_GATE_DIM + SB_QKV_DIM + MERGE_DIM
IN_SPLITS = (NSA_Q_DIM, NSA_Q_DIM + NSA_KV_DIM, NSA_Q_DIM + NSA_KV_DIM + NSA_GATE_DIM,
             NSA_Q_DIM + NSA_KV_DIM + NSA_GATE_DIM + SB_QKV_DIM)

kernel_name = "hybrid_nsa_stickbreaking_gated_block"


def rmsnorm(x, g):
    xf = x.astype(jnp.float32)
    y = xf * lax.rsqrt(jnp.mean(xf * xf, axis=-1, keepdims=True) + RMS_EPS)
    return (y * g.astype(jnp.float32)).astype(x.dtype)


def modulate(h, shift, scale):
    return h * (1.0 + scale) + shift


def alibi_slopes():
    i = jnp.arange(1, NSA_HEADS + 1, dtype=jnp.float32)
    return jnp.exp2(-8.0 * i / NSA_HEADS).reshape(NSA_KV_GROUPS, NSA_REP)


def compress(k, pos_emb, w1, w2):
    n_cmp = (k.shape[2] - CMP_BLOCK) // CMP_STRIDE + 1
    idx = jnp.arange(n_cmp)[:, None] * CMP_STRIDE + jnp.arange(CMP_BLOCK)[None, :]
    blocks = k[:, :, idx] + pos_emb
    flat = blocks.reshape(*blocks.shape[:3], CMP_BLOCK * HEAD_DIM)
    return jax.nn.gelu(flat @ w1) @ w2


def nsa_compressed(q, kc, vc, slopes):
    S = q.shape[3]
    n = kc.shape[2]
    t = jnp.arange(S)
    end = jnp.arange(n) * CMP_STRIDE + CMP_BLOCK - 1
    dist = (t[:, None] - end[None, :]).astype(jnp.float32)
    valid = dist >= 0
    s = jnp.einsum('bgrqd,bgnd->bgrqn', q.astype(jnp.float32), kc.astype(jnp.float32)) * ATTN_SCALE
    s = s - slopes[:, :, None, None] * dist
    s = jnp.where(valid, s, NEG_INF)
    p = jax.nn.softmax(s, axis=-1)
    p = jnp.where(valid.any(-1)[:, None], p, 0.0)
    o = jnp.einsum('bgrqn,bgnd->bgrqd', p.astype(vc.dtype), vc)
    return o, p.sum(axis=2)


def nsa_select_indices(p_cmp, S):
    n_cmp = p_cmp.shape[-1]
    n_sel = S // SEL_BLOCK
    c_start = jnp.arange(n_cmp) * CMP_STRIDE
    c_end = c_start + CMP_BLOCK - 1
    s_start = jnp.arange(n_sel) * SEL_BLOCK
    s_end = s_start + SEL_BLOCK - 1
    overlap = ((c_start[:, None] <= s_end[None, :]) & (c_end[:, None] >= s_start[None, :])).astype(jnp.float32)
    p_slc = jnp.einsum('bgqn,nj->bgqj', p_cmp, overlap)
    cur = (jnp.arange(S) // SEL_BLOCK)[:, None]
    j = jnp.arange(n_sel)[None, :]
    valid = j <= cur
    forced = (j == 0) | (j == cur) | (j == cur - 1)
    prio = jnp.where(valid, p_slc, NEG_INF)
    prio = jnp.where(forced & valid, FORCE_PRIORITY, prio)
    _, idx = lax.top_k(prio, min(SEL_TOPK, n_sel))
    return idx


def nsa_selected(q, k, v, idx, slopes):
    B, G, R, S, hd = q.shape
    n = idx.shape[-1]
    kb = k.reshape(B, G, S // SEL_BLOCK, SEL_BLOCK, hd)
    vb = v.reshape(B, G, S // SEL_BLOCK, SEL_BLOCK, hd)
    nc = S // SEL_Q_CHUNK
    q_c = q.reshape(B, G, R, nc, SEL_Q_CHUNK, hd).transpose(3, 0, 1, 2, 4, 5)
    idx_c = idx.reshape(B, G, nc, SEL_Q_CHUNK, n).transpose(2, 0, 1, 3, 4)
    t_c = jnp.arange(S).reshape(nc, SEL_Q_CHUNK)
    bi = jnp.arange(B)[:, None, None, None]
    gi = jnp.arange(G)[None, :, None, None]

    def chunk(args):
        qq, ii, tt = args
        ks = kb[bi, gi, ii]
        vs = vb[bi, gi, ii]
        pos = ii[..., None] * SEL_BLOCK + jnp.arange(SEL_BLOCK)
        dist = (tt[:, None, None] - pos).astype(jnp.float32)
        s = jnp.einsum('bgrqd,bgqnld->bgrqnl', qq.astype(jnp.float32), ks.astype(jnp.float32)) * ATTN_SCALE
        s = s - slopes[:, :, None, None, None] * dist[:, :, None]
        s = jnp.where((dist >= 0)[:, :, None], s, NEG_INF)
        p = jax.nn.softmax(s, axis=(-2, -1))
        return jnp.einsum('bgrqnl,bgqnld->bgrqd', p.astype(v.dtype), vs)

    o = lax.map(chunk, (q_c, idx_c, t_c))
    return o.transpose(1, 2, 3, 0, 4, 5).reshape(B, G, R, S, hd)


def nsa_window(q, k, v, slopes):
    B, G, R, S, hd = q.shape
    kp = jnp.pad(k, ((0, 0), (0, 0), (WINDOW, 0), (0, 0)))
    vp = jnp.pad(v, ((0, 0), (0, 0), (WINDOW, 0), (0, 0)))
    nq = S // Q_BLOCK
    q_c = q.reshape(B, G, R, nq, Q_BLOCK, hd).transpose(3, 0, 1, 2, 4, 5)
    starts = jnp.arange(nq) * Q_BLOCK
    span = WINDOW + Q_BLOCK

    def blk(args):
        qq, q0 = args
        kk = lax.dynamic_slice_in_dim(kp, q0, span, axis=2)
        vv = lax.dynamic_slice_in_dim(vp, q0, span, axis=2)
        t = q0 + jnp.arange(Q_BLOCK)
        s_pos = q0 - WINDOW + jnp.arange(span)
        dist = t[:, None] - s_pos[None, :]
        mask = (dist >= 0) & (dist < WINDOW) & (s_pos[None, :] >= 0)
        s = jnp.einsum('bgrqd,bgkd->bgrqk', qq.astype(jnp.float32), kk.astype(jnp.float32)) * ATTN_SCALE
        s = s - slopes[:, :, None, None] * dist.astype(jnp.float32)
        s = jnp.where(mask, s, NEG_INF)
        p = jax.nn.softmax(s, axis=-1)
        return jnp.einsum('bgrqk,bgkd->bgrqd', p.astype(v.dtype), vv)

    o = lax.map(blk, (q_c, starts))
    return o.transpose(1, 2, 3, 0, 4, 5).reshape(B, G, R, S, hd)


def stick_breaking(q, k, v):
    B, H, S, hd = q.shape
    nq = S // Q_BLOCK
    q_c = q.reshape(B, H, nq, Q_BLOCK, hd).transpose(2, 0, 1, 3, 4)
    starts = jnp.arange(nq) * Q_BLOCK
    kf = k.astype(jnp.float32)
    s_pos = jnp.arange(S)

    def blk(args):
        qq, q0 = args
        t = q0 + jnp.arange(Q_BLOCK)
        mask = s_pos[None, :] < t[:, None]
        z = jnp.einsum('bhqd,bhkd->bhqk', qq.astype(jnp.float32), kf) * ATTN_SCALE
        log_keep = jnp.where(mask, -jax.nn.softplus(z), 0.0)
        later = lax.cumsum(log_keep, axis=3, reverse=True) - log_keep
        a = jnp.where(mask, jnp.exp(jax.nn.log_sigmoid(z) + later), 0.0)
        return jnp.einsum('bhqk,bhkd->bhqd', a.astype(v.dtype), v)

    o = lax.map(blk, (q_c, starts))
    return o.transpose(1, 2, 0, 3, 4).reshape(B, H, S, hd)


def hybrid_mixer(h, w_in, cmp_pos_k, cmp_w1_k, cmp_w2_k, cmp_pos_v, cmp_w1_v, cmp_w2_v,
                 w_proj_a, w_proj_b, w_out):
    B, S, _ = h.shape
    G, R, hd = NSA_KV_GROUPS, NSA_REP, HEAD_DIM
    proj = h @ w_in
    q_a, kv_a, gate_a, qkv_b, merge = jnp.split(proj, IN_SPLITS, axis=-1)

    q_a = q_a.reshape(B, S, G, R, hd).transpose(0, 2, 3, 1, 4)
    kv_a = kv_a.reshape(B, S, 6, G, hd).transpose(2, 0, 3, 1, 4)
    k_cmp, v_cmp, k_sel, v_sel, k_win, v_win = (kv_a[i] for i in range(6))
    g_a = jax.nn.sigmoid(gate_a.reshape(B, S, G, R, 3).transpose(0, 2, 3, 1, 4))
    slopes = alibi_slopes()
    kc = compress(k_cmp, cmp_pos_k, cmp_w1_k, cmp_w2_k)
    vc = compress(v_cmp, cmp_pos_v, cmp_w1_v, cmp_w2_v)
    o_cmp, p_cmp = nsa_compressed(q_a, kc, vc, slopes)
    idx = nsa_select_indices(p_cmp, S)
    o_sel = nsa_selected(q_a, k_sel, v_sel, idx, slopes)
    o_win = nsa_window(q_a, k_win, v_win, slopes)
    o_a = g_a[..., 0:1] * o_cmp + g_a[..., 1:2] * o_sel + g_a[..., 2:3] * o_win
    y_a = o_a.transpose(0, 3, 1, 2, 4).reshape(B, S, NSA_Q_DIM) @ w_proj_a

    qkv_b = qkv_b.reshape(B, S, 3, SB_HEADS, hd).transpose(2, 0, 3, 1, 4)
    o_b = stick_breaking(qkv_b[0], qkv_b[1], qkv_b[2])
    y_b = o_b.transpose(0, 2, 1, 3).reshape(B, S, SB_HEADS * hd) @ w_proj_b

    merge = merge.reshape(B, S, 2, D_MODEL)
    y = jax.nn.sigmoid(merge[:, :, 0]) * y_a + jax.nn.sigmoid(merge[:, :, 1]) * y_b
    return y @ w_out


def swiglu(h, w_gate, w_up, w_down):
    return (jax.nn.silu(h @ w_gate) * (h @ w_up)) @ w_down


def setup_inputs(seed: int = 0) -> dict:
    key = jax.random.key(seed)
    ks = jax.random.split(key, 20)

    def dense(k, shape, fan_in):
        return jax.random.normal(k, shape, jnp.float32) * (fan_in ** -0.5)

    def gain(k, shape):
        return 1.0 + 0.05 * jax.random.normal(k, shape, jnp.float32)

    L = DEPTH
    cmp_in = CMP_BLOCK * HEAD_DIM
    return {
        "x": jax.random.normal(ks[0], (BATCH, SEQ, D_MODEL), jnp.float32),
        "c": jax.random.normal(ks[1], (BATCH, D_MODEL), jnp.float32),
        "w_ada": dense(ks[2], (L, D_MODEL, 6 * D_MODEL), D_MODEL),
        "b_ada": 0.02 * jax.random.normal(ks[3], (L, 6 * D_MODEL), jnp.float32),
        "norm_mix_g": gain(ks[4], (L, D_MODEL)),
        "w_in": dense(ks[5], (L, D_MODEL, D_IN), D_MODEL),
        "cmp_pos_k": 0.02 * jax.random.normal(ks[6], (L, CMP_BLOCK, HEAD_DIM), jnp.float32),
        "cmp_w1_k": dense(ks[7], (L, cmp_in, CMP_HIDDEN), cmp_in),
        "cmp_w2_k": dense(ks[8], (L, CMP_HIDDEN, HEAD_DIM), CMP_HIDDEN),
        "cmp_pos_v": 0.02 * jax.random.normal(ks[9], (L, CMP_BLOCK, HEAD_DIM), jnp.float32),
        "cmp_w1_v": dense(ks[10], (L, cmp_in, CMP_HIDDEN), cmp_in),
        "cmp_w2_v": dense(ks[11], (L, CMP_HIDDEN, HEAD_DIM), CMP_HIDDEN),
        "w_proj_a": dense(ks[12], (L, NSA_Q_DIM, D_MODEL), NSA_Q_DIM),
        "w_proj_b": dense(ks[13], (L, SB_HEADS * HEAD_DIM, D_MODEL), SB_HEADS * HEAD_DIM),
        "w_out": dense(ks[14], (L, D_MODEL, D_MODEL), D_MODEL),
        "norm_ffn_g": gain(ks[15], (L, D_MODEL)),
        "w_ffn_gate": dense(ks[16], (L, D_MODEL, D_FF), D_MODEL),
        "w_ffn_up": dense(ks[17], (L, D_MODEL, D_FF), D_MODEL),
        "w_ffn_down": dense(ks[18], (L, D_FF, D_MODEL), D_FF),
        "norm_final_g": gain(ks[19], (D_MODEL,)),
    }


def reference(x, c, w_ada, b_ada, norm_mix_g, w_in, cmp_pos_k, cmp_w1_k, cmp_w2_k,
              cmp_pos_v, cmp_w1_v, cmp_w2_v, w_proj_a, w_proj_b, w_out, norm_ffn_g,
              w_ffn_gate, w_ffn_up, w_ffn_down, norm_final_g):
    for l in range(DEPTH):
        mod = jax.nn.silu(c) @ w_ada[l] + b_ada[l]
        sh1, sc1, g1, sh2, sc2, g2 = jnp.split(mod[:, None, :], 6, axis=-1)
        h = modulate(rmsnorm(x, norm_mix_g[l]), sh1, sc1)
        x = x + g1 * hybrid_mixer(h, w_in[l], cmp_pos_k[l], cmp_w1_k[l], cmp_w2_k[l],
                                  cmp_pos_v[l], cmp_w1_v[l], cmp_w2_v[l],
                                  w_proj_a[l], w_proj_b[l], w_out[l])
        h = modulate(rmsnorm(x, norm_ffn_g[l]), sh2, sc2)
        x = x + g2 * swiglu(h, w_ffn_gate[l], w_ffn_up[l], w_ffn_down[l])
    return rmsnorm(x, norm_final_g)
```

```python
import numpy as np
from contextlib import ExitStack
import concourse.bass as bass
import concourse.mybir as mybir
from concourse.bass_utils import run_bass_kernel_spmd

F32 = mybir.dt.float32
BF16 = mybir.dt.bfloat16
AF = mybir.ActivationFunctionType
ALU = mybir.AluOpType
AX = mybir.AxisListType

S = 2048
D = 1024
KC = 8
NT = 16
TB = 512
NB = S // TB
DIN = 4888
DFF = 2816
NFF = 22
EPS = 1e-6
NCORES = 8
C_QA = 0
C_KV = 512
C_GA = 1280
C_SB = 1304
C_MG = 2840
BIG = 1.0e9
SUB_ENG = "dve"


class Builder:
    def __init__(self, nc, es, n_dma_sp=20, n_dma_pool=20):
        self.nc = nc
        self.es = es
        self.engs = {"pe": nc.tensor, "act": nc.scalar, "dve": nc.vector, "pool": nc.gpsimd, "sp": nc.sync}
        self.sem = {}
        self.cnt = {}
        for e in ("pe", "act", "dve", "pool"):
            self.sem[e] = es.enter_context(nc.semaphore("c_" + e))
            self.cnt[e] = 0
        self.dsems = {"sp": [], "pool": []}
        for q, n in (("sp", n_dma_sp), ("pool", n_dma_pool)):
            for i in range(n):
                name = "d_%s%d" % (q, i)
                self.sem[name] = es.enter_context(nc.semaphore(name))
                self.cnt[name] = 0
                self.dsems[q].append(name)
        self.drr = {"sp": 0, "pool": 0}
        self.seen = {e: {} for e in self.engs}
        self.res = {}
        self.nwaits = 0
        self.ninst = 0

    def _need(self, eng, reads, writes):
        need = {}

        def add(ev, same_ok):
            if ev is None:
                return
            s, v = ev
            if same_ok and s == eng:
                return
            if need.get(s, 0) < v:
                need[s] = v

        for k in reads:
            r = self.res.get(k)
            if r is not None:
                add(r[0], False)
        for k in writes:
            r = self.res.get(k)
            if r is not None:
                add(r[0], True)
                for s, v in r[1].items():
                    add((s, v), True)
        out = []
        seen = self.seen[eng]
        for s, v in need.items():
            if seen.get(s, 0) < v:
                out.append((s, v))
                seen[s] = v
        return out

    def _emit_waits(self, eng, waits):
        e = self.engs[eng]
        for s, v in waits:
            e.wait_ge(self.sem[s], v)
            self.nwaits += 1

    def _post(self, ev, reads, writes):
        for k in reads:
            r = self.res.get(k)
            if r is None:
                r = [None, {}]
                self.res[k] = r
            if r[1].get(ev[0], 0) < ev[1]:
                r[1][ev[0]] = ev[1]
        for k in writes:
            self.res[k] = [ev, {}]

    def last(self, key):
        r = self.res.get(key)
        return None if r is None else r[0]

    def op(self, eng, fn, reads=(), writes=()):
        waits = self._need(eng, reads, writes)
        self._emit_waits(eng, waits)
        ins = fn(self.engs[eng])
        self.cnt[eng] += 1
        ins.then_inc(self.sem[eng], 1)
        self._post((eng, self.cnt[eng]), reads, writes)
        self.ninst += 1
        return ins

    def dma(self, q, pairs, reads=(), writes=()):
        sems = self.dsems[q]
        i = self.drr[q]
        self.drr[q] = (i + 1) % len(sems)
        name = sems[i]
        waits = self._need(q, reads, writes)
        prev = self.cnt[name]
        if self.seen[q].get(name, 0) < prev:
            waits.append((name, prev))
            self.seen[q][name] = prev
        self._emit_waits(q, waits)
        e = self.engs[q]
        for o, i_ in pairs:
            e.dma_start(out=o, in_=i_).then_inc(self.sem[name], 16)
            self.ninst += 1
        self.cnt[name] = prev + 16 * len(pairs)
        self._post((name, self.cnt[name]), reads, writes)

    def finish(self, eng="sp"):
        e = self.engs[eng]
        for name in self.dsems["sp"] + self.dsems["pool"]:
            if self.cnt[name] > 0:
                e.wait_ge(self.sem[name], self.cnt[name])
        for c in ("pe", "act", "dve", "pool"):
            if self.cnt[c] > 0:
                e.wait_ge(self.sem[c], self.cnt[c])


class Rot:
    def __init__(self, items):
        self.items = items
        self.i = 0

    def next(self):
        it = self.items[self.i]
        self.i = (self.i + 1) % len(self.items)
        return it


def sb_ap(t, off, dims):
    return bass.AP(t, off, dims)


def build(nseq=2, debug=(), stop_after=None, branches=("cmp", "win", "sel")):
    nc = bass.Bass("TRN2", target_bir_lowering=False)
    dbg_outs = {}

    def din(name, shape):
        return nc.dram_tensor(name, list(shape), F32, kind="ExternalInput").ap()

    x_d = din("x", [nseq, S, D])
    csT_d = din("csT", [128, 16])
    w_ada_d = din("w_ada", [D, 6 * D])
    b_ada_d = din("b_ada", [1, 6 * D])
    gmixT_d = din("gmixT", [128, 8])
    gffnT_d = din("gffnT", [128, 8])
    gfin_d = din("gfin", [1, D])
    w_in_d = din("w_in", [D, DIN])
    w_pa_d = din("w_proj_a", [512, D])
    w_pb_d = din("w_proj_b", [512, D])
    w_out_d = din("w_out", [D, D])
    w_fg_d = din("w_ffn_gate", [D, DFF])
    w_fu_d = din("w_ffn_up", [D, DFF])
    w_fd_d = din("w_ffn_down", [DFF, D])
    ident_d = din("ident", [128, 128])
    sel2_d = din("sel2", [2, 256])
    i2_d = din("i2", [2, 2])
    tri_lt_d = din("tri_lt", [128, 128])
    tsel_d = din("tsel", [128, 2048])
    twin_d = din("twin", [128, 640])
    tcmp_d = din("tcmp", [128, 16 * 128])
    selmul_d = din("selmul", [128, 16 * 32])
    seladd_d = din("seladd", [128, 16 * 32])
    rv0_d = din("rv0", [128, 1])
    w1k_d = din("w1k", [128, 32 * 128])
    w1v_d = din("w1v", [128, 32 * 128])
    posk_d = din("posk", [128, 32])
    posv_d = din("posv", [128, 32])
    w2k_d = din("w2k", [128, 64])
    w2v_d = din("w2v", [128, 64])
    tri_ge_d = din("tri_ge", [128, 128])
    out_d = nc.dram_tensor("out", [nseq, S, D], F32, kind="ExternalOutput").ap()

    with ExitStack() as es:
        bld = Builder(nc, es)
        op = bld.op
        dma = bld.dma

        uid = [0]

        def SB(name, shape, dt=F32, stack=es):
            uid[0] += 1
            return stack.enter_context(nc.sbuf_tensor("s%d_%s" % (uid[0], name), list(shape), dt))

        def dbg(name, ap, shape, reads):
            if name not in debug:
                return
            o = nc.dram_tensor("dbg_" + name, list(shape), ap.dtype, kind="ExternalOutput").ap()
            dbg_outs[name] = o
            dma("sp", [(o, ap)], reads=reads, writes=())

        psum = [es.enter_context(nc.psum_tensor("ps%d" % i, [128, 512], F32)) for i in range(8)]

        def PSF(i):
            return psum[i][:]

        def PSB(i):
            return psum[i][:].bitcast(BF16)

        def pk(i):
            return "ps%d" % i

        identb = SB("identb", [128, 128], BF16)
        identf = SB("identf", [128, 128], F32)
        modT = SB("modT", [128, 64], F32)
        aT = SB("aT", [128, 32], F32)
        shT = SB("shT", [128, 32], F32)
        gbc = SB("gbc", [128, 4 * D], F32)
        gfin = SB("gfin_bc", [128, D], F32)
        tri_lt = SB("tri_lt", [128, 128], F32)
        tri_ge = SB("tri_ge", [128, 128], F32)
        ones512 = SB("ones512", [128, 512], F32)
        w_in_v = w_in_d.rearrange("(c p) n -> p c n", p=128)
        dma("sp", [(tri_lt[:], tri_lt_d), (tri_ge[:], tri_ge_d)], writes=["tri"])
        op("dve", lambda e: e.memset(ones512[:], 1.0), writes=["ones512"])

        dma("sp", [(identf[:], ident_d)], writes=["identf"])
        dma("pool", [(identb[:], ident_d)], writes=["identb"])
        dma("sp", [(gfin[:], bass.AP(gfin_d.tensor, 0, [[0, 128], [1, D]]))], writes=["gfin"])

        with ExitStack() as p0:
            cs = SB("cs", [128, 16], F32, p0)
            gT = SB("gT", [128, 16], F32, p0)
            sel2 = SB("sel2", [2, 256], F32, p0)
            i2 = SB("i2", [2, 2], F32, p0)
            ones12 = SB("ones12", [1, 2], F32, p0)
            modsb = SB("modsb", [2, 6 * D], F32, p0)
            wa = [SB("wa%d" % i, [128, KC * 512], F32, p0) for i in range(2)]
            ba = [SB("ba%d" % i, [1, 512], F32, p0) for i in range(2)]
            dma("sp", [(cs[:], csT_d)], writes=["cs"])
            dma("sp", [(gT[:, 0:8], gmixT_d), (gT[:, 8:16], gffnT_d)], writes=["gT"])
            dma("sp", [(sel2[:], sel2_d), (i2[:], i2_d)], writes=["sel2", "i2"])
            op("dve", lambda e: e.memset(ones12[:], 1.0), writes=["ones12"])
            op("act", lambda e: e.activation(out=cs[:], in_=cs[:], func=AF.Silu), reads=["cs"], writes=["cs"])
            cs3 = cs[:].rearrange("p (c b) -> p c b", b=2)
            w_ada_v = w_ada_d.rearrange("(c p) n -> p c n", p=128)
            for nb in range(12):
                wt = wa[nb % 2]
                bt = ba[nb % 2]
                wk = "wa%d" % (nb % 2)
                wt3 = wt[:].rearrange("p (c n) -> p c n", c=KC)
                dma("sp", [(wt3[:, c, :], w_ada_v[:, c, nb * 512:(nb + 1) * 512]) for c in range(KC)]
                    + [(bt[:], b_ada_d[:, nb * 512:(nb + 1) * 512])], writes=[wk])
                pb = 4 + (nb % 2)
                for c in range(KC):
                    op("pe", lambda e, c=c: e.matmul(psum[pb][0:2, :], lhsT=cs3[:, c, :], rhs=wt3[:, c, :],
                                                     start=(c == 0), stop=False),
                       reads=["cs", wk], writes=[pk(pb)])
                op("pe", lambda e: e.matmul(psum[pb][0:2, :], lhsT=ones12[:], rhs=bt[:], start=False, stop=True),
                   reads=["ones12", wk], writes=[pk(pb)])
                op("act", lambda e: e.copy(out=modsb[:, nb * 512:(nb + 1) * 512], in_=psum[pb][0:2, :]),
                   reads=[], writes=[pk(pb), "modsb"])
            for b in range(nseq):
                for w in range(2):
                    for hh in range(2):
                        col0 = (2048 if w == 0 else 5120) + hh * 512
                        pb = 4 + (hh % 2)
                        op("pe", lambda e, b=b, col0=col0, pb=pb: e.matmul(
                            psum[pb][:], lhsT=sel2[:, b * 128:(b + 1) * 128], rhs=modsb[:, col0:col0 + 512],
                            start=True, stop=True), reads=["sel2", "modsb"], writes=[pk(pb)])
                        o0 = (b * 2 + w) * D + hh * 512
                        op("act", lambda e, pb=pb, o0=o0: e.copy(out=gbc[:, o0:o0 + 512], in_=psum[pb][:]),
                           writes=[pk(pb), "gbc"])
            for j in range(32):
                col0 = [0, 1024, 3072, 4096][j // 8] + (j % 8) * 128
                op("pe", lambda e, j=j, col0=col0: e.matmul(psum[6][:, j * 2:j * 2 + 2], lhsT=modsb[:, col0:col0 + 128],
                                                            rhs=i2[:], start=True, stop=True),
                   reads=["modsb", "i2"], writes=[pk(6)])
            op("dve", lambda e: e.tensor_copy(out=modT[:], in_=psum[6][:, 0:64]), writes=[pk(6), "modT"])
            for w in range(2):
                for b in range(nseq):
                    i0 = (w * 2 + b) * 8
                    src_sh = bass.AP(modT, (w * 16) * 2 + b, [[64, 128], [2, 8]])
                    src_sc = bass.AP(modT, (w * 16 + 8) * 2 + b, [[64, 128], [2, 8]])
                    op("dve", lambda e, i0=i0, src_sh=src_sh: e.tensor_copy(out=shT[:, i0:i0 + 8], in_=src_sh),
                       reads=["modT"], writes=["shT"])
                    op("dve", lambda e, i0=i0, src_sc=src_sc, w=w: e.scalar_tensor_tensor(
                        out=aT[:, i0:i0 + 8], in0=src_sc, scalar=1.0, in1=gT[:, w * 8:(w + 1) * 8],
                        op0=ALU.add, op1=ALU.mult), reads=["modT", "gT"], writes=["aT"])
            dbg("modT", modT[:], [128, 64], ["modT"])
            dbg("aT", aT[:], [128, 32], ["aT"])
            dbg("shT", shT[:], [128, 32], ["shT"])
            dbg("gbc", gbc[:], [128, 4 * D], ["gbc"])
            for e in ("pe", "act", "dve", "pool", "sp"):
                bld.finish(e)

        def norm_block_to_T(xtiles, xkeys, widx, b, dstT3, dcol0, dkey, xn_rot, small_rot, tp_banks):
            xns = []
            for j in range(4):
                xn, xnk = xn_rot.next()
                sm, smk = small_rot.next()
                xt = xtiles[j]
                op("act", lambda e, xt=xt, xn=xn, sm=sm: e.activation(out=xn[:], in_=xt, func=AF.Square,
                                                                      accum_out=sm[:, 0:1]),
                   reads=[xkeys[j]], writes=[xnk, smk])
                op("act", lambda e, sm=sm: e.activation(out=sm[:, 1:2], in_=sm[:, 0:1], func=AF.Sqrt,
                                                        scale=1.0 / D, bias=epsb[:, 0:1]),
                   reads=[smk, "epsb"], writes=[smk])
                op("dve", lambda e, sm=sm: e.reciprocal(out=sm[:, 2:3], in_=sm[:, 1:2]), reads=[smk], writes=[smk])
                op("dve", lambda e, xt=xt, xn=xn, sm=sm: e.tensor_scalar(out=xn[:], in0=xt, scalar1=sm[:, 2:3],
                                                                         scalar2=None, op0=ALU.mult),
                   reads=[xkeys[j], smk], writes=[xnk])
                xns.append((xn, xnk))
            for c in range(KC):
                pb = tp_banks[c // 2]
                for j in range(4):
                    xn, xnk = xns[j]
                    o = PSB(pb)[:, (c % 2) * 512 + j * 128:(c % 2) * 512 + (j + 1) * 128]
                    op("pe", lambda e, o=o, xn=xn, c=c: e.transpose(o, xn[:, c * 128:(c + 1) * 128], identb[:]),
                       reads=[xnk, "identb"], writes=[pk(pb)])
            i0 = (widx * 2 + b) * 8
            for c in range(KC):
                pb = tp_banks[c // 2]
                src = PSB(pb)[:, (c % 2) * 512:(c % 2) * 512 + 512]
                dst = dstT3[:, c, dcol0:dcol0 + 512]
                if c % 2 == 0:
                    op("act", lambda e, src=src, dst=dst, c=c: e.activation(
                        out=dst, in_=src, func=AF.Identity, scale=aT[:, i0 + c:i0 + c + 1],
                        bias=shT[:, i0 + c:i0 + c + 1]), reads=["aT", "shT"], writes=[pk(pb), dkey])
                else:
                    op("dve", lambda e, src=src, dst=dst, c=c: e.tensor_scalar(
                        out=dst, in0=src, scalar1=aT[:, i0 + c:i0 + c + 1], scalar2=shT[:, i0 + c:i0 + c + 1],
                        op0=ALU.mult, op1=ALU.add), reads=["aT", "shT"], writes=[pk(pb), dkey])


        def pipeline(units, stages, lags):
            n = len(units)
            maxlag = max(lags)
            for k in range(n + maxlag):
                for st, lg in zip(stages, lags):
                    u = k - lg
                    if 0 <= u < n:
                        st(units[u])

        def phase_B(s):
            with ExitStack() as pb_:
                wsb = SB("wsb", [128, KC * 1536], BF16, pb_)
                wsb3 = wsb[:].rearrange("p (c n) -> p c n", c=KC)
                v_all = SB("v_all", [128, NT * 512], BF16, pb_)
                qTs = [SB("qTb%d" % i, [128, S], BF16, pb_) for i in range(2)]
                kTs = [SB("kTb%d" % i, [128, S], BF16, pb_) for i in range(2)]
                om_rot = Rot([(SB("om%d" % i, [128, 512], F32, pb_), "om%d" % i) for i in range(3)])
                P_rot = Rot([(SB("P%d" % i, [128, 514], F32, pb_), "P%d" % i) for i in range(6)])
                a_rot = Rot([(SB("a%d" % i, [128, 512], BF16, pb_), "a%d" % i) for i in range(4)])
                aT_rot = Rot([(SB("aT%d" % i, [128, 512], BF16, pb_), "aTs%d" % i) for i in range(4)])
                z_rot = Rot([0, 1, 2])
                t_rot = Rot([3, 4])
                o_rot = Rot([5, 6])
                dma("pool", [(wsb3[:, c, :], w_in_v[:, c, C_SB:C_SB + 1536]) for c in range(KC)], writes=["wsb"])
                hkeys = ["hT%d" % i for i in range(NB)]
                pr_rot = Rot([0, 1, 2, 3])
                for tt in range(NT):
                    pb = pr_rot.next()
                    for c in range(KC):
                        op("pe", lambda e, c=c, pb=pb, tt=tt: e.matmul(
                            PSF(pb), lhsT=hT3[:, c, tt * 128:(tt + 1) * 128], rhs=wsb3[:, c, 1024:1536],
                            start=(c == 0), stop=(c == KC - 1)), reads=[hkeys[tt // 4], "wsb"], writes=[pk(pb)])
                    if tt % 2 == 0:
                        op("act", lambda e, pb=pb, tt=tt: e.copy(out=v_all[:, tt * 512:(tt + 1) * 512], in_=PSF(pb)),
                           writes=[pk(pb), "v_all"])
                    else:
                        op("dve", lambda e, pb=pb, tt=tt: e.tensor_copy(out=v_all[:, tt * 512:(tt + 1) * 512], in_=PSF(pb)),
                           writes=[pk(pb), "v_all"])
                for hp in range(4):
                    qT = qTs[hp % 2]
                    kT = kTs[hp % 2]
                    qk = "qTb%d" % (hp % 2)
                    kk = "kTb%d" % (hp % 2)
                    for blk in range(NB):
                        for which in range(2):
                            pb = pr_rot.next()
                            col0 = which * 512 + hp * 128
                            for c in range(KC):
                                op("pe", lambda e, c=c, pb=pb, col0=col0, blk=blk: e.matmul(
                                    PSF(pb), lhsT=wsb3[:, c, col0:col0 + 128], rhs=hT3[:, c, blk * 512:(blk + 1) * 512],
                                    start=(c == 0), stop=(c == KC - 1)), reads=[hkeys[blk], "wsb"], writes=[pk(pb)])
                            if which == 0:
                                op("act", lambda e, pb=pb, blk=blk, qT=qT: e.activation(
                                    out=qT[:, blk * 512:(blk + 1) * 512], in_=PSF(pb), func=AF.Copy, scale=0.125),
                                   writes=[pk(pb), qk])
                            else:
                                op("dve", lambda e, pb=pb, blk=blk, kT=kT: e.tensor_copy(
                                    out=kT[:, blk * 512:(blk + 1) * 512], in_=PSF(pb)), writes=[pk(pb), kk])
                    units = []
                    for i in range(NT):
                        jmax = i // 4
                        ob = o_rot.next()
                        nun = 2 * (jmax + 1)
                        cnt = 0
                        for j in range(jmax, -1, -1):
                            for a in range(2):
                                w = min(512, 128 * (i + 1) - 512 * j)
                                cnt += 1
                                units.append(dict(i=i, j=j, a=a, w=w, diag=(j == jmax), ob=ob,
                                                  first=(j == jmax), last=(j == 0), final=(cnt == nun)))
                    lastP = {}

                    def s1(u):
                        a, i, j, w = u["a"], u["i"], u["j"], u["w"]
                        zb = z_rot.next()
                        u["zb"] = zb
                        op("pe", lambda e: e.matmul(PSF(zb)[:, 0:w], lhsT=qT[a * 64:(a + 1) * 64, i * 128:(i + 1) * 128],
                                                    rhs=kT[a * 64:(a + 1) * 64, 512 * j:512 * j + w], start=True, stop=True),
                           reads=[qk, kk], writes=[pk(zb)])

                    def s2(u):
                        a, i, j, w, zb = u["a"], u["i"], u["j"], u["w"], u["zb"]
                        om, omk = om_rot.next()
                        P, Pk = P_rot.next()
                        at, ak = a_rot.next()
                        u["at"], u["ak"] = at, ak
                        op("act", lambda e: e.activation(out=om[:, 0:w], in_=PSF(zb)[:, 0:w], func=AF.Sigmoid, scale=-1.0),
                           writes=[pk(zb), omk])
                        if u["diag"]:
                            op("dve", lambda e: e.tensor_tensor(out=om[:, w - 128:w], in0=om[:, w - 128:w], in1=tri_ge[:],
                                                                op=ALU.max), reads=["tri"], writes=[omk])
                            op("pool", lambda e: e.memset(P[:, w:w + 1], 1.0), writes=[Pk + "c"])
                            init = 1.0
                            rds = [omk, "ones512"]
                        else:
                            Pp, Ppk = lastP[a]
                            op("pool", lambda e: e.tensor_copy(out=P[:, w:w + 1], in_=Pp[:, 0:1]), reads=[Ppk], writes=[Pk + "c"])
                            init = Pp[:, 0:1]
                            rds = [omk, "ones512", Ppk]
                        rin = bass.AP(om, w - 1, [[512, 128], [-1, w]])
                        rout = bass.AP(P, w - 1, [[514, 128], [-1, w]])
                        op("dve", lambda e: e.tensor_tensor_scan(out=rout, data0=rin, data1=ones512[:, 0:w], initial=init,
                                                                 op0=ALU.mult, op1=ALU.mult), reads=rds, writes=[Pk])
                        op(SUB_ENG, lambda e: e.tensor_tensor(out=at[:, 0:w], in0=P[:, 1:w + 1], in1=P[:, 0:w], op=ALU.subtract),
                           reads=[Pk, Pk + "c"], writes=[ak])
                        lastP[a] = (P, Pk)
                        u["aev"] = bld.last(ak)

                    def s3(u):
                        w, at, ak = u["w"], u["at"], u["ak"]
                        assert bld.last(ak) == u["aev"], "rotation depth too small"
                        tb = t_rot.next()
                        u["tb"] = tb
                        for kb in range(w // 128):
                            op("pe", lambda e, kb=kb: e.transpose(PSB(tb)[:, kb * 128:(kb + 1) * 128],
                                                                  at[:, kb * 128:(kb + 1) * 128], identb[:]),
                               reads=[ak, "identb"], writes=[pk(tb)])

                    def s4(u):
                        w, tb = u["w"], u["tb"]
                        aTt, aTk = aT_rot.next()
                        u["aT"], u["aTk"] = aTt, aTk
                        op("act", lambda e: e.copy(out=aTt[:, 0:w], in_=PSB(tb)[:, 0:w]), writes=[pk(tb), aTk])
                        u["aTev"] = bld.last(aTk)

                    def s5(u):
                        a, i, j, w, ob = u["a"], u["i"], u["j"], u["w"], u["ob"]
                        aTt, aTk = u["aT"], u["aTk"]
                        assert bld.last(aTk) == u["aTev"], "rotation depth too small (aT)"
                        nkb = w // 128
                        for kb in range(nkb):
                            kt = 4 * j + kb
                            vc0 = kt * 512 + (2 * hp + a) * 64
                            op("pe", lambda e, kb=kb, vc0=vc0: e.matmul(
                                PSF(ob)[a * 64:(a + 1) * 64, 0:128], lhsT=v_all[:, vc0:vc0 + 64],
                                rhs=aTt[:, kb * 128:(kb + 1) * 128],
                                start=(u["first"] and kb == 0), stop=(u["last"] and kb == nkb - 1)),
                               reads=[aTk, "v_all"], writes=[pk(ob)])
                        if u["final"]:
                            op("dve", lambda e: e.tensor_copy(out=obT[:, hp * S + i * 128:hp * S + (i + 1) * 128],
                                                              in_=PSF(ob)[:, 0:128]), writes=[pk(ob), "obT%d" % (i // 4)])

                    pipeline(units, [s1, s2, s3, s4, s5], [0, 0, 2, 2, 3])
                if s == 0:
                    dbg("obT", obT[:], [128, 4 * S], ["obT%d" % i for i in range(NB)])
                for e in ("pe", "act", "dve", "pool", "sp"):
                    bld.finish(e)

        def phase_C(s):
            hkeys = ["hT%d" % i for i in range(NB)]
            with ExitStack() as pc_:
                qA = [SB("qA%d" % r, [128, S], BF16, pc_) for r in range(4)]
                kselT = SB("kselT", [128, S], BF16, pc_)
                kwinT = SB("kwinT", [128, S], BF16, pc_)
                vsel = SB("vsel", [128, NT * 128], BF16, pc_)
                vwin = SB("vwin", [128, NT * 128], BF16, pc_)
                gates = SB("gates", [128, NT * 24], F32, pc_)
                kcT2 = SB("kcT2", [128, 128], BF16, pc_)
                vcs = [SB("vcs%d" % g, [128, 64], BF16, pc_) for g in range(2)]
                pr_rot = Rot([0, 1, 2, 3])
                with ExitStack() as c1:
                    wns = SB("wns", [128, KC * 1304], BF16, c1)
                    wns3 = wns[:].rearrange("p (c n) -> p c n", c=KC)
                    kcmpT = SB("kcmpT", [128, S], BF16, c1)
                    vcmpT = SB("vcmpT", [128, S], BF16, c1)
                    w1k = SB("w1k", [128, 32 * 128], BF16, c1)
                    w1v = SB("w1v", [128, 32 * 128], BF16, c1)
                    posk = SB("posk", [128, 32], BF16, c1)
                    posv = SB("posv", [128, 32], BF16, c1)
                    w2k = SB("w2k", [128, 64], BF16, c1)
                    w2v = SB("w2v", [128, 64], BF16, c1)
                    dma("pool", [(wns3[:, c, :], w_in_v[:, c, 0:1304]) for c in range(KC)], writes=["wns"])
                    dma("pool", [(w1k[:], w1k_d), (w1v[:], w1v_d), (posk[:], posk_d), (posv[:], posv_d),
                                 (w2k[:], w2k_d), (w2v[:], w2v_d)], writes=["cmpw"])
                    for tt in range(NT):
                        pb = pr_rot.next()
                        for c in range(KC):
                            op("pe", lambda e, c=c, pb=pb, tt=tt: e.matmul(
                                PSF(pb)[:, 0:408], lhsT=hT3[:, c, tt * 128:(tt + 1) * 128], rhs=wns3[:, c, 896:1304],
                                start=(c == 0), stop=(c == KC - 1)), reads=[hkeys[tt // 4], "wns"], writes=[pk(pb)])
                        op("act", lambda e, pb=pb, tt=tt: e.copy(out=vsel[:, tt * 128:(tt + 1) * 128], in_=PSF(pb)[:, 0:128]),
                           writes=[pk(pb), "vsel"])
                        op("dve", lambda e, pb=pb, tt=tt: e.tensor_copy(out=vwin[:, tt * 128:(tt + 1) * 128],
                                                                        in_=PSF(pb)[:, 256:384]), writes=[pk(pb), "vwin"])
                        op("act", lambda e, pb=pb, tt=tt: e.activation(out=gates[:, tt * 24:(tt + 1) * 24],
                                                                       in_=PSF(pb)[:, 384:408], func=AF.Sigmoid),
                           writes=[pk(pb), "gates"])
                    for (dst, dk, col0) in ((kcmpT, "kcmpT", 512), (vcmpT, "vcmpT", 640), (kselT, "kselT", 768),
                                            (kwinT, "kwinT", 1024)):
                        for blk in range(NB):
                            pb = pr_rot.next()
                            for c in range(KC):
                                op("pe", lambda e, c=c, pb=pb, blk=blk, col0=col0: e.matmul(
                                    PSF(pb), lhsT=wns3[:, c, col0:col0 + 128], rhs=hT3[:, c, blk * 512:(blk + 1) * 512],
                                    start=(c == 0), stop=(c == KC - 1)), reads=[hkeys[blk], "wns"], writes=[pk(pb)])
                            if blk % 2 == 0:
                                op("act", lambda e, pb=pb, blk=blk, dst=dst: e.copy(out=dst[:, blk * 512:(blk + 1) * 512],
                                                                                   in_=PSF(pb)), writes=[pk(pb), dk])
                            else:
                                op("dve", lambda e, pb=pb, blk=blk, dst=dst: e.tensor_copy(
                                    out=dst[:, blk * 512:(blk + 1) * 512], in_=PSF(pb)), writes=[pk(pb), dk])
                    for r in range(4):
                        for blk in range(NB):
                            pb = pr_rot.next()
                            for g in range(2):
                                col0 = (g * 4 + r) * 64
                                for c in range(KC):
                                    op("pe", lambda e, c=c, pb=pb, blk=blk, col0=col0, g=g: e.matmul(
                                        PSF(pb)[g * 64:(g + 1) * 64, :], lhsT=wns3[:, c, col0:col0 + 64],
                                        rhs=hT3[:, c, blk * 512:(blk + 1) * 512], start=(c == 0), stop=(c == KC - 1)),
                                       reads=[hkeys[blk], "wns"], writes=[pk(pb)])
                            op("act", lambda e, pb=pb, blk=blk, r=r: e.activation(
                                out=qA[r][:, blk * 512:(blk + 1) * 512], in_=PSF(pb), func=AF.Copy, scale=0.125),
                               writes=[pk(pb), "qA%d" % r])
                    cu = SB("cu", [128, 128], F32, c1)
                    ct1 = SB("ct1", [128, 128], F32, c1)
                    ct2 = SB("ct2", [128, 128], F32, c1)
                    cgel = SB("cgel", [128, 128], BF16, c1)
                    cconst = SB("cconst", [128, 1], F32, c1)
                    op("dve", lambda e: e.memset(kcT2[:], 0.0), writes=["kcT2"])
                    for g in range(2):
                        op("dve", lambda e, g=g: e.memset(vcs[g][:], 0.0), writes=["vcs%d" % g])
                    for kv in range(2):
                        xT, xk = (kcmpT, "kcmpT") if kv == 0 else (vcmpT, "vcmpT")
                        w1 = w1k if kv == 0 else w1v
                        pos = posk if kv == 0 else posv
                        w2 = w2k if kv == 0 else w2v
                        for g in range(2):
                            pb = pr_rot.next()
                            rows = slice(g * 64, (g + 1) * 64)
                            for l in range(32):
                                op("pe", lambda e, l=l: e.matmul(PSF(pb)[:, 0:127], lhsT=w1[rows, l * 128:(l + 1) * 128],
                                                                 rhs=xT[rows, l:l + 2017:16], start=(l == 0), stop=(l == 31)),
                                   reads=[xk, "cmpw"], writes=[pk(pb)])
                            for l in range(32):
                                op("pe", lambda e, l=l: e.matmul(PSF(pb)[:, 127:128], lhsT=w1[rows, l * 128:(l + 1) * 128],
                                                                 rhs=pos[rows, l:l + 1], start=(l == 0), stop=(l == 31)),
                                   reads=["cmpw"], writes=[pk(pb)])
                            op("act", lambda e: e.copy(out=cconst[:], in_=PSF(pb)[:, 127:128]), writes=[pk(pb), "cconst"])
                            op("act", lambda e: e.activation(out=cu[:, 0:127], in_=PSF(pb)[:, 0:127], func=AF.Identity,
                                                             bias=cconst[:, 0:1]), reads=["cconst"], writes=[pk(pb), "cu"])
                            op("dve", lambda e: e.tensor_tensor(out=ct1[:, 0:127], in0=cu[:, 0:127], in1=cu[:, 0:127], op=ALU.mult),
                               reads=["cu"], writes=["ct1"])
                            op("dve", lambda e: e.tensor_scalar(out=ct1[:, 0:127], in0=ct1[:, 0:127], scalar1=0.044715, scalar2=1.0,
                                                                op0=ALU.mult, op1=ALU.add), reads=["ct1"], writes=["ct1"])
                            op("dve", lambda e: e.tensor_tensor(out=ct1[:, 0:127], in0=ct1[:, 0:127], in1=cu[:, 0:127], op=ALU.mult),
                               reads=["ct1", "cu"], writes=["ct1"])
                            op("act", lambda e: e.activation(out=ct2[:, 0:127], in_=ct1[:, 0:127], func=AF.Tanh,
                                                             scale=0.7978845608028654), reads=["ct1"], writes=["ct2"])
                            op("dve", lambda e: e.scalar_tensor_tensor(out=ct2[:, 0:127], in0=ct2[:, 0:127], scalar=1.0,
                                                                       in1=cu[:, 0:127], op0=ALU.add, op1=ALU.mult),
                               reads=["ct2", "cu"], writes=["ct2"])
                            op("dve", lambda e: e.tensor_scalar(out=cgel[:, 0:127], in0=ct2[:, 0:127], scalar1=0.5, scalar2=None,
                                                                op0=ALU.mult), reads=["ct2"], writes=["cgel"])
                            pb2 = pr_rot.next()
                            if kv == 0:
                                op("pe", lambda e: e.matmul(PSF(pb2)[g * 64:(g + 1) * 64, 0:127], lhsT=w2[:, :], rhs=cgel[:, 0:127],
                                                            start=True, stop=True), reads=["cgel", "cmpw"], writes=[pk(pb2)])
                                op("act", lambda e: e.copy(out=kcT2[g * 64:(g + 1) * 64, 0:127],
                                                           in_=PSF(pb2)[g * 64:(g + 1) * 64, 0:127]), writes=[pk(pb2), "kcT2"])
                            else:
                                op("pe", lambda e: e.matmul(PSF(pb2)[0:127, 0:64], lhsT=cgel[:, 0:127], rhs=w2[:, :],
                                                            start=True, stop=True), reads=["cgel", "cmpw"], writes=[pk(pb2)])
                                op("act", lambda e: e.copy(out=vcs[g][0:127, :], in_=PSF(pb2)[0:127, 0:64]),
                                   writes=[pk(pb2), "vcs%d" % g])
                    if s == 0:
                        dbg("kcT2", kcT2[:], [128, 128], ["kcT2"])
                        dbg("vcs0", vcs[0][:], [128, 64], ["vcs0"])
                        dbg("vcs1", vcs[1][:], [128, 64], ["vcs1"])
                        dbg("gates", gates[:], [128, NT * 24], ["gates"])
                    for e in ("pe", "act", "dve", "pool", "sp"):
                        bld.finish(e)
                if stop_after == "C1":
                    return
                with ExitStack() as c2:
                    tsel = SB("tsel", [128, 2048], F32, c2)
                    twin = SB("twin", [128, 640], F32, c2)
                    tcmp = SB("tcmp", [128, 16 * 128], F32, c2)
                    selmul = SB("selmul", [128, 16 * 32], F32, c2)
                    seladd = SB("seladd", [128, 16 * 32], F32, c2)
                    rv0 = SB("rv0", [128, 1], F32, c2)
                    dma("sp", [(tsel[:], tsel_d), (twin[:], twin_d), (tcmp[:], tcmp_d), (selmul[:], selmul_d),
                               (seladd[:], seladd_d), (rv0[:], rv0_d)], writes=["tabs"])
                    pp = SB("pp", [128, 132], F32, c2)
                    op("dve", lambda e: e.memset(pp[:], 0.0), writes=["pp"])
                    W_rot = Rot([(SB("Ws%d" % i, [128, 2048], F32, c2), "Ws%d" % i) for i in range(2)])
                    Es_rot = Rot([(SB("Es%d" % i, [128, 2048], BF16, c2), "Es%d" % i) for i in range(3)])
                    Ew_rot = Rot([(SB("Ew%d" % i, [128, 640], BF16, c2), "Ew%d" % i) for i in range(3)])
                    Eb_rot = Rot([(SB("Eb%d" % i, [128, 128], BF16, c2), "Eb%d" % i) for i in range(4)])
                    PT_rot = Rot([(SB("PTs%d" % i, [128, 2048], BF16, c2), "PTs%d" % i) for i in range(2)])
                    Ec_rot = Rot([(SB("Ec%d" % i, [128, 128], F32, c2), "Ec%d" % i) for i in range(2)])
                    sm_rot = Rot([(SB("smc%d" % i, [128, 8], F32, c2), "smc%d" % i) for i in range(6)])
                    Dm_rot = Rot([(SB("Dm%d" % i, [128, 128], BF16, c2), "Dm%d" % i) for i in range(4)])
                    selb_rot = Rot([(SB("selb%d" % i, [128, 32], F32, c2), "selb%d" % i) for i in range(2)])
                    pr1 = SB("pr1", [128, 32], F32, c2)
                    pr2 = SB("pr2", [128, 32], F32, c2)
                    m8 = SB("m8", [128, 8], F32, c2)
                    z_rot = Rot([0, 1, 2, 3])
                    t_rot = Rot([4, 5])
                    print("[build] C2 sbuf bytes remaining/partition:", nc.sbuf_bytes_remaining)
                    units = []
                    for g in range(2):
                        for i in range(NT):
                            for r in range(4):
                                units.append(dict(g=g, i=i, r=r, br="cmp"))
                            for br_ in ("win", "sel"):
                                if br_ in branches:
                                    for r in range(4):
                                        units.append(dict(g=g, i=i, r=r, br=br_))
                    cur_selb = {}
                    g0v = bass.AP(gates, 0, [[NT * 24, 128], [3, 8]])
                    op("dve", lambda e: e.tensor_scalar(out=g0v, in0=g0v, scalar1=rv0[:, 0:1], scalar2=None, op0=ALU.mult),
                       reads=["gates", "tabs"], writes=["gates"])

                    def chunks_of(u):
                        i = u["i"]
                        if u["br"] == "cmp":
                            return [(0, 128, None)]
                        if u["br"] == "win":
                            nblk = min(i, 4) + 1
                            k0 = 128 * (i - nblk + 1)
                            tc0 = 640 - 128 * nblk
                            res = []
                            c = 0
                            while c < 128 * nblk:
                                w = min(512, 128 * nblk - c)
                                res.append((k0 + c, w, tc0 + c))
                                c += w
                            return res
                        res = []
                        for j in range(i // 4 + 1):
                            w = min(512, 128 * (i + 1) - 512 * j)
                            res.append((512 * j, w, 512 * j - 128 * i + 1920))
                        return res

                    def s1(u):
                        g, i, r, br = u["g"], u["i"], u["r"], u["br"]
                        rows = slice(g * 64, (g + 1) * 64)
                        kT, kk = {"cmp": (kcT2, "kcT2"), "win": (kwinT, "kwinT"), "sel": (kselT, "kselT")}[br]
                        u["zb"] = []
                        for (k0, w, tc0) in chunks_of(u):
                            zb = z_rot.next()
                            u["zb"].append(zb)
                            op("pe", lambda e, zb=zb, k0=k0, w=w: e.matmul(
                                PSF(zb)[:, 0:w], lhsT=qA[r][rows, i * 128:(i + 1) * 128], rhs=kT[rows, k0:k0 + w],
                                start=True, stop=True), reads=["qA%d" % r, kk], writes=[pk(zb)])

                    def sA(u):
                        g, i, r, br = u["g"], u["i"], u["r"], u["br"]
                        h = g * 4 + r
                        slope = 2.0 ** (-(h + 1))
                        chs = chunks_of(u)
                        if br == "cmp":
                            W, Wk = Ec_rot.next()
                            tab = tcmp[:, i * 128:(i + 1) * 128]
                            zb = u["zb"][0]
                            op("dve", lambda e: e.scalar_tensor_tensor(out=W[:, 0:128], in0=tab, scalar=slope, in1=PSF(zb)[:, 0:128],
                                                                       op0=ALU.mult, op1=ALU.add), reads=["tabs"], writes=[pk(zb), Wk])
                            width = 128
                        else:
                            W, Wk = W_rot.next()
                            off = 0
                            for (k0, w, tc0), zb in zip(chs, u["zb"]):
                                tab = (twin if br == "win" else tsel)[:, tc0:tc0 + w]
                                op("dve", lambda e, off=off, w=w, tab=tab, zb=zb: e.scalar_tensor_tensor(
                                    out=W[:, off:off + w], in0=tab, scalar=slope, in1=PSF(zb)[:, 0:w], op0=ALU.mult, op1=ALU.add),
                                   reads=["tabs"], writes=[pk(zb), Wk])
                                if br == "sel":
                                    sb_, sbk = cur_selb[(g, i)]
                                    nb_ = w // 64
                                    b0 = k0 // 64
                                    in1 = bass.AP(sb_, b0, [[32, 128], [1, nb_], [0, 64]])
                                    Wv = W[:, off:off + w].rearrange("p (a b) -> p a b", b=64)
                                    op("pool", lambda e, Wv=Wv, in1=in1: e.tensor_tensor(out=Wv, in0=Wv, in1=in1, op=ALU.add),
                                       reads=[sbk], writes=[Wk])
                                off += w
                            width = off
                        u["width"] = width
                        u["W"], u["Wk"] = W, Wk
                        u["Wev"] = bld.last(Wk)

                    def sB(u):
                        br, width, W, Wk = u["br"], u["width"], u["W"], u["Wk"]
                        assert bld.last(Wk) == u["Wev"], "rotation depth too small (W)"
                        sm, smk = sm_rot.next()
                        u["sm"], u["smk"] = sm, smk
                        op("dve", lambda e: e.tensor_reduce(out=sm[:, 0:1], in_=W[:, 0:width], axis=AX.X, op=ALU.max, negate=True),
                           reads=[Wk], writes=[smk])
                        if br == "cmp":
                            E, Ek = W, Wk
                        else:
                            E, Ek = (Ew_rot if br == "win" else Es_rot).next()
                        op("act", lambda e: e.activation(out=E[:, 0:width], in_=W[:, 0:width], func=AF.Exp, bias=sm[:, 0:1],
                                                         accum_out=sm[:, 1:2]), reads=[Wk, smk], writes=[Ek, smk])
                        u["E32"], u["E32k"] = E, Ek
                        if br == "cmp":
                            Eb, Ebk = Eb_rot.next()
                            op("act", lambda e: e.copy(out=Eb[:, 0:128], in_=E[:, 0:128]), reads=[Ek], writes=[Ebk])
                            E, Ek = Eb, Ebk
                        u["E"], u["Ek"] = E, Ek
                        u["Eev"] = bld.last(Ek)
                        u["smev"] = bld.last(smk)

                    def sC(u):
                        g, i, r, br = u["g"], u["i"], u["r"], u["br"]
                        h = g * 4 + r
                        sm, smk = u["sm"], u["smk"]
                        assert bld.last(smk) == u["smev"], "rotation depth too small (sm)"
                        Dm, Dmk = Dm_rot.next()
                        u["Dm"], u["Dmk"] = Dm, Dmk
                        op("dve", lambda e: e.reciprocal(out=sm[:, 2:3], in_=sm[:, 1:2]), reads=[smk], writes=[smk])
                        bri = {"cmp": 0, "sel": 1, "win": 2}[br]
                        gcol = i * 24 + h * 3 + bri
                        op("dve", lambda e: e.tensor_scalar(out=Dm[:], in0=identf[:], scalar1=sm[:, 2:3], scalar2=gates[:, gcol:gcol + 1],
                                                            op0=ALU.mult, op1=ALU.mult), reads=[smk, "identf", "gates"], writes=[Dmk])
                        u["Dmev"] = bld.last(Dmk)
                        if br == "cmp":
                            E, Ek = u["E32"], u["E32k"]
                            if r == 0:
                                op("dve", lambda e: e.tensor_scalar(out=pp[:, 1:129], in0=E[:, 0:128], scalar1=sm[:, 2:3], scalar2=None,
                                                                    op0=ALU.mult), reads=[Ek, smk], writes=["pp"])
                            else:
                                op("dve", lambda e: e.scalar_tensor_tensor(out=pp[:, 1:129], in0=E[:, 0:128], scalar=sm[:, 2:3],
                                                                           in1=pp[:, 1:129], op0=ALU.mult, op1=ALU.add),
                                   reads=[Ek, smk, "pp"], writes=["pp"])
                            if r == 3:
                                selb, selbk = selb_rot.next()
                                cur_selb[(g, i)] = (selb, selbk)
                                ppv = pp[:, 0:128].rearrange("p (a b) -> p a b", b=4)
                                op("dve", lambda e: e.tensor_reduce(out=pr1[:], in_=ppv, axis=AX.X, op=ALU.add), reads=["pp"], writes=["pr1"])
                                op("dve", lambda e: e.tensor_tensor(out=pr2[:], in0=pr1[:], in1=pp[:, 4:132:4], op=ALU.add),
                                   reads=["pr1", "pp"], writes=["pr2"])
                                op("dve", lambda e: e.tensor_tensor(out=pr2[:], in0=pr2[:], in1=selmul[:, i * 32:(i + 1) * 32], op=ALU.mult),
                                   reads=["pr2", "tabs"], writes=["pr2"])
                                op("dve", lambda e: e.tensor_tensor(out=pr2[:], in0=pr2[:], in1=seladd[:, i * 32:(i + 1) * 32], op=ALU.add),
                                   reads=["pr2", "tabs"], writes=["pr2"])
                                op("dve", lambda e: e.max(out=m8[:], in_=pr2[:]), reads=["pr2"], writes=["m8"])
                                op("dve", lambda e: e.tensor_scalar(out=pr1[:], in0=pr2[:], scalar1=m8[:, 7:8], scalar2=None, op0=ALU.is_ge),
                                   reads=["pr2", "m8"], writes=["pr1"])
                                op("dve", lambda e: e.tensor_scalar(out=selb[:], in0=pr1[:], scalar1=-1.0, scalar2=BIG, op0=ALU.add,
                                                                    op1=ALU.mult), reads=["pr1"], writes=[selbk])

                    def sD(u):
                        E, Ek, Dm, Dmk, width = u["E"], u["Ek"], u["Dm"], u["Dmk"], u["width"]
                        assert bld.last(Ek) == u["Eev"] and bld.last(Dmk) == u["Dmev"], "rotation depth too small"
                        PT, PTk = PT_rot.next()
                        u["PT"], u["PTk"] = PT, PTk
                        nkb = width // 128
                        kb = 0
                        while kb < nkb:
                            n = min(4, nkb - kb)
                            tb = t_rot.next()
                            for q in range(n):
                                op("pe", lambda e, q=q, kb=kb, tb=tb: e.matmul(
                                    PSF(tb)[:, q * 128:(q + 1) * 128], lhsT=E[:, (kb + q) * 128:(kb + q + 1) * 128], rhs=Dm[:],
                                    start=True, stop=True), reads=[Ek, Dmk], writes=[pk(tb)])
                            op("act", lambda e, kb=kb, n=n, tb=tb: e.copy(out=PT[:, kb * 128:(kb + n) * 128], in_=PSF(tb)[:, 0:n * 128]),
                               writes=[pk(tb), PTk])
                            kb += n
                        u["PTev"] = bld.last(PTk)

                    def sE(u):
                        g, i, r, br = u["g"], u["i"], u["r"], u["br"]
                        PT, PTk, width = u["PT"], u["PTk"], u["width"]
                        assert bld.last(PTk) == u["PTev"], "rotation depth too small (PT)"
                        nkb = width // 128
                        ob = 6 + r // 2
                        orow = slice((r % 2) * 64, (r % 2) * 64 + 64)
                        chs = chunks_of(u)
                        for kb in range(nkb):
                            if br == "cmp":
                                lhsT = vcs[g][:, :]
                                vk = "vcs%d" % g
                            else:
                                key0 = chs[0][0] + kb * 128
                                vt = vwin if br == "win" else vsel
                                vk = "vwin" if br == "win" else "vsel"
                                lhsT = vt[:, (key0 // 128) * 128 + g * 64:(key0 // 128) * 128 + g * 64 + 64]
                            op("pe", lambda e, kb=kb, lhsT=lhsT: e.matmul(
                                PSF(ob)[orow, 0:128], lhsT=lhsT, rhs=PT[:, kb * 128:(kb + 1) * 128],
                                start=(br == "cmp" and kb == 0), stop=(br == branches[-1] and kb == nkb - 1)), reads=[PTk, vk], writes=[pk(ob)])
                        if br == branches[-1]:
                            kc_ = 2 * g + r // 2
                            op("dve", lambda e: e.tensor_copy(out=oaT[orow, kc_ * S + i * 128:kc_ * S + (i + 1) * 128],
                                                              in_=PSF(ob)[orow, 0:128]), writes=[pk(ob), "oaT%d" % (i // 4)])

                    pipeline(units, [sE, sD, sC, sB, sA, s1], [5, 4, 3, 2, 1, 0])
                    if s == 0:
                        dbg("oaT", oaT[:], [128, 4 * S], ["oaT%d" % i for i in range(NB)])
                    for e in ("pe", "act", "dve", "pool", "sp"):
                        bld.finish(e)

        def phase_D(s):
            hkeys = ["hT%d" % i for i in range(NB)]
            oaT3 = oaT[:].rearrange("p (c t) -> p c t", c=4)
            obT3 = obT[:].rearrange("p (c t) -> p c t", c=4)
            g1 = gbc[:, (s * 2 + 0) * D:(s * 2 + 1) * D]
            with ExitStack() as pd_:
                wpa = SB("wpa", [128, 4 * D], BF16, pd_)
                wpb = SB("wpb", [128, 4 * D], BF16, pd_)
                wout = SB("wout", [128, KC * D], BF16, pd_)
                wmg = SB("wmg", [128, KC * 2048], BF16, pd_)
                wpa3 = wpa[:].rearrange("p (c n) -> p c n", c=4)
                wpb3 = wpb[:].rearrange("p (c n) -> p c n", c=4)
                wout3 = wout[:].rearrange("p (c n) -> p c n", c=KC)
                wmg3 = wmg[:].rearrange("p (c n) -> p c n", c=KC)
                w_pa_v = w_pa_d.rearrange("(c p) n -> p c n", p=128)
                w_pb_v = w_pb_d.rearrange("(c p) n -> p c n", p=128)
                w_out_v = w_out_d.rearrange("(c p) n -> p c n", p=128)
                dma("pool", [(wpa3[:, c, :], w_pa_v[:, c, :]) for c in range(4)], writes=["wpa"])
                dma("pool", [(wpb3[:, c, :], w_pb_v[:, c, :]) for c in range(4)], writes=["wpb"])
                for hh in range(4):
                    dma("pool", [(wmg3[:, c, hh * 512:(hh + 1) * 512], w_in_v[:, c, C_MG + hh * 512:C_MG + (hh + 1) * 512])
                                 for c in range(KC)], writes=["wmg%d" % hh])
                dma("pool", [(wout3[:, c, :], w_out_v[:, c, :]) for c in range(KC)], writes=["wout"])
                x_rot = Rot([(SB("xd%d" % i, [128, D], F32, pd_), "xd%d" % i) for i in range(6)])
                yT = SB("yT", [128, KC * TB], BF16, pd_)
                yT3 = yT[:].rearrange("p (c t) -> p c t", c=KC)
                s_rot = Rot([(SB("sg%d" % i, [128, 512], F32, pd_), "sg%d" % i) for i in range(4)])
                tmp_rot = Rot([(SB("tmpd%d" % i, [128, 512], F32, pd_), "tmpd%d" % i) for i in range(2)])
                bank_sets = Rot([[0, 1, 2, 3], [4, 5, 6, 7]])
                o_rot = Rot([0, 1, 2, 3, 4, 5, 6, 7])
                print("[build] D sbuf bytes remaining:", nc.sbuf_bytes_remaining)
                for blk in range(NB):
                    t0 = blk * TB
                    xt = []
                    for j in range(4):
                        x_, xk = x_rot.next()
                        tt = blk * 4 + j
                        dma("sp", [(x_[:], x_d[s, tt * 128:(tt + 1) * 128, :])], writes=[xk])
                        xt.append((x_, xk))
                    for fc in range(KC):
                        bA, bB, bC, bD = bank_sets.next()
                        for kc in range(4):
                            op("pe", lambda e, kc=kc: e.matmul(PSF(bA), lhsT=wpa3[:, kc, fc * 128:(fc + 1) * 128],
                                                               rhs=oaT3[:, kc, t0:t0 + TB], start=(kc == 0), stop=(kc == 3)),
                               reads=["wpa", "oaT%d" % blk], writes=[pk(bA)])
                        for kc in range(4):
                            op("pe", lambda e, kc=kc: e.matmul(PSF(bB), lhsT=wpb3[:, kc, fc * 128:(fc + 1) * 128],
                                                               rhs=obT3[:, kc, t0:t0 + TB], start=(kc == 0), stop=(kc == 3)),
                               reads=["wpb", "obT%d" % blk], writes=[pk(bB)])
                        for (bb, c0) in ((bC, fc * 128), (bD, 1024 + fc * 128)):
                            for kc in range(KC):
                                op("pe", lambda e, kc=kc, bb=bb, c0=c0: e.matmul(
                                    PSF(bb), lhsT=wmg3[:, kc, c0:c0 + 128], rhs=hT3[:, kc, t0:t0 + TB],
                                    start=(kc == 0), stop=(kc == KC - 1)), reads=["wmg%d" % (c0 // 512), hkeys[blk]], writes=[pk(bb)])
                        s0, s0k = s_rot.next()
                        s1_, s1k = s_rot.next()
                        op("act", lambda e: e.activation(out=s0[:], in_=PSF(bC), func=AF.Sigmoid), writes=[pk(bC), s0k])
                        op("act", lambda e: e.activation(out=s1_[:], in_=PSF(bD), func=AF.Sigmoid), writes=[pk(bD), s1k])
                        op("dve", lambda e: e.tensor_tensor(out=s0[:], in0=PSF(bA), in1=s0[:], op=ALU.mult), reads=[s0k], writes=[pk(bA), s0k])
                        op("dve", lambda e: e.tensor_tensor(out=s1_[:], in0=PSF(bB), in1=s1_[:], op=ALU.mult), reads=[s1k], writes=[pk(bB), s1k])
                        op("dve", lambda e: e.tensor_tensor(out=yT3[:, fc, :], in0=s0[:], in1=s1_[:], op=ALU.add),
                           reads=[s0k, s1k], writes=["yT"])
                    if s == 0 and blk == 0:
                        dbg("yT", yT[:], [128, KC * TB], ["yT"])
                    for j in range(4):
                        x_, xk = xt[j]
                        for hh in range(2):
                            ob = o_rot.next()
                            for fc in range(KC):
                                op("pe", lambda e, fc=fc: e.matmul(PSF(ob), lhsT=yT3[:, fc, j * 128:(j + 1) * 128],
                                                                   rhs=wout3[:, fc, hh * 512:(hh + 1) * 512],
                                                                   start=(fc == 0), stop=(fc == KC - 1)),
                                   reads=["yT", "wout"], writes=[pk(ob)])
                            tm, tmk = tmp_rot.next()
                            op("dve", lambda e: e.tensor_tensor(out=tm[:], in0=PSF(ob), in1=g1[:, hh * 512:(hh + 1) * 512], op=ALU.mult),
                               reads=["gbc"], writes=[pk(ob), tmk])
                            op("dve", lambda e: e.tensor_tensor(out=x_[:, hh * 512:(hh + 1) * 512], in0=x_[:, hh * 512:(hh + 1) * 512],
                                                                in1=tm[:], op=ALU.add), reads=[tmk, xk], writes=[xk])
                        tt = blk * 4 + j
                        dma("sp", [(out_d[s, tt * 128:(tt + 1) * 128, :], x_[:])], reads=[xk], writes=["xm%d" % tt])
                for e in ("pe", "act", "dve", "pool", "sp"):
                    bld.finish(e)

        def phase_E(s, TBE=1024):
            nsub = TBE // 512
            ntile = TBE // 128
            g2 = gbc[:, (s * 2 + 1) * D:(s * 2 + 2) * D]
            with ExitStack() as pe_:
                wd = SB("wd", [128, NFF * D], BF16, pe_)
                wd3 = wd[:].rearrange("p (c n) -> p c n", c=NFF)
                w_fd_v = w_fd_d.rearrange("(c p) n -> p c n", p=128)
                for c0 in range(0, NFF, 4):
                    n = min(4, NFF - c0)
                    dma("pool", [(wd3[:, c, :], w_fd_v[:, c, :]) for c in range(c0, c0 + n)], writes=["wd%d" % c0])
                wdkeys = ["wd%d" % c0 for c0 in range(0, NFF, 4)]
                w_fg_v = w_fg_d.rearrange("(c p) n -> p c n", p=128)
                w_fu_v = w_fu_d.rearrange("(c p) n -> p c n", p=128)
                wg_rot = Rot([(SB("wg%d" % i, [128, KC * 256], BF16, pe_), "wg%d" % i) for i in range(2)])
                wu_rot = Rot([(SB("wu%d" % i, [128, KC * 256], BF16, pe_), "wu%d" % i) for i in range(2)])
                x_rot = Rot([(SB("xe%d" % i, [128, D], F32, pe_), "xe%d" % i) for i in range(ntile + 2)])
                xn_rot = Rot([(SB("xne%d" % i, [128, D], BF16, pe_), "xne%d" % i) for i in range(5)])
                sm_rot = Rot([(SB("sme%d" % i, [128, 4], F32, pe_), "sme%d" % i) for i in range(6)])
                h2T = SB("h2T", [128, KC * TBE], BF16, pe_)
                h2T3 = h2T[:].rearrange("p (c t) -> p c t", c=KC)
                actT = SB("actT", [128, NFF * TBE], BF16, pe_)
                actT3 = actT[:].rearrange("p (c t) -> p c t", c=NFF)
                sg_rot = Rot([(SB("sge%d" % i, [128, 512], F32, pe_), "sge%d" % i) for i in range(3)])
                tmp_rot = Rot([(SB("tmpe%d" % i, [128, 512], F32, pe_), "tmpe%d" % i) for i in range(2)])
                gu_rot = Rot([0, 1, 2, 3])
                o_rot = Rot([4, 5, 6, 7])
                print("[build] E sbuf bytes remaining:", nc.sbuf_bytes_remaining)
                for blk in range(S // TBE):
                    xt = []
                    for j in range(ntile):
                        x_, xk = x_rot.next()
                        tt = blk * ntile + j
                        dma("sp", [(x_[:], out_d[s, tt * 128:(tt + 1) * 128, :])], reads=["xm%d" % tt], writes=[xk])
                        xt.append((x_, xk))
                    for sub in range(nsub):
                        norm_block_to_T([xt[sub * 4 + j][0][:] for j in range(4)], [xt[sub * 4 + j][1] for j in range(4)],
                                        1, s, h2T3, sub * 512, "h2T%d" % sub, xn_rot, sm_rot, [4, 5, 6, 7])
                    if s == 0 and blk == 0:
                        dbg("h2T", h2T[:], [128, KC * TBE], ["h2T%d" % i for i in range(nsub)])
                    for grp in range(NFF // 2):
                        wg, wgk = wg_rot.next()
                        wu, wuk = wu_rot.next()
                        wg3 = wg[:].rearrange("p (c n) -> p c n", c=KC)
                        wu3 = wu[:].rearrange("p (c n) -> p c n", c=KC)
                        dma("pool", [(wg3[:, c, :], w_fg_v[:, c, grp * 256:(grp + 1) * 256]) for c in range(KC)], writes=[wgk])
                        dma("pool", [(wu3[:, c, :], w_fu_v[:, c, grp * 256:(grp + 1) * 256]) for c in range(KC)], writes=[wuk])
                        for f2 in range(2):
                            ffc = grp * 2 + f2
                            for sub in range(nsub):
                                bG = gu_rot.next()
                                bU = gu_rot.next()
                                for (bb, w3, wk_) in ((bG, wg3, wgk), (bU, wu3, wuk)):
                                    for kc in range(KC):
                                        op("pe", lambda e, kc=kc, bb=bb, w3=w3: e.matmul(
                                            PSF(bb), lhsT=w3[:, kc, f2 * 128:(f2 + 1) * 128], rhs=h2T3[:, kc, sub * 512:(sub + 1) * 512],
                                            start=(kc == 0), stop=(kc == KC - 1)), reads=[wk_, "h2T%d" % sub], writes=[pk(bb)])
                                sg, sgk = sg_rot.next()
                                op("act", lambda e: e.activation(out=sg[:], in_=PSF(bG), func=AF.Silu), writes=[pk(bG), sgk])
                                op("dve", lambda e: e.tensor_tensor(out=actT3[:, ffc, sub * 512:(sub + 1) * 512], in0=PSF(bU), in1=sg[:],
                                                                    op=ALU.mult), reads=[sgk], writes=[pk(bU), "actT%d" % sub])
                    for j in range(ntile):
                        x_, xk = xt[j]
                        sm, smk = sm_rot.next()
                        for hh in range(2):
                            ob = o_rot.next()
                            for ffc in range(NFF):
                                op("pe", lambda e, ffc=ffc: e.matmul(PSF(ob), lhsT=actT3[:, ffc, j * 128:(j + 1) * 128],
                                                                     rhs=wd3[:, ffc, hh * 512:(hh + 1) * 512],
                                                                     start=(ffc == 0), stop=(ffc == NFF - 1)),
                                   reads=["actT%d" % (j // 4), wdkeys[ffc // 4]], writes=[pk(ob)])
                            tm, tmk = tmp_rot.next()
                            op("dve", lambda e: e.tensor_tensor(out=tm[:], in0=PSF(ob), in1=g2[:, hh * 512:(hh + 1) * 512], op=ALU.mult),
                               reads=["gbc"], writes=[pk(ob), tmk])
                            op("dve", lambda e: e.tensor_tensor(out=x_[:, hh * 512:(hh + 1) * 512], in0=x_[:, hh * 512:(hh + 1) * 512],
                                                                in1=tm[:], op=ALU.add), reads=[tmk, xk], writes=[xk])
                        xn, xnk = xn_rot.next()
                        op("act", lambda e: e.activation(out=xn[:], in_=x_[:], func=AF.Square, accum_out=sm[:, 0:1]),
                           reads=[xk], writes=[xnk, smk])
                        op("act", lambda e: e.activation(out=sm[:, 1:2], in_=sm[:, 0:1], func=AF.Sqrt, scale=1.0 / D, bias=epsb[:, 0:1]),
                           reads=[smk, "epsb"], writes=[smk])
                        op("dve", lambda e: e.reciprocal(out=sm[:, 2:3], in_=sm[:, 1:2]), reads=[smk], writes=[smk])
                        op("dve", lambda e: e.scalar_tensor_tensor(out=x_[:], in0=x_[:], scalar=sm[:, 2:3], in1=gfin[:], op0=ALU.mult,
                                                                   op1=ALU.mult), reads=[xk, smk, "gfin"], writes=[xk])
                        tt = blk * ntile + j
                        dma("sp", [(out_d[s, tt * 128:(tt + 1) * 128, :], x_[:])], reads=[xk], writes=["xo%d" % tt])
                for e in ("pe", "act", "dve", "pool", "sp"):
                    bld.finish(e)

        epsb = SB("epsb", [128, 1], F32)
        op("dve", lambda e: e.memset(epsb[:], EPS), writes=["epsb"])

        for s in range(nseq):
          with ExitStack() as seqscope:
            hT = SB("hT", [128, KC * S], BF16, seqscope)
            hT3 = hT[:].rearrange("p (c t) -> p c t", c=KC)
            obT = SB("obT", [128, 4 * S], BF16, seqscope)
            oaT = SB("oaT", [128, 4 * S], BF16, seqscope)
            with ExitStack() as pa:
                xb = [SB("xa%d" % i, [128, D], F32, pa) for i in range(6)]
                xnb = [SB("xna%d" % i, [128, D], BF16, pa) for i in range(6)]
                smb = [SB("sma%d" % i, [128, 4], F32, pa) for i in range(6)]
                x_rot = Rot([(t, "xa%d" % i) for i, t in enumerate(xb)])
                xn_rot = Rot([(t, "xna%d" % i) for i, t in enumerate(xnb)])
                sm_rot = Rot([(t, "sma%d" % i) for i, t in enumerate(smb)])
                for blk in range(NB):
                    tiles = []
                    keys = []
                    for j in range(4):
                        xt, xk = x_rot.next()
                        tt = blk * 4 + j
                        dma("sp", [(xt[:], x_d[s, tt * 128:(tt + 1) * 128, :])], writes=[xk])
                        tiles.append(xt[:])
                        keys.append(xk)
                    norm_block_to_T(tiles, keys, 0, s, hT3, blk * TB, "hT%d" % blk, xn_rot, sm_rot, [0, 1, 2, 3])
                if s == 0:
                    dbg("hT", hT[:], [128, KC * S], ["hT%d" % i for i in range(NB)])
                for e in ("pe", "act", "dve", "pool", "sp"):
                    bld.finish(e)
            if stop_after == "A":
                break
            phase_B(s)
            if stop_after == "B":
                break
            phase_C(s)
            if stop_after in ("C", "C1"):
                break
            phase_D(s)
            if stop_after == "D":
                break
          phase_E(s)

        for e in ("sp",):
            bld.finish(e)
    print("[build] instructions=%d waits=%d" % (bld.ninst, bld.nwaits))
    return nc, dbg_outs


NEGD = -1.0e9


def _tsel():
    p = np.arange(128)[:, None].astype(np.float64)
    u = np.arange(2048)[None, :].astype(np.float64)
    rel = u - 1920.0
    t = np.where(rel <= p, rel - p, NEGD)
    return np.ascontiguousarray(t.astype(np.float32))


def _twin():
    p = np.arange(128)[:, None].astype(np.float64)
    c = np.arange(640)[None, :].astype(np.float64)
    dist = p + 512.0 - c
    t = np.where((dist >= 0) & (dist < 512), -dist, NEGD)
    return np.ascontiguousarray(t.astype(np.float32))


def _tcmp():
    out = np.full((128, 16, 128), NEGD, np.float64)
    p = np.arange(128)[:, None]
    n = np.arange(127)[None, :]
    for i in range(16):
        t = 128 * i + p
        dist = t - (16 * n + 31)
        out[:, i, :127] = np.where(dist >= 0, -dist, NEGD)
    return np.ascontiguousarray(out.reshape(128, 16 * 128).astype(np.float32))


def _selt():
    mul = np.zeros((128, 16, 32), np.float32)
    add = np.zeros((128, 16, 32), np.float32)
    j = np.arange(32)[None, :]
    for i in range(16):
        cur = (128 * i + np.arange(128)[:, None]) // 64
        valid = j <= cur
        forced = (j == 0) | (j == cur) | (j == cur - 1)
        mul[:, i, :] = (valid & ~forced).astype(np.float32)
        add[:, i, :] = np.where(valid, np.where(forced, 1.0e4, 0.0), -1.0e30).astype(np.float32)
    return np.ascontiguousarray(mul.reshape(128, 512)), np.ascontiguousarray(add.reshape(128, 512))


def _w1(w1):
    w = np.asarray(w1, np.float32).reshape(32, 64, 128).transpose(1, 0, 2).reshape(64, 32 * 128)
    return np.ascontiguousarray(np.concatenate([w, w], axis=0))


def host_inputs(inputs, nseq=2, ncores=NCORES):
    f = lambda a: np.ascontiguousarray(np.asarray(a, dtype=np.float32))
    x = f(inputs["x"])
    c = f(inputs["c"])
    common = {
        "w_ada": f(inputs["w_ada"][0]),
        "b_ada": f(inputs["b_ada"][0]).reshape(1, -1),
        "gmixT": f(f(inputs["norm_mix_g"][0]).reshape(KC, 128).T),
        "gffnT": f(f(inputs["norm_ffn_g"][0]).reshape(KC, 128).T),
        "gfin": f(inputs["norm_final_g"]).reshape(1, -1),
        "w_in": f(inputs["w_in"][0]),
        "w_proj_a": f(inputs["w_proj_a"][0]),
        "w_proj_b": f(inputs["w_proj_b"][0]),
        "w_out": f(inputs["w_out"][0]),
        "w_ffn_gate": f(inputs["w_ffn_gate"][0]),
        "w_ffn_up": f(inputs["w_ffn_up"][0]),
        "w_ffn_down": f(inputs["w_ffn_down"][0]),
        "ident": np.eye(128, dtype=np.float32),
        "sel2": f(np.concatenate([np.repeat(np.array([[1.0], [0.0]]), 128, axis=1),
                                  np.repeat(np.array([[0.0], [1.0]]), 128, axis=1)], axis=1)),
        "i2": np.eye(2, dtype=np.float32),
        "tri_lt": f(np.tril(np.ones((128, 128)), -1)),
        "tsel": _tsel(), "twin": _twin(), "tcmp": _tcmp(), "selmul": _selt()[0], "seladd": _selt()[1],
        "rv0": f((np.arange(128) >= 31).astype(np.float32).reshape(128, 1)),
        "w1k": _w1(inputs["cmp_w1_k"][0]), "w1v": _w1(inputs["cmp_w1_v"][0]),
        "posk": f(np.tile(f(inputs["cmp_pos_k"][0]).T, (2, 1))), "posv": f(np.tile(f(inputs["cmp_pos_v"][0]).T, (2, 1))),
        "w2k": f(inputs["cmp_w2_k"][0]), "w2v": f(inputs["cmp_w2_v"][0]),
        "tri_ge": f(np.triu(np.ones((128, 128)), 0)),
    }
    maps = []
    for k in range(ncores):
        xs = x[k * nseq:(k + 1) * nseq]
        cb = c[k * nseq:(k + 1) * nseq]
        csT = np.zeros((128, 16), np.float32)
        ct = cb.reshape(nseq, KC, 128).transpose(2, 1, 0)
        csT.reshape(128, KC, 2)[:, :, :nseq] = ct
        m = dict(common)
        m["x"] = f(xs)
        m["csT"] = csT
        maps.append(m)
    return maps


def kernel(**inputs):
    nseq = 2
    nc, _ = build(nseq=nseq)
    maps = host_inputs(inputs, nseq=nseq, ncores=NCORES)
    res = run_bass_kernel_spmd(nc, maps, core_ids=list(range(NCORES)))
    out = np.concatenate([r["out"] for r in res.results], axis=0)
    return out.astype(np.float32)
```

```python
import numpy as np
from contextlib import ExitStack
import concourse.bass as bass
import concourse.mybir as mybir
from concourse.bass_utils import run_bass_kernel_spmd

F32 = mybir.dt.float32
BF16 = mybir.dt.bfloat16
AF = mybir.ActivationFunctionType
ALU = mybir.AluOpType
AX = mybir.AxisListType

S = 2048
D = 1024
KC = 8
NT = 16
TB = 512
NB = S // TB
DIN = 4888
DFF = 2816
NFF = 22
EPS = 1e-6
NCORES = 8
C_QA = 0
C_KV = 512
C_GA = 1280
C_SB = 1304
C_MG = 2840
BIG = 1.0e9
SUB_ENG = "dve"
import os
SKIP1 = bool(int(os.environ.get("SKIP1", "0")))
SKIP2 = bool(int(os.environ.get("SKIP2", "0")))


class Builder:
    def __init__(self, nc, es, n_dma_sp=20, n_dma_pool=20):
        self.nc = nc
        self.es = es
        self.engs = {"pe": nc.tensor, "act": nc.scalar, "dve": nc.vector, "pool": nc.gpsimd, "sp": nc.sync}
        self.sem = {}
        self.cnt = {}
        for e in ("pe", "act", "dve", "pool"):
            self.sem[e] = es.enter_context(nc.semaphore("c_" + e))
            self.cnt[e] = 0
        self.dsems = {"sp": [], "pool": []}
        for q, n in (("sp", n_dma_sp), ("pool", n_dma_pool)):
            for i in range(n):
                name = "d_%s%d" % (q, i)
                self.sem[name] = es.enter_context(nc.semaphore(name))
                self.cnt[name] = 0
                self.dsems[q].append(name)
        self.drr = {"sp": 0, "pool": 0}
        self.seen = {e: {} for e in self.engs}
        self.res = {}
        self.nwaits = 0
        self.ninst = 0

    def _need(self, eng, reads, writes):
        need = {}

        def add(ev, same_ok):
            if ev is None:
                return
            s, v = ev
            if same_ok and s == eng:
                return
            if need.get(s, 0) < v:
                need[s] = v

        for k in reads:
            r = self.res.get(k)
            if r is not None:
                add(r[0], False)
        for k in writes:
            r = self.res.get(k)
            if r is not None:
                add(r[0], True)
                for s, v in r[1].items():
                    add((s, v), True)
        out = []
        seen = self.seen[eng]
        for s, v in need.items():
            if seen.get(s, 0) < v:
                out.append((s, v))
                seen[s] = v
        return out

    def _emit_waits(self, eng, waits):
        e = self.engs[eng]
        for s, v in waits:
            e.wait_ge(self.sem[s], v)
            self.nwaits += 1

    def _post(self, ev, reads, writes):
        for k in reads:
            r = self.res.get(k)
            if r is None:
                r = [None, {}]
                self.res[k] = r
            if r[1].get(ev[0], 0) < ev[1]:
                r[1][ev[0]] = ev[1]
        for k in writes:
            self.res[k] = [ev, {}]

    def last(self, key):
        r = self.res.get(key)
        return None if r is None else r[0]

    def op(self, eng, fn, reads=(), writes=()):
        waits = self._need(eng, reads, writes)
        self._emit_waits(eng, waits)
        ins = fn(self.engs[eng])
        self.cnt[eng] += 1
        ins.then_inc(self.sem[eng], 1)
        self._post((eng, self.cnt[eng]), reads, writes)
        self.ninst += 1
        return ins

    def dma(self, q, pairs, reads=(), writes=()):
        sems = self.dsems[q]
        i = self.drr[q]
        self.drr[q] = (i + 1) % len(sems)
        name = sems[i]
        waits = self._need(q, reads, writes)
        prev = self.cnt[name]
        if self.seen[q].get(name, 0) < prev:
            waits.append((name, prev))
            self.seen[q][name] = prev
        self._emit_waits(q, waits)
        e = self.engs[q]
        for o, i_ in pairs:
            e.dma_start(out=o, in_=i_).then_inc(self.sem[name], 16)
            self.ninst += 1
        self.cnt[name] = prev + 16 * len(pairs)
        self._post((name, self.cnt[name]), reads, writes)

    def finish(self, eng="sp"):
        e = self.engs[eng]
        for name in self.dsems["sp"] + self.dsems["pool"]:
            if self.cnt[name] > 0:
                e.wait_ge(self.sem[name], self.cnt[name])
        for c in ("pe", "act", "dve", "pool"):
            if self.cnt[c] > 0:
                e.wait_ge(self.sem[c], self.cnt[c])


class Rot:
    def __init__(self, items):
        self.items = items
        self.i = 0

    def next(self):
        it = self.items[self.i]
        self.i = (self.i + 1) % len(self.items)
        return it


def sb_ap(t, off, dims):
    return bass.AP(t, off, dims)


def build(nseq=2, debug=(), stop_after=None, branches=("cmp", "win", "sel")):
    nc = bass.Bass("TRN2", target_bir_lowering=False)
    dbg_outs = {}

    def din(name, shape):
        return nc.dram_tensor(name, list(shape), F32, kind="ExternalInput").ap()

    x_d = din("x", [nseq, S, D])
    csT_d = din("csT", [128, 16])
    w_ada_d = din("w_ada", [D, 6 * D])
    b_ada_d = din("b_ada", [1, 6 * D])
    gmixT_d = din("gmixT", [128, 8])
    gffnT_d = din("gffnT", [128, 8])
    gfin_d = din("gfin", [1, D])
    w_in_d = din("w_in", [D, DIN])
    w_pa_d = din("w_proj_a", [512, D])
    w_pb_d = din("w_proj_b", [512, D])
    w_out_d = din("w_out", [D, D])
    w_fg_d = din("w_ffn_gate", [D, DFF])
    w_fu_d = din("w_ffn_up", [D, DFF])
    w_fd_d = din("w_ffn_down", [DFF, D])
    ident_d = din("ident", [128, 128])
    sel2_d = din("sel2", [2, 256])
    i2_d = din("i2", [2, 2])
    tri_lt_d = din("tri_lt", [128, 128])
    tsel_d = din("tsel", [128, 2048])
    twin_d = din("twin", [128, 640])
    tcmp_d = din("tcmp", [128, 16 * 128])
    selmul_d = din("selmul", [128, 16 * 32])
    seladd_d = din("seladd", [128, 16 * 32])
    rv0_d = din("rv0", [128, 1])
    w1k_d = din("w1k", [128, 32 * 128])
    w1v_d = din("w1v", [128, 32 * 128])
    posk_d = din("posk", [128, 32])
    posv_d = din("posv", [128, 32])
    w2k_d = din("w2k", [128, 64])
    w2v_d = din("w2v", [128, 64])
    tri_ge_d = din("tri_ge", [128, 128])
    out_d = nc.dram_tensor("out", [nseq, S, D], F32, kind="ExternalOutput").ap()

    with ExitStack() as es:
        bld = Builder(nc, es)
        op = bld.op
        dma = bld.dma

        uid = [0]

        def SB(name, shape, dt=F32, stack=es):
            uid[0] += 1
            return stack.enter_context(nc.sbuf_tensor("s%d_%s" % (uid[0], name), list(shape), dt))

        def dbg(name, ap, shape, reads):
            if name not in debug:
                return
            o = nc.dram_tensor("dbg_" + name, list(shape), ap.dtype, kind="ExternalOutput").ap()
            dbg_outs[name] = o
            dma("sp", [(o, ap)], reads=reads, writes=())

        psum = [es.enter_context(nc.psum_tensor("ps%d" % i, [128, 512], F32)) for i in range(8)]

        def PSF(i):
            return psum[i][:]

        def PSB(i):
            return psum[i][:].bitcast(BF16)

        def pk(i):
            return "ps%d" % i

        identb = SB("identb", [128, 128], BF16)
        identf = SB("identf", [128, 128], F32)
        modT = SB("modT", [128, 64], F32)
        aT = SB("aT", [128, 32], F32)
        shT = SB("shT", [128, 32], F32)
        gbc = SB("gbc", [128, 4 * D], F32)
        gfin = SB("gfin_bc", [128, D], F32)
        tri_lt = SB("tri_lt", [128, 128], F32)
        tri_ge = SB("tri_ge", [128, 128], F32)
        ones512 = SB("ones512", [128, 512], F32)
        w_in_v = w_in_d.rearrange("(c p) n -> p c n", p=128)
        dma("sp", [(tri_lt[:], tri_lt_d), (tri_ge[:], tri_ge_d)], writes=["tri"])
        op("dve", lambda e: e.memset(ones512[:], 1.0), writes=["ones512"])

        dma("sp", [(identf[:], ident_d)], writes=["identf"])
        dma("pool", [(identb[:], ident_d)], writes=["identb"])
        dma("sp", [(gfin[:], bass.AP(gfin_d.tensor, 0, [[0, 128], [1, D]]))], writes=["gfin"])

        with ExitStack() as p0:
            cs = SB("cs", [128, 16], F32, p0)
            gT = SB("gT", [128, 16], F32, p0)
            sel2 = SB("sel2", [2, 256], F32, p0)
            i2 = SB("i2", [2, 2], F32, p0)
            ones12 = SB("ones12", [1, 2], F32, p0)
            modsb = SB("modsb", [2, 6 * D], F32, p0)
            wa = [SB("wa%d" % i, [128, KC * 512], F32, p0) for i in range(2)]
            ba = [SB("ba%d" % i, [1, 512], F32, p0) for i in range(2)]
            dma("sp", [(cs[:], csT_d)], writes=["cs"])
            dma("sp", [(gT[:, 0:8], gmixT_d), (gT[:, 8:16], gffnT_d)], writes=["gT"])
            dma("sp", [(sel2[:], sel2_d), (i2[:], i2_d)], writes=["sel2", "i2"])
            op("dve", lambda e: e.memset(ones12[:], 1.0), writes=["ones12"])
            op("act", lambda e: e.activation(out=cs[:], in_=cs[:], func=AF.Silu), reads=["cs"], writes=["cs"])
            cs3 = cs[:].rearrange("p (c b) -> p c b", b=2)
            w_ada_v = w_ada_d.rearrange("(c p) n -> p c n", p=128)
            for nb in range(12):
                wt = wa[nb % 2]
                bt = ba[nb % 2]
                wk = "wa%d" % (nb % 2)
                wt3 = wt[:].rearrange("p (c n) -> p c n", c=KC)
                dma("sp", [(wt3[:, c, :], w_ada_v[:, c, nb * 512:(nb + 1) * 512]) for c in range(KC)]
                    + [(bt[:], b_ada_d[:, nb * 512:(nb + 1) * 512])], writes=[wk])
                pb = 4 + (nb % 2)
                for c in range(KC):
                    op("pe", lambda e, c=c: e.matmul(psum[pb][0:2, :], lhsT=cs3[:, c, :], rhs=wt3[:, c, :],
                                                     start=(c == 0), stop=False),
                       reads=["cs", wk], writes=[pk(pb)])
                op("pe", lambda e: e.matmul(psum[pb][0:2, :], lhsT=ones12[:], rhs=bt[:], start=False, stop=True),
                   reads=["ones12", wk], writes=[pk(pb)])
                op("act", lambda e: e.copy(out=modsb[:, nb * 512:(nb + 1) * 512], in_=psum[pb][0:2, :]),
                   reads=[], writes=[pk(pb), "modsb"])
            for b in range(nseq):
                for w in range(2):
                    for hh in range(2):
                        col0 = (2048 if w == 0 else 5120) + hh * 512
                        pb = 4 + (hh % 2)
                        op("pe", lambda e, b=b, col0=col0, pb=pb: e.matmul(
                            psum[pb][:], lhsT=sel2[:, b * 128:(b + 1) * 128], rhs=modsb[:, col0:col0 + 512],
                            start=True, stop=True), reads=["sel2", "modsb"], writes=[pk(pb)])
                        o0 = (b * 2 + w) * D + hh * 512
                        op("act", lambda e, pb=pb, o0=o0: e.copy(out=gbc[:, o0:o0 + 512], in_=psum[pb][:]),
                           writes=[pk(pb), "gbc"])
            for j in range(32):
                col0 = [0, 1024, 3072, 4096][j // 8] + (j % 8) * 128
                op("pe", lambda e, j=j, col0=col0: e.matmul(psum[6][:, j * 2:j * 2 + 2], lhsT=modsb[:, col0:col0 + 128],
                                                            rhs=i2[:], start=True, stop=True),
                   reads=["modsb", "i2"], writes=[pk(6)])
            op("dve", lambda e: e.tensor_copy(out=modT[:], in_=psum[6][:, 0:64]), writes=[pk(6), "modT"])
            for w in range(2):
                for b in range(nseq):
                    i0 = (w * 2 + b) * 8
                    src_sh = bass.AP(modT, (w * 16) * 2 + b, [[64, 128], [2, 8]])
                    src_sc = bass.AP(modT, (w * 16 + 8) * 2 + b, [[64, 128], [2, 8]])
                    op("dve", lambda e, i0=i0, src_sh=src_sh: e.tensor_copy(out=shT[:, i0:i0 + 8], in_=src_sh),
                       reads=["modT"], writes=["shT"])
                    op("dve", lambda e, i0=i0, src_sc=src_sc, w=w: e.scalar_tensor_tensor(
                        out=aT[:, i0:i0 + 8], in0=src_sc, scalar=1.0, in1=gT[:, w * 8:(w + 1) * 8],
                        op0=ALU.add, op1=ALU.mult), reads=["modT", "gT"], writes=["aT"])
            dbg("modT", modT[:], [128, 64], ["modT"])
            dbg("aT", aT[:], [128, 32], ["aT"])
            dbg("shT", shT[:], [128, 32], ["shT"])
            dbg("gbc", gbc[:], [128, 4 * D], ["gbc"])
            for e in ("pe", "act", "dve", "pool", "sp"):
                bld.finish(e)

        def norm_block_to_T(xtiles, xkeys, widx, b, dstT3, dcol0, dkey, xn_rot, small_rot, tp_banks):
            xns = []
            for j in range(4):
                xn, xnk = xn_rot.next()
                sm, smk = small_rot.next()
                xt = xtiles[j]
                op("act", lambda e, xt=xt, xn=xn, sm=sm: e.activation(out=xn[:], in_=xt, func=AF.Square,
                                                                      accum_out=sm[:, 0:1]),
                   reads=[xkeys[j]], writes=[xnk, smk])
                op("act", lambda e, sm=sm: e.activation(out=sm[:, 1:2], in_=sm[:, 0:1], func=AF.Sqrt,
                                                        scale=1.0 / D, bias=epsb[:, 0:1]),
                   reads=[smk, "epsb"], writes=[smk])
                op("dve", lambda e, sm=sm: e.reciprocal(out=sm[:, 2:3], in_=sm[:, 1:2]), reads=[smk], writes=[smk])
                op("dve", lambda e, xt=xt, xn=xn, sm=sm: e.tensor_scalar(out=xn[:], in0=xt, scalar1=sm[:, 2:3],
                                                                         scalar2=None, op0=ALU.mult),
                   reads=[xkeys[j], smk], writes=[xnk])
                xns.append((xn, xnk))
            for c in range(KC):
                pb = tp_banks[c // 2]
                for j in range(4):
                    xn, xnk = xns[j]
                    o = PSB(pb)[:, (c % 2) * 512 + j * 128:(c % 2) * 512 + (j + 1) * 128]
                    op("pe", lambda e, o=o, xn=xn, c=c: e.transpose(o, xn[:, c * 128:(c + 1) * 128], identb[:]),
                       reads=[xnk, "identb"], writes=[pk(pb)])
            i0 = (widx * 2 + b) * 8
            for c in range(KC):
                pb = tp_banks[c // 2]
                src = PSB(pb)[:, (c % 2) * 512:(c % 2) * 512 + 512]
                dst = dstT3[:, c, dcol0:dcol0 + 512]
                if c % 2 == 0:
                    op("act", lambda e, src=src, dst=dst, c=c: e.activation(
                        out=dst, in_=src, func=AF.Identity, scale=aT[:, i0 + c:i0 + c + 1],
                        bias=shT[:, i0 + c:i0 + c + 1]), reads=["aT", "shT"], writes=[pk(pb), dkey])
                else:
                    op("dve", lambda e, src=src, dst=dst, c=c: e.tensor_scalar(
                        out=dst, in0=src, scalar1=aT[:, i0 + c:i0 + c + 1], scalar2=shT[:, i0 + c:i0 + c + 1],
                        op0=ALU.mult, op1=ALU.add), reads=["aT", "shT"], writes=[pk(pb), dkey])


        def pipeline(units, stages, lags):
            n = len(units)
            maxlag = max(lags)
            for k in range(n + maxlag):
                for st, lg in zip(stages, lags):
                    u = k - lg
                    if 0 <= u < n:
                        st(units[u])

        def phase_B(s):
            with ExitStack() as pb_:
                wsb = SB("wsb", [128, KC * 1536], BF16, pb_)
                wsb3 = wsb[:].rearrange("p (c n) -> p c n", c=KC)
                v_all = SB("v_all", [128, NT * 512], BF16, pb_)
                qTs = [SB("qTb%d" % i, [128, S], BF16, pb_) for i in range(2)]
                kTs = [SB("kTb%d" % i, [128, S], BF16, pb_) for i in range(2)]
                om_rot = Rot([(SB("om%d" % i, [128, 512], F32, pb_), "om%d" % i) for i in range(4)])
                P_rot = Rot([(SB("P%d" % i, [128, 514], F32, pb_), "P%d" % i) for i in range(6)])
                a_rot = Rot([(SB("a%d" % i, [128, 512], BF16, pb_), "a%d" % i) for i in range(5)])
                aT_rot = Rot([(SB("aT%d" % i, [128, 512], BF16, pb_), "aTs%d" % i) for i in range(4)])
                z_rot = Rot([0, 1, 2])
                t_rot = Rot([3, 4])
                o_rot = Rot([5, 6])
                dma("pool", [(wsb3[:, c, :], w_in_v[:, c, C_SB:C_SB + 1536]) for c in range(KC)], writes=["wsb"])
                hkeys = ["hT%d" % i for i in range(NB)]
                pr_rot = Rot([0, 1, 2, 3])
                for tt in range(NT):
                    pb = pr_rot.next()
                    for c in range(KC):
                        op("pe", lambda e, c=c, pb=pb, tt=tt: e.matmul(
                            PSF(pb), lhsT=hT3[:, c, tt * 128:(tt + 1) * 128], rhs=wsb3[:, c, 1024:1536],
                            start=(c == 0), stop=(c == KC - 1)), reads=[hkeys[tt // 4], "wsb"], writes=[pk(pb)])
                    if tt % 2 == 0:
                        op("act", lambda e, pb=pb, tt=tt: e.copy(out=v_all[:, tt * 512:(tt + 1) * 512], in_=PSF(pb)),
                           writes=[pk(pb), "v_all"])
                    else:
                        op("dve", lambda e, pb=pb, tt=tt: e.tensor_copy(out=v_all[:, tt * 512:(tt + 1) * 512], in_=PSF(pb)),
                           writes=[pk(pb), "v_all"])
                for hp in range(4):
                    qT = qTs[hp % 2]
                    kT = kTs[hp % 2]
                    qk = "qTb%d" % (hp % 2)
                    kk = "kTb%d" % (hp % 2)
                    for blk in range(NB):
                        for which in range(2):
                            pb = pr_rot.next()
                            col0 = which * 512 + hp * 128
                            for c in range(KC):
                                op("pe", lambda e, c=c, pb=pb, col0=col0, blk=blk: e.matmul(
                                    PSF(pb), lhsT=wsb3[:, c, col0:col0 + 128], rhs=hT3[:, c, blk * 512:(blk + 1) * 512],
                                    start=(c == 0), stop=(c == KC - 1)), reads=[hkeys[blk], "wsb"], writes=[pk(pb)])
                            if which == 0:
                                op("act", lambda e, pb=pb, blk=blk, qT=qT: e.activation(
                                    out=qT[:, blk * 512:(blk + 1) * 512], in_=PSF(pb), func=AF.Copy, scale=0.125),
                                   writes=[pk(pb), qk])
                            else:
                                op("dve", lambda e, pb=pb, blk=blk, kT=kT: e.tensor_copy(
                                    out=kT[:, blk * 512:(blk + 1) * 512], in_=PSF(pb)), writes=[pk(pb), kk])
                    units = []
                    for i in range(NT):
                        jmax = i // 4
                        ob = o_rot.next()
                        nun = 2 * (jmax + 1)
                        cnt = 0
                        for j in range(jmax, -1, -1):
                            for a in range(2):
                                w = min(512, 128 * (i + 1) - 512 * j)
                                cnt += 1
                                units.append(dict(i=i, j=j, a=a, w=w, diag=(j == jmax), ob=ob,
                                                  first=(j == jmax), last=(j == 0), final=(cnt == nun)))
                    lastP = {}

                    def s1(u):
                        a, i, j, w = u["a"], u["i"], u["j"], u["w"]
                        zb = z_rot.next()
                        u["zb"] = zb
                        op("pe", lambda e: e.matmul(PSF(zb)[:, 0:w], lhsT=qT[a * 64:(a + 1) * 64, i * 128:(i + 1) * 128],
                                                    rhs=kT[a * 64:(a + 1) * 64, 512 * j:512 * j + w], start=True, stop=True),
                           reads=[qk, kk], writes=[pk(zb)])

                    def sA(u):
                        a, i, j, w, zb = u["a"], u["i"], u["j"], u["w"], u["zb"]
                        om, omk = om_rot.next()
                        u["om"], u["omk"] = om, omk
                        op("act", lambda e: e.activation(out=om[:, 0:w], in_=PSF(zb)[:, 0:w], func=AF.Sigmoid, scale=-1.0),
                           writes=[pk(zb), omk])
                        u["omev"] = bld.last(omk)

                    def s2(u):
                        a, i, j, w, zb = u["a"], u["i"], u["j"], u["w"], u["zb"]
                        om, omk = u["om"], u["omk"]
                        assert bld.last(omk) == u["omev"], "rotation depth too small (om)"
                        P, Pk = P_rot.next()
                        at, ak = a_rot.next()
                        u["at"], u["ak"] = at, ak
                        if u["diag"]:
                            op("dve", lambda e: e.tensor_tensor(out=om[:, w - 128:w], in0=om[:, w - 128:w], in1=tri_ge[:],
                                                                op=ALU.max), reads=["tri"], writes=[omk])
                            op("pool", lambda e: e.memset(P[:, w:w + 1], 1.0), writes=[Pk + "c"])
                            init = 1.0
                            rds = [omk, "ones512"]
                        else:
                            Pp, Ppk = lastP[a]
                            op("pool", lambda e: e.tensor_copy(out=P[:, w:w + 1], in_=Pp[:, 0:1]), reads=[Ppk], writes=[Pk + "c"])
                            init = Pp[:, 0:1]
                            rds = [omk, "ones512", Ppk]
                        rin = bass.AP(om, w - 1, [[512, 128], [-1, w]])
                        rout = bass.AP(P, w - 1, [[514, 128], [-1, w]])
                        op("dve", lambda e: e.tensor_tensor_scan(out=rout, data0=rin, data1=ones512[:, 0:w], initial=init,
                                                                 op0=ALU.mult, op1=ALU.mult), reads=rds, writes=[Pk])
                        op("dve", lambda e: e.tensor_tensor(out=at[:, 0:w], in0=P[:, 1:w + 1], in1=P[:, 0:w], op=ALU.subtract),
                           reads=[Pk, Pk + "c"], writes=[ak])
                        lastP[a] = (P, Pk)
                        u["aev"] = bld.last(ak)

                    def s3(u):
                        w, at, ak = u["w"], u["at"], u["ak"]
                        assert bld.last(ak) == u["aev"], "rotation depth too small"
                        tb = t_rot.next()
                        u["tb"] = tb
                        for kb in range(w // 128):
                            op("pe", lambda e, kb=kb: e.transpose(PSB(tb)[:, kb * 128:(kb + 1) * 128],
                                                                  at[:, kb * 128:(kb + 1) * 128], identb[:]),
                               reads=[ak, "identb"], writes=[pk(tb)])

                    def s4(u):
                        w, tb = u["w"], u["tb"]
                        aTt, aTk = aT_rot.next()
                        u["aT"], u["aTk"] = aTt, aTk
                        op("act", lambda e: e.copy(out=aTt[:, 0:w], in_=PSB(tb)[:, 0:w]), writes=[pk(tb), aTk])
                        u["aTev"] = bld.last(aTk)

                    def s5(u):
                        a, i, j, w, ob = u["a"], u["i"], u["j"], u["w"], u["ob"]
                        aTt, aTk = u["aT"], u["aTk"]
                        assert bld.last(aTk) == u["aTev"], "rotation depth too small (aT)"
                        nkb = w // 128
                        for kb in range(nkb):
                            kt = 4 * j + kb
                            vc0 = kt * 512 + (2 * hp + a) * 64
                            op("pe", lambda e, kb=kb, vc0=vc0: e.matmul(
                                PSF(ob)[a * 64:(a + 1) * 64, 0:128], lhsT=v_all[:, vc0:vc0 + 64],
                                rhs=aTt[:, kb * 128:(kb + 1) * 128],
                                start=(u["first"] and kb == 0), stop=(u["last"] and kb == nkb - 1)),
                               reads=[aTk, "v_all"], writes=[pk(ob)])
                        if u["final"]:
                            op("dve", lambda e: e.tensor_copy(out=obT[:, hp * S + i * 128:hp * S + (i + 1) * 128],
                                                              in_=PSF(ob)[:, 0:128]), writes=[pk(ob), "obT%d" % (i // 4)])

                    pipeline(units, [s5, s3, s4, s2, sA, s1], [4, 3, 3, 2, 1, 0])
                if s == 0:
                    dbg("obT", obT[:], [128, 4 * S], ["obT%d" % i for i in range(NB)])
                for e in ("pe", "act", "dve", "pool", "sp"):
                    bld.finish(e)

        def phase_C(s):
            hkeys = ["hT%d" % i for i in range(NB)]
            with ExitStack() as pc_:
                qA = [SB("qA%d" % r, [128, S], BF16, pc_) for r in range(4)]
                kselT = SB("kselT", [128, S], BF16, pc_)
                kwinT = SB("kwinT", [128, S], BF16, pc_)
                vsel = SB("vsel", [128, NT * 128], BF16, pc_)
                vwin = SB("vwin", [128, NT * 128], BF16, pc_)
                gates = SB("gates", [128, NT * 24], F32, pc_)
                kcT2 = SB("kcT2", [128, 128], BF16, pc_)
                vcs = [SB("vcs%d" % g, [128, 64], BF16, pc_) for g in range(2)]
                negb = SB("negb", [128, 384], F32, pc_)
                pr_rot = Rot([0, 1, 2, 3])
                with ExitStack() as c1:
                    wns = SB("wns", [128, KC * 1304], BF16, c1)
                    wns3 = wns[:].rearrange("p (c n) -> p c n", c=KC)
                    kcmpT = SB("kcmpT", [128, S], BF16, c1)
                    vcmpT = SB("vcmpT", [128, S], BF16, c1)
                    w1k = SB("w1k", [128, 32 * 128], BF16, c1)
                    w1v = SB("w1v", [128, 32 * 128], BF16, c1)
                    posk = SB("posk", [128, 32], BF16, c1)
                    posv = SB("posv", [128, 32], BF16, c1)
                    w2k = SB("w2k", [128, 64], BF16, c1)
                    w2v = SB("w2v", [128, 64], BF16, c1)
                    dma("pool", [(wns3[:, c, :], w_in_v[:, c, 0:1304]) for c in range(KC)], writes=["wns"])
                    dma("pool", [(w1k[:], w1k_d), (w1v[:], w1v_d), (posk[:], posk_d), (posv[:], posv_d),
                                 (w2k[:], w2k_d), (w2v[:], w2v_d)], writes=["cmpw"])
                    for tt in range(NT):
                        pb = pr_rot.next()
                        for c in range(KC):
                            op("pe", lambda e, c=c, pb=pb, tt=tt: e.matmul(
                                PSF(pb)[:, 0:408], lhsT=hT3[:, c, tt * 128:(tt + 1) * 128], rhs=wns3[:, c, 896:1304],
                                start=(c == 0), stop=(c == KC - 1)), reads=[hkeys[tt // 4], "wns"], writes=[pk(pb)])
                        op("act", lambda e, pb=pb, tt=tt: e.copy(out=vsel[:, tt * 128:(tt + 1) * 128], in_=PSF(pb)[:, 0:128]),
                           writes=[pk(pb), "vsel"])
                        op("dve", lambda e, pb=pb, tt=tt: e.tensor_copy(out=vwin[:, tt * 128:(tt + 1) * 128],
                                                                        in_=PSF(pb)[:, 256:384]), writes=[pk(pb), "vwin"])
                        op("act", lambda e, pb=pb, tt=tt: e.activation(out=gates[:, tt * 24:(tt + 1) * 24],
                                                                       in_=PSF(pb)[:, 384:408], func=AF.Sigmoid),
                           writes=[pk(pb), "gates"])
                    for (dst, dk, col0) in ((kcmpT, "kcmpT", 512), (vcmpT, "vcmpT", 640), (kselT, "kselT", 768),
                                            (kwinT, "kwinT", 1024)):
                        for blk in range(NB):
                            pb = pr_rot.next()
                            for c in range(KC):
                                op("pe", lambda e, c=c, pb=pb, blk=blk, col0=col0: e.matmul(
                                    PSF(pb), lhsT=wns3[:, c, col0:col0 + 128], rhs=hT3[:, c, blk * 512:(blk + 1) * 512],
                                    start=(c == 0), stop=(c == KC - 1)), reads=[hkeys[blk], "wns"], writes=[pk(pb)])
                            if blk % 2 == 0:
                                op("act", lambda e, pb=pb, blk=blk, dst=dst: e.copy(out=dst[:, blk * 512:(blk + 1) * 512],
                                                                                   in_=PSF(pb)), writes=[pk(pb), dk])
                            else:
                                op("dve", lambda e, pb=pb, blk=blk, dst=dst: e.tensor_copy(
                                    out=dst[:, blk * 512:(blk + 1) * 512], in_=PSF(pb)), writes=[pk(pb), dk])
                    qsq_rot = Rot([(SB("qsq%d" % i, [128, 512], BF16, c1), "qsq%d" % i) for i in range(2)])
                    ones_bf = SB("ones_bf", [128, 128], BF16, c1)
                    op("dve", lambda e: e.memset(ones_bf[:], 1.0), writes=["ones_bf"])
                    for r in range(4):
                        for blk in range(NB):
                            pb = pr_rot.next()
                            for g in range(2):
                                col0 = (g * 4 + r) * 64
                                for c in range(KC):
                                    op("pe", lambda e, c=c, pb=pb, blk=blk, col0=col0, g=g: e.matmul(
                                        PSF(pb)[g * 64:(g + 1) * 64, :], lhsT=wns3[:, c, col0:col0 + 64],
                                        rhs=hT3[:, c, blk * 512:(blk + 1) * 512], start=(c == 0), stop=(c == KC - 1)),
                                       reads=[hkeys[blk], "wns"], writes=[pk(pb)])
                            op("act", lambda e, pb=pb, blk=blk, r=r: e.activation(
                                out=qA[r][:, blk * 512:(blk + 1) * 512], in_=PSF(pb), func=AF.Copy, scale=0.125),
                               writes=[pk(pb), "qA%d" % r])
                            qs, qsk = qsq_rot.next()
                            op("act", lambda e, pb=pb, qs=qs: e.activation(out=qs[:], in_=PSF(pb), func=AF.Square),
                               writes=[pk(pb), qsk])
                            for j in range(4 if not SKIP1 else 0):
                                for g in range(2):
                                    col = (g * 4 + r) * 16 + blk * 4 + j
                                    op("pe", lambda e, j=j, g=g, col=col, qs=qs: e.matmul(
                                        PSF(7 - g)[:, col:col + 1], lhsT=qs[g * 64:(g + 1) * 64, j * 128:(j + 1) * 128],
                                        rhs=ones_bf[g * 64:(g + 1) * 64, 0:1], start=True, stop=True),
                                       reads=[qsk, "ones_bf"], writes=[pk(7 - g)])
                    cu = SB("cu", [128, 128], F32, c1)
                    ct1 = SB("ct1", [128, 128], F32, c1)
                    ct2 = SB("ct2", [128, 128], F32, c1)
                    cgel = SB("cgel", [128, 128], BF16, c1)
                    cconst = SB("cconst", [128, 1], F32, c1)
                    op("dve", lambda e: e.memset(kcT2[:], 0.0), writes=["kcT2"])
                    for g in range(2):
                        op("dve", lambda e, g=g: e.memset(vcs[g][:], 0.0), writes=["vcs%d" % g])
                    for kv in range(2):
                        xT, xk = (kcmpT, "kcmpT") if kv == 0 else (vcmpT, "vcmpT")
                        w1 = w1k if kv == 0 else w1v
                        pos = posk if kv == 0 else posv
                        w2 = w2k if kv == 0 else w2v
                        for g in range(2):
                            pb = pr_rot.next()
                            rows = slice(g * 64, (g + 1) * 64)
                            for l in range(32):
                                op("pe", lambda e, l=l: e.matmul(PSF(pb)[:, 0:127], lhsT=w1[rows, l * 128:(l + 1) * 128],
                                                                 rhs=xT[rows, l:l + 2017:16], start=(l == 0), stop=(l == 31)),
                                   reads=[xk, "cmpw"], writes=[pk(pb)])
                            for l in range(32):
                                op("pe", lambda e, l=l: e.matmul(PSF(pb)[:, 127:128], lhsT=w1[rows, l * 128:(l + 1) * 128],
                                                                 rhs=pos[rows, l:l + 1], start=(l == 0), stop=(l == 31)),
                                   reads=["cmpw"], writes=[pk(pb)])
                            op("act", lambda e: e.copy(out=cconst[:], in_=PSF(pb)[:, 127:128]), writes=[pk(pb), "cconst"])
                            op("act", lambda e: e.activation(out=cu[:, 0:127], in_=PSF(pb)[:, 0:127], func=AF.Identity,
                                                             bias=cconst[:, 0:1]), reads=["cconst"], writes=[pk(pb), "cu"])
                            op("dve", lambda e: e.tensor_tensor(out=ct1[:, 0:127], in0=cu[:, 0:127], in1=cu[:, 0:127], op=ALU.mult),
                               reads=["cu"], writes=["ct1"])
                            op("dve", lambda e: e.tensor_scalar(out=ct1[:, 0:127], in0=ct1[:, 0:127], scalar1=0.044715, scalar2=1.0,
                                                                op0=ALU.mult, op1=ALU.add), reads=["ct1"], writes=["ct1"])
                            op("dve", lambda e: e.tensor_tensor(out=ct1[:, 0:127], in0=ct1[:, 0:127], in1=cu[:, 0:127], op=ALU.mult),
                               reads=["ct1", "cu"], writes=["ct1"])
                            op("act", lambda e: e.activation(out=ct2[:, 0:127], in_=ct1[:, 0:127], func=AF.Tanh,
                                                             scale=0.7978845608028654), reads=["ct1"], writes=["ct2"])
                            op("dve", lambda e: e.scalar_tensor_tensor(out=ct2[:, 0:127], in0=ct2[:, 0:127], scalar=1.0,
                                                                       in1=cu[:, 0:127], op0=ALU.add, op1=ALU.mult),
                               reads=["ct2", "cu"], writes=["ct2"])
                            op("dve", lambda e: e.tensor_scalar(out=cgel[:, 0:127], in0=ct2[:, 0:127], scalar1=0.5, scalar2=None,
                                                                op0=ALU.mult), reads=["ct2"], writes=["cgel"])
                            pb2 = pr_rot.next()
                            if kv == 0:
                                op("pe", lambda e: e.matmul(PSF(pb2)[g * 64:(g + 1) * 64, 0:127], lhsT=w2[:, :], rhs=cgel[:, 0:127],
                                                            start=True, stop=True), reads=["cgel", "cmpw"], writes=[pk(pb2)])
                                op("act", lambda e: e.copy(out=kcT2[g * 64:(g + 1) * 64, 0:127],
                                                           in_=PSF(pb2)[g * 64:(g + 1) * 64, 0:127]), writes=[pk(pb2), "kcT2"])
                            else:
                                op("pe", lambda e: e.matmul(PSF(pb2)[0:127, 0:64], lhsT=cgel[:, 0:127], rhs=w2[:, :],
                                                            start=True, stop=True), reads=["cgel", "cmpw"], writes=[pk(pb2)])
                                op("act", lambda e: e.copy(out=vcs[g][0:127, :], in_=PSF(pb2)[0:127, 0:64]),
                                   writes=[pk(pb2), "vcs%d" % g])
                    qn_all = SB("qn_all", [128, 128], F32, c1)
                    for g in range(2):
                        op("act", lambda e, g=g: e.activation(out=qn_all[:, g * 64:(g + 1) * 64], in_=PSF(7 - g)[:, g * 64:(g + 1) * 64],
                                                              func=AF.Sqrt, scale=1.0 / 64.0), writes=[pk(7 - g), "qn_all"])
                    ksq = SB("ksq", [128, S], BF16, c1)
                    km = SB("km", [128, 24], F32, c1)
                    km2 = SB("km2", [128, 8], F32, c1)
                    op("dve", lambda e: e.memset(km[:], 0.0), writes=["km"])
                    for bi, (kT_, kk_, ncol) in enumerate(((kcT2, "kcT2", 128), (kselT, "kselT", S), (kwinT, "kwinT", S)) if not SKIP2 else ()):
                        op("act", lambda e, kT_=kT_, ncol=ncol: e.activation(out=ksq[:, 0:ncol], in_=kT_[:, 0:ncol], func=AF.Square),
                           reads=[kk_], writes=["ksq"])
                        for g in range(2):
                            for c in range((ncol + 511) // 512):
                                w = min(512, ncol - c * 512)
                                pb = pr_rot.next()
                                op("pe", lambda e, g=g, c=c, w=w, pb=pb: e.matmul(
                                    PSF(pb)[:, 0:w], lhsT=ones_bf[g * 64:(g + 1) * 64, :], rhs=ksq[g * 64:(g + 1) * 64, c * 512:c * 512 + w],
                                    start=True, stop=True), reads=["ksq", "ones_bf"], writes=[pk(pb)])
                                idx = (bi * 2 + g) * 4 + c
                                op("dve", lambda e, pb=pb, w=w, idx=idx: e.tensor_reduce(out=km[:, idx:idx + 1], in_=PSF(pb)[:, 0:w],
                                                                                       axis=AX.X, op=ALU.max), writes=[pk(pb), "km"])
                    op("dve", lambda e: e.tensor_reduce(out=km2[:, 0:6], in_=km[:].rearrange("p (a b) -> p a b", b=4), axis=AX.X, op=ALU.max),
                       reads=["km"], writes=["km2"])
                    op("act", lambda e: e.activation(out=km2[:, 0:6], in_=km2[:, 0:6], func=AF.Sqrt), reads=["km2"], writes=["km2"])
                    for bi in range(3):
                        for h in range(8):
                            kc_i = bi * 2 + h // 4
                            op("dve", lambda e, bi=bi, h=h, kc_i=kc_i: e.tensor_scalar(
                                out=negb[:, bi * 128 + h * 16:bi * 128 + (h + 1) * 16], in0=qn_all[:, h * 16:(h + 1) * 16],
                                scalar1=km2[:, kc_i:kc_i + 1], scalar2=-1.0, op0=ALU.mult, op1=ALU.mult),
                               reads=["qn_all", "km2"], writes=["negb"])
                    if s == 0:
                        dbg("negb", negb[:], [128, 384], ["negb"])
                        dbg("kcT2", kcT2[:], [128, 128], ["kcT2"])
                        dbg("vcs0", vcs[0][:], [128, 64], ["vcs0"])
                        dbg("vcs1", vcs[1][:], [128, 64], ["vcs1"])
                        dbg("gates", gates[:], [128, NT * 24], ["gates"])
                    for e in ("pe", "act", "dve", "pool", "sp"):
                        bld.finish(e)
                if stop_after == "C1":
                    return
                with ExitStack() as c2:
                    tsel = SB("tsel", [128, 2048], F32, c2)
                    twin = SB("twin", [128, 640], F32, c2)
                    tcmp = SB("tcmp", [128, 16 * 128], F32, c2)
                    selmul = SB("selmul", [128, 16 * 32], F32, c2)
                    seladd = SB("seladd", [128, 16 * 32], F32, c2)
                    rv0 = SB("rv0", [128, 1], F32, c2)
                    dma("sp", [(tsel[:], tsel_d), (twin[:], twin_d), (tcmp[:], tcmp_d), (selmul[:], selmul_d),
                               (seladd[:], seladd_d), (rv0[:], rv0_d)], writes=["tabs"])
                    pp = SB("pp", [128, 132], F32, c2)
                    op("dve", lambda e: e.memset(pp[:], 0.0), writes=["pp"])
                    W_rot = Rot([(SB("Ws%d" % i, [128, 2048], F32, c2), "Ws%d" % i) for i in range(2)])
                    Es_rot = Rot([(SB("Es%d" % i, [128, 2048], BF16, c2), "Es%d" % i) for i in range(3)])
                    Ew_rot = Rot([(SB("Ew%d" % i, [128, 640], BF16, c2), "Ew%d" % i) for i in range(3)])
                    Eb_rot = Rot([(SB("Eb%d" % i, [128, 128], BF16, c2), "Eb%d" % i) for i in range(4)])
                    PT_rot = Rot([(SB("PTs%d" % i, [128, 2048], BF16, c2), "PTs%d" % i) for i in range(2)])
                    Ec_rot = Rot([(SB("Ec%d" % i, [128, 128], F32, c2), "Ec%d" % i) for i in range(2)])
                    sm_rot = Rot([(SB("smc%d" % i, [128, 8], F32, c2), "smc%d" % i) for i in range(6)])
                    Dm_rot = Rot([(SB("Dm%d" % i, [128, 128], BF16, c2), "Dm%d" % i) for i in range(4)])
                    selb_rot = Rot([(SB("selb%d" % i, [128, 32], F32, c2), "selb%d" % i) for i in range(2)])
                    pr1 = SB("pr1", [128, 32], F32, c2)
                    pr2 = SB("pr2", [128, 32], F32, c2)
                    m8 = SB("m8", [128, 8], F32, c2)
                    z_rot = Rot([0, 1, 2, 3])
                    t_rot = Rot([4, 5])
                    print("[build] C2 sbuf bytes remaining/partition:", nc.sbuf_bytes_remaining)
                    units = []
                    for g in range(2):
                        for i in range(NT):
                            for r in range(4):
                                units.append(dict(g=g, i=i, r=r, br="cmp"))
                            for br_ in ("win", "sel"):
                                if br_ in branches:
                                    for r in range(4):
                                        units.append(dict(g=g, i=i, r=r, br=br_))
                    cur_selb = {}
                    nrv0 = SB("nrv0", [128, 1], F32, c2)
                    op("dve", lambda e: e.tensor_scalar(out=nrv0[:], in0=rv0[:], scalar1=-1.0, scalar2=1.0, op0=ALU.mult, op1=ALU.add),
                       reads=["tabs"], writes=["nrv0"])
                    g0v = bass.AP(gates, 0, [[NT * 24, 128], [3, 8]])
                    op("dve", lambda e: e.tensor_scalar(out=g0v, in0=g0v, scalar1=rv0[:, 0:1], scalar2=None, op0=ALU.mult),
                       reads=["gates", "tabs"], writes=["gates"])

                    def chunks_of(u):
                        i = u["i"]
                        if u["br"] == "cmp":
                            return [(0, 128, None)]
                        if u["br"] == "win":
                            nblk = min(i, 4) + 1
                            k0 = 128 * (i - nblk + 1)
                            tc0 = 640 - 128 * nblk
                            res = []
                            c = 0
                            while c < 128 * nblk:
                                w = min(512, 128 * nblk - c)
                                res.append((k0 + c, w, tc0 + c))
                                c += w
                            return res
                        res = []
                        for j in range(i // 4 + 1):
                            w = min(512, 128 * (i + 1) - 512 * j)
                            res.append((512 * j, w, 512 * j - 128 * i + 1920))
                        return res

                    def s1(u):
                        g, i, r, br = u["g"], u["i"], u["r"], u["br"]
                        rows = slice(g * 64, (g + 1) * 64)
                        kT, kk = {"cmp": (kcT2, "kcT2"), "win": (kwinT, "kwinT"), "sel": (kselT, "kselT")}[br]
                        u["zb"] = []
                        for (k0, w, tc0) in chunks_of(u):
                            zb = z_rot.next()
                            u["zb"].append(zb)
                            op("pe", lambda e, zb=zb, k0=k0, w=w: e.matmul(
                                PSF(zb)[:, 0:w], lhsT=qA[r][rows, i * 128:(i + 1) * 128], rhs=kT[rows, k0:k0 + w],
                                start=True, stop=True), reads=["qA%d" % r, kk], writes=[pk(zb)])

                    def sA(u):
                        g, i, r, br = u["g"], u["i"], u["r"], u["br"]
                        h = g * 4 + r
                        slope = 2.0 ** (-(h + 1))
                        chs = chunks_of(u)
                        if br == "cmp":
                            W, Wk = Ec_rot.next()
                            tab = tcmp[:, i * 128:(i + 1) * 128]
                            zb = u["zb"][0]
                            op("dve", lambda e: e.scalar_tensor_tensor(out=W[:, 0:128], in0=tab, scalar=slope, in1=PSF(zb)[:, 0:128],
                                                                       op0=ALU.mult, op1=ALU.add), reads=["tabs"], writes=[pk(zb), Wk])
                            width = 128
                        else:
                            W, Wk = W_rot.next()
                            off = 0
                            for (k0, w, tc0), zb in zip(chs, u["zb"]):
                                tab = (twin if br == "win" else tsel)[:, tc0:tc0 + w]
                                op("dve", lambda e, off=off, w=w, tab=tab, zb=zb: e.scalar_tensor_tensor(
                                    out=W[:, off:off + w], in0=tab, scalar=slope, in1=PSF(zb)[:, 0:w], op0=ALU.mult, op1=ALU.add),
                                   reads=["tabs"], writes=[pk(zb), Wk])
                                if br == "sel":
                                    sb_, sbk = cur_selb[(g, i)]
                                    nb_ = w // 64
                                    b0 = k0 // 64
                                    in1 = bass.AP(sb_, b0, [[32, 128], [1, nb_], [0, 64]])
                                    Wv = W[:, off:off + w].rearrange("p (a b) -> p a b", b=64)
                                    op("pool", lambda e, Wv=Wv, in1=in1: e.tensor_tensor(out=Wv, in0=Wv, in1=in1, op=ALU.add),
                                       reads=[sbk], writes=[Wk])
                                off += w
                            width = off
                        u["width"] = width
                        u["W"], u["Wk"] = W, Wk
                        u["Wev"] = bld.last(Wk)

                    def sB(u):
                        br, width, W, Wk = u["br"], u["width"], u["W"], u["Wk"]
                        assert bld.last(Wk) == u["Wev"], "rotation depth too small (W)"
                        sm, smk = sm_rot.next()
                        u["sm"], u["smk"] = sm, smk
                        g_, i_, r_ = u["g"], u["i"], u["r"]
                        bcol = {"cmp": 0, "sel": 1, "win": 2}[br] * 128 + (g_ * 4 + r_) * 16 + i_
                        if br == "cmp":
                            E, Ek = W, Wk
                        else:
                            E, Ek = (Ew_rot if br == "win" else Es_rot).next()
                        op("act", lambda e: e.activation(out=E[:, 0:width], in_=W[:, 0:width], func=AF.Exp, bias=negb[:, bcol:bcol + 1],
                                                         accum_out=sm[:, 1:2]), reads=[Wk, "negb"], writes=[Ek, smk])
                        if br == "cmp" and i_ == 0:
                            op("dve", lambda e: e.tensor_tensor(out=sm[:, 1:2], in0=sm[:, 1:2], in1=nrv0[:, 0:1], op=ALU.add),
                               reads=[smk, "nrv0"], writes=[smk])
                        u["E32"], u["E32k"] = E, Ek
                        if br == "cmp":
                            Eb, Ebk = Eb_rot.next()
                            op("act", lambda e: e.copy(out=Eb[:, 0:128], in_=E[:, 0:128]), reads=[Ek], writes=[Ebk])
                            E, Ek = Eb, Ebk
                        u["E"], u["Ek"] = E, Ek
                        u["Eev"] = bld.last(Ek)
                        u["smev"] = bld.last(smk)

                    def sC(u):
                        g, i, r, br = u["g"], u["i"], u["r"], u["br"]
                        h = g * 4 + r
                        sm, smk = u["sm"], u["smk"]
                        assert bld.last(smk) == u["smev"], "rotation depth too small (sm)"
                        Dm, Dmk = Dm_rot.next()
                        u["Dm"], u["Dmk"] = Dm, Dmk
                        op("dve", lambda e: e.reciprocal(out=sm[:, 2:3], in_=sm[:, 1:2]), reads=[smk], writes=[smk])
                        bri = {"cmp": 0, "sel": 1, "win": 2}[br]
                        gcol = i * 24 + h * 3 + bri
                        op("pool", lambda e: e.tensor_scalar(out=Dm[:], in0=identf[:], scalar1=sm[:, 2:3], scalar2=gates[:, gcol:gcol + 1],
                                                             op0=ALU.mult, op1=ALU.mult), reads=[smk, "identf", "gates"], writes=[Dmk])
                        u["Dmev"] = bld.last(Dmk)
                        if br == "cmp":
                            E, Ek = u["E32"], u["E32k"]
                            if r == 0:
                                op("dve", lambda e: e.tensor_scalar(out=pp[:, 1:129], in0=E[:, 0:128], scalar1=sm[:, 2:3], scalar2=None,
                                                                    op0=ALU.mult), reads=[Ek, smk], writes=["pp"])
                            else:
                                op("dve", lambda e: e.scalar_tensor_tensor(out=pp[:, 1:129], in0=E[:, 0:128], scalar=sm[:, 2:3],
                                                                           in1=pp[:, 1:129], op0=ALU.mult, op1=ALU.add),
                                   reads=[Ek, smk, "pp"], writes=["pp"])
                            if r == 3:
                                selb, selbk = selb_rot.next()
                                cur_selb[(g, i)] = (selb, selbk)
                                ppv = pp[:, 0:128].rearrange("p (a b) -> p a b", b=4)
                                op("dve", lambda e: e.tensor_reduce(out=pr1[:], in_=ppv, axis=AX.X, op=ALU.add), reads=["pp"], writes=["pr1"])
                                op("dve", lambda e: e.tensor_tensor(out=pr2[:], in0=pr1[:], in1=pp[:, 4:132:4], op=ALU.add),
                                   reads=["pr1", "pp"], writes=["pr2"])
                                op("dve", lambda e: e.tensor_tensor(out=pr2[:], in0=pr2[:], in1=selmul[:, i * 32:(i + 1) * 32], op=ALU.mult),
                                   reads=["pr2", "tabs"], writes=["pr2"])
                                op("dve", lambda e: e.tensor_tensor(out=pr2[:], in0=pr2[:], in1=seladd[:, i * 32:(i + 1) * 32], op=ALU.add),
                                   reads=["pr2", "tabs"], writes=["pr2"])
                                op("dve", lambda e: e.max(out=m8[:], in_=pr2[:]), reads=["pr2"], writes=["m8"])
                                op("dve", lambda e: e.tensor_scalar(out=pr1[:], in0=pr2[:], scalar1=m8[:, 7:8], scalar2=None, op0=ALU.is_ge),
                                   reads=["pr2", "m8"], writes=["pr1"])
                                op("dve", lambda e: e.tensor_scalar(out=selb[:], in0=pr1[:], scalar1=-1.0, scalar2=BIG, op0=ALU.add,
                                                                    op1=ALU.mult), reads=["pr1"], writes=[selbk])

                    def sD(u):
                        E, Ek, Dm, Dmk, width = u["E"], u["Ek"], u["Dm"], u["Dmk"], u["width"]
                        assert bld.last(Ek) == u["Eev"] and bld.last(Dmk) == u["Dmev"], "rotation depth too small"
                        PT, PTk = PT_rot.next()
                        u["PT"], u["PTk"] = PT, PTk
                        nkb = width // 128
                        kb = 0
                        while kb < nkb:
                            n = min(4, nkb - kb)
                            tb = t_rot.next()
                            for q in range(n):
                                op("pe", lambda e, q=q, kb=kb, tb=tb: e.matmul(
                                    PSF(tb)[:, q * 128:(q + 1) * 128], lhsT=E[:, (kb + q) * 128:(kb + q + 1) * 128], rhs=Dm[:],
                                    start=True, stop=True), reads=[Ek, Dmk], writes=[pk(tb)])
                            op("act", lambda e, kb=kb, n=n, tb=tb: e.copy(out=PT[:, kb * 128:(kb + n) * 128], in_=PSF(tb)[:, 0:n * 128]),
                               writes=[pk(tb), PTk])
                            kb += n
                        u["PTev"] = bld.last(PTk)

                    def sE(u):
                        g, i, r, br = u["g"], u["i"], u["r"], u["br"]
                        PT, PTk, width = u["PT"], u["PTk"], u["width"]
                        assert bld.last(PTk) == u["PTev"], "rotation depth too small (PT)"
                        nkb = width // 128
                        ob = 6 + r // 2
                        orow = slice((r % 2) * 64, (r % 2) * 64 + 64)
                        chs = chunks_of(u)
                        for kb in range(nkb):
                            if br == "cmp":
                                lhsT = vcs[g][:, :]
                                vk = "vcs%d" % g
                            else:
                                key0 = chs[0][0] + kb * 128
                                vt = vwin if br == "win" else vsel
                                vk = "vwin" if br == "win" else "vsel"
                                lhsT = vt[:, (key0 // 128) * 128 + g * 64:(key0 // 128) * 128 + g * 64 + 64]
                            op("pe", lambda e, kb=kb, lhsT=lhsT: e.matmul(
                                PSF(ob)[orow, 0:128], lhsT=lhsT, rhs=PT[:, kb * 128:(kb + 1) * 128],
                                start=(br == "cmp" and kb == 0), stop=(br == branches[-1] and kb == nkb - 1)), reads=[PTk, vk], writes=[pk(ob)])
                        if br == branches[-1]:
                            kc_ = 2 * g + r // 2
                            op("dve", lambda e: e.tensor_copy(out=oaT[orow, kc_ * S + i * 128:kc_ * S + (i + 1) * 128],
                                                              in_=PSF(ob)[orow, 0:128]), writes=[pk(ob), "oaT%d" % (i // 4)])

                    pipeline(units, [sE, sD, sC, sB, sA, s1], [5, 4, 3, 2, 1, 0])
                    if s == 0:
                        dbg("oaT", oaT[:], [128, 4 * S], ["oaT%d" % i for i in range(NB)])
                    for e in ("pe", "act", "dve", "pool", "sp"):
                        bld.finish(e)

        def phase_D(s):
            hkeys = ["hT%d" % i for i in range(NB)]
            oaT3 = oaT[:].rearrange("p (c t) -> p c t", c=4)
            obT3 = obT[:].rearrange("p (c t) -> p c t", c=4)
            g1 = gbc[:, (s * 2 + 0) * D:(s * 2 + 1) * D]
            with ExitStack() as pd_:
                wpa = SB("wpa", [128, 4 * D], BF16, pd_)
                wpb = SB("wpb", [128, 4 * D], BF16, pd_)
                wout = SB("wout", [128, KC * D], BF16, pd_)
                wmg = SB("wmg", [128, KC * 2048], BF16, pd_)
                wpa3 = wpa[:].rearrange("p (c n) -> p c n", c=4)
                wpb3 = wpb[:].rearrange("p (c n) -> p c n", c=4)
                wout3 = wout[:].rearrange("p (c n) -> p c n", c=KC)
                wmg3 = wmg[:].rearrange("p (c n) -> p c n", c=KC)
                w_pa_v = w_pa_d.rearrange("(c p) n -> p c n", p=128)
                w_pb_v = w_pb_d.rearrange("(c p) n -> p c n", p=128)
                w_out_v = w_out_d.rearrange("(c p) n -> p c n", p=128)
                dma("pool", [(wpa3[:, c, :], w_pa_v[:, c, :]) for c in range(4)], writes=["wpa"])
                dma("pool", [(wpb3[:, c, :], w_pb_v[:, c, :]) for c in range(4)], writes=["wpb"])
                for hh in range(4):
                    dma("pool", [(wmg3[:, c, hh * 512:(hh + 1) * 512], w_in_v[:, c, C_MG + hh * 512:C_MG + (hh + 1) * 512])
                                 for c in range(KC)], writes=["wmg%d" % hh])
                dma("pool", [(wout3[:, c, :], w_out_v[:, c, :]) for c in range(KC)], writes=["wout"])
                x_rot = Rot([(SB("xd%d" % i, [128, D], F32, pd_), "xd%d" % i) for i in range(6)])
                yT = SB("yT", [128, KC * TB], BF16, pd_)
                yT3 = yT[:].rearrange("p (c t) -> p c t", c=KC)
                s_rot = Rot([(SB("sg%d" % i, [128, 512], F32, pd_), "sg%d" % i) for i in range(4)])
                tmp_rot = Rot([(SB("tmpd%d" % i, [128, 512], F32, pd_), "tmpd%d" % i) for i in range(2)])
                bank_sets = Rot([[0, 1, 2, 3], [4, 5, 6, 7]])
                o_rot = Rot([0, 1, 2, 3, 4, 5, 6, 7])
                print("[build] D sbuf bytes remaining:", nc.sbuf_bytes_remaining)
                for blk in range(NB):
                    t0 = blk * TB
                    xt = []
                    for j in range(4):
                        x_, xk = x_rot.next()
                        tt = blk * 4 + j
                        dma("sp", [(x_[:], x_d[s, tt * 128:(tt + 1) * 128, :])], writes=[xk])
                        xt.append((x_, xk))
                    for fc in range(KC):
                        bA, bB, bC, bD = bank_sets.next()
                        for kc in range(4):
                            op("pe", lambda e, kc=kc: e.matmul(PSF(bA), lhsT=wpa3[:, kc, fc * 128:(fc + 1) * 128],
                                                               rhs=oaT3[:, kc, t0:t0 + TB], start=(kc == 0), stop=(kc == 3)),
                               reads=["wpa", "oaT%d" % blk], writes=[pk(bA)])
                        for kc in range(4):
                            op("pe", lambda e, kc=kc: e.matmul(PSF(bB), lhsT=wpb3[:, kc, fc * 128:(fc + 1) * 128],
                                                               rhs=obT3[:, kc, t0:t0 + TB], start=(kc == 0), stop=(kc == 3)),
                               reads=["wpb", "obT%d" % blk], writes=[pk(bB)])
                        for (bb, c0) in ((bC, fc * 128), (bD, 1024 + fc * 128)):
                            for kc in range(KC):
                                op("pe", lambda e, kc=kc, bb=bb, c0=c0: e.matmul(
                                    PSF(bb), lhsT=wmg3[:, kc, c0:c0 + 128], rhs=hT3[:, kc, t0:t0 + TB],
                                    start=(kc == 0), stop=(kc == KC - 1)), reads=["wmg%d" % (c0 // 512), hkeys[blk]], writes=[pk(bb)])
                        s0, s0k = s_rot.next()
                        s1_, s1k = s_rot.next()
                        op("act", lambda e: e.activation(out=s0[:], in_=PSF(bC), func=AF.Sigmoid), writes=[pk(bC), s0k])
                        op("act", lambda e: e.activation(out=s1_[:], in_=PSF(bD), func=AF.Sigmoid), writes=[pk(bD), s1k])
                        op("dve", lambda e: e.tensor_tensor(out=s0[:], in0=PSF(bA), in1=s0[:], op=ALU.mult), reads=[s0k], writes=[pk(bA), s0k])
                        op("dve", lambda e: e.tensor_tensor(out=s1_[:], in0=PSF(bB), in1=s1_[:], op=ALU.mult), reads=[s1k], writes=[pk(bB), s1k])
                        op("dve", lambda e: e.tensor_tensor(out=yT3[:, fc, :], in0=s0[:], in1=s1_[:], op=ALU.add),
                           reads=[s0k, s1k], writes=["yT"])
                    if s == 0 and blk == 0:
                        dbg("yT", yT[:], [128, KC * TB], ["yT"])
                    for j in range(4):
                        x_, xk = xt[j]
                        for hh in range(2):
                            ob = o_rot.next()
                            for fc in range(KC):
                                op("pe", lambda e, fc=fc: e.matmul(PSF(ob), lhsT=yT3[:, fc, j * 128:(j + 1) * 128],
                                                                   rhs=wout3[:, fc, hh * 512:(hh + 1) * 512],
                                                                   start=(fc == 0), stop=(fc == KC - 1)),
                                   reads=["yT", "wout"], writes=[pk(ob)])
                            tm, tmk = tmp_rot.next()
                            op("dve", lambda e: e.tensor_tensor(out=tm[:], in0=PSF(ob), in1=g1[:, hh * 512:(hh + 1) * 512], op=ALU.mult),
                               reads=["gbc"], writes=[pk(ob), tmk])
                            op("dve", lambda e: e.tensor_tensor(out=x_[:, hh * 512:(hh + 1) * 512], in0=x_[:, hh * 512:(hh + 1) * 512],
                                                                in1=tm[:], op=ALU.add), reads=[tmk, xk], writes=[xk])
                        tt = blk * 4 + j
                        dma("sp", [(out_d[s, tt * 128:(tt + 1) * 128, :], x_[:])], reads=[xk], writes=["xm%d" % tt])
                for e in ("pe", "act", "dve", "pool", "sp"):
                    bld.finish(e)

        def phase_E(s, TBE=1024):
            nsub = TBE // 512
            ntile = TBE // 128
            g2 = gbc[:, (s * 2 + 1) * D:(s * 2 + 2) * D]
            with ExitStack() as pe_:
                wd = SB("wd", [128, NFF * D], BF16, pe_)
                wd3 = wd[:].rearrange("p (c n) -> p c n", c=NFF)
                w_fd_v = w_fd_d.rearrange("(c p) n -> p c n", p=128)
                for c0 in range(0, NFF, 4):
                    n = min(4, NFF - c0)
                    dma("pool", [(wd3[:, c, :], w_fd_v[:, c, :]) for c in range(c0, c0 + n)], writes=["wd%d" % c0])
                wdkeys = ["wd%d" % c0 for c0 in range(0, NFF, 4)]
                w_fg_v = w_fg_d.rearrange("(c p) n -> p c n", p=128)
                w_fu_v = w_fu_d.rearrange("(c p) n -> p c n", p=128)
                wg_rot = Rot([(SB("wg%d" % i, [128, KC * 256], BF16, pe_), "wg%d" % i) for i in range(2)])
                wu_rot = Rot([(SB("wu%d" % i, [128, KC * 256], BF16, pe_), "wu%d" % i) for i in range(2)])
                x_rot = Rot([(SB("xe%d" % i, [128, D], F32, pe_), "xe%d" % i) for i in range(ntile + 2)])
                xn_rot = Rot([(SB("xne%d" % i, [128, D], BF16, pe_), "xne%d" % i) for i in range(5)])
                sm_rot = Rot([(SB("sme%d" % i, [128, 4], F32, pe_), "sme%d" % i) for i in range(6)])
                h2T = SB("h2T", [128, KC * TBE], BF16, pe_)
                h2T3 = h2T[:].rearrange("p (c t) -> p c t", c=KC)
                actT = SB("actT", [128, NFF * TBE], BF16, pe_)
                actT3 = actT[:].rearrange("p (c t) -> p c t", c=NFF)
                sg_rot = Rot([(SB("sge%d" % i, [128, 512], F32, pe_), "sge%d" % i) for i in range(3)])
                tmp_rot = Rot([(SB("tmpe%d" % i, [128, 512], F32, pe_), "tmpe%d" % i) for i in range(2)])
                gu_rot = Rot([0, 1, 2, 3])
                o_rot = Rot([4, 5, 6, 7])
                print("[build] E sbuf bytes remaining:", nc.sbuf_bytes_remaining)
                for blk in range(S // TBE):
                    xt = []
                    for j in range(ntile):
                        x_, xk = x_rot.next()
                        tt = blk * ntile + j
                        dma("sp", [(x_[:], out_d[s, tt * 128:(tt + 1) * 128, :])], reads=["xm%d" % tt], writes=[xk])
                        xt.append((x_, xk))
                    for sub in range(nsub):
                        norm_block_to_T([xt[sub * 4 + j][0][:] for j in range(4)], [xt[sub * 4 + j][1] for j in range(4)],
                                        1, s, h2T3, sub * 512, "h2T%d" % sub, xn_rot, sm_rot, [4, 5, 6, 7])
                    if s == 0 and blk == 0:
                        dbg("h2T", h2T[:], [128, KC * TBE], ["h2T%d" % i for i in range(nsub)])
                    for grp in range(NFF // 2):
                        wg, wgk = wg_rot.next()
                        wu, wuk = wu_rot.next()
                        wg3 = wg[:].rearrange("p (c n) -> p c n", c=KC)
                        wu3 = wu[:].rearrange("p (c n) -> p c n", c=KC)
                        dma("pool", [(wg3[:, c, :], w_fg_v[:, c, grp * 256:(grp + 1) * 256]) for c in range(KC)], writes=[wgk])
                        dma("pool", [(wu3[:, c, :], w_fu_v[:, c, grp * 256:(grp + 1) * 256]) for c in range(KC)], writes=[wuk])
                        for f2 in range(2):
                            ffc = grp * 2 + f2
                            for sub in range(nsub):
                                bG = gu_rot.next()
                                bU = gu_rot.next()
                                for (bb, w3, wk_) in ((bG, wg3, wgk), (bU, wu3, wuk)):
                                    for kc in range(KC):
                                        op("pe", lambda e, kc=kc, bb=bb, w3=w3: e.matmul(
                                            PSF(bb), lhsT=w3[:, kc, f2 * 128:(f2 + 1) * 128], rhs=h2T3[:, kc, sub * 512:(sub + 1) * 512],
                                            start=(kc == 0), stop=(kc == KC - 1)), reads=[wk_, "h2T%d" % sub], writes=[pk(bb)])
                                sg, sgk = sg_rot.next()
                                op("act", lambda e: e.activation(out=sg[:], in_=PSF(bG), func=AF.Silu), writes=[pk(bG), sgk])
                                op("dve", lambda e: e.tensor_tensor(out=actT3[:, ffc, sub * 512:(sub + 1) * 512], in0=PSF(bU), in1=sg[:],
                                                                    op=ALU.mult), reads=[sgk], writes=[pk(bU), "actT%d" % sub])
                    for j in range(ntile):
                        x_, xk = xt[j]
                        sm, smk = sm_rot.next()
                        for hh in range(2):
                            ob = o_rot.next()
                            for ffc in range(NFF):
                                op("pe", lambda e, ffc=ffc: e.matmul(PSF(ob), lhsT=actT3[:, ffc, j * 128:(j + 1) * 128],
                                                                     rhs=wd3[:, ffc, hh * 512:(hh + 1) * 512],
                                                                     start=(ffc == 0), stop=(ffc == NFF - 1)),
                                   reads=["actT%d" % (j // 4), wdkeys[ffc // 4]], writes=[pk(ob)])
                            tm, tmk = tmp_rot.next()
                            op("dve", lambda e: e.tensor_tensor(out=tm[:], in0=PSF(ob), in1=g2[:, hh * 512:(hh + 1) * 512], op=ALU.mult),
                               reads=["gbc"], writes=[pk(ob), tmk])
                            op("dve", lambda e: e.tensor_tensor(out=x_[:, hh * 512:(hh + 1) * 512], in0=x_[:, hh * 512:(hh + 1) * 512],
                                                                in1=tm[:], op=ALU.add), reads=[tmk, xk], writes=[xk])
                        xn, xnk = xn_rot.next()
                        op("act", lambda e: e.activation(out=xn[:], in_=x_[:], func=AF.Square, accum_out=sm[:, 0:1]),
                           reads=[xk], writes=[xnk, smk])
                        op("act", lambda e: e.activation(out=sm[:, 1:2], in_=sm[:, 0:1], func=AF.Sqrt, scale=1.0 / D, bias=epsb[:, 0:1]),
                           reads=[smk, "epsb"], writes=[smk])
                        op("dve", lambda e: e.reciprocal(out=sm[:, 2:3], in_=sm[:, 1:2]), reads=[smk], writes=[smk])
                        op("dve", lambda e: e.scalar_tensor_tensor(out=x_[:], in0=x_[:], scalar=sm[:, 2:3], in1=gfin[:], op0=ALU.mult,
                                                                   op1=ALU.mult), reads=[xk, smk, "gfin"], writes=[xk])
                        tt = blk * ntile + j
                        dma("sp", [(out_d[s, tt * 128:(tt + 1) * 128, :], x_[:])], reads=[xk], writes=["xo%d" % tt])
                for e in ("pe", "act", "dve", "pool", "sp"):
                    bld.finish(e)

        epsb = SB("epsb", [128, 1], F32)
        op("dve", lambda e: e.memset(epsb[:], EPS), writes=["epsb"])

        for s in range(nseq):
          with ExitStack() as seqscope:
            hT = SB("hT", [128, KC * S], BF16, seqscope)
            hT3 = hT[:].rearrange("p (c t) -> p c t", c=KC)
            obT = SB("obT", [128, 4 * S], BF16, seqscope)
            oaT = SB("oaT", [128, 4 * S], BF16, seqscope)
            with ExitStack() as pa:
                xb = [SB("xa%d" % i, [128, D], F32, pa) for i in range(6)]
                xnb = [SB("xna%d" % i, [128, D], BF16, pa) for i in range(6)]
                smb = [SB("sma%d" % i, [128, 4], F32, pa) for i in range(6)]
                x_rot = Rot([(t, "xa%d" % i) for i, t in enumerate(xb)])
                xn_rot = Rot([(t, "xna%d" % i) for i, t in enumerate(xnb)])
                sm_rot = Rot([(t, "sma%d" % i) for i, t in enumerate(smb)])
                for blk in range(NB):
                    tiles = []
                    keys = []
                    for j in range(4):
                        xt, xk = x_rot.next()
                        tt = blk * 4 + j
                        dma("sp", [(xt[:], x_d[s, tt * 128:(tt + 1) * 128, :])], writes=[xk])
                        tiles.append(xt[:])
                        keys.append(xk)
                    norm_block_to_T(tiles, keys, 0, s, hT3, blk * TB, "hT%d" % blk, xn_rot, sm_rot, [0, 1, 2, 3])
                if s == 0:
                    dbg("hT", hT[:], [128, KC * S], ["hT%d" % i for i in range(NB)])
                for e in ("pe", "act", "dve", "pool", "sp"):
                    bld.finish(e)
            if stop_after == "A":
                break
            phase_B(s)
            if stop_after == "B":
                break
            phase_C(s)
            if stop_after in ("C", "C1"):
                break
            phase_D(s)
            if stop_after == "D":
                break
          phase_E(s)

        for e in ("sp",):
            bld.finish(e)
    print("[build] instructions=%d waits=%d" % (bld.ninst, bld.nwaits))
    return nc, dbg_outs


NEGD = -1.0e9


def _tsel():
    p = np.arange(128)[:, None].astype(np.float64)
    u = np.arange(2048)[None, :].astype(np.float64)
    rel = u - 1920.0
    t = np.where(rel <= p, rel - p, NEGD)
    return np.ascontiguousarray(t.astype(np.float32))


def _twin():
    p = np.arange(128)[:, None].astype(np.float64)
    c = np.arange(640)[None, :].astype(np.float64)
    dist = p + 512.0 - c
    t = np.where((dist >= 0) & (dist < 512), -dist, NEGD)
    return np.ascontiguousarray(t.astype(np.float32))


def _tcmp():
    out = np.full((128, 16, 128), NEGD, np.float64)
    p = np.arange(128)[:, None]
    n = np.arange(127)[None, :]
    for i in range(16):
        t = 128 * i + p
        dist = t - (16 * n + 31)
        out[:, i, :127] = np.where(dist >= 0, -dist, NEGD)
    return np.ascontiguousarray(out.reshape(128, 16 * 128).astype(np.float32))


def _selt():
    mul = np.zeros((128, 16, 32), np.float32)
    add = np.zeros((128, 16, 32), np.float32)
    j = np.arange(32)[None, :]
    for i in range(16):
        cur = (128 * i + np.arange(128)[:, None]) // 64
        valid = j <= cur
        forced = (j == 0) | (j == cur) | (j == cur - 1)
        mul[:, i, :] = (valid & ~forced).astype(np.float32)
        add[:, i, :] = np.where(valid, np.where(forced, 1.0e4, 0.0), -1.0e30).astype(np.float32)
    return np.ascontiguousarray(mul.reshape(128, 512)), np.ascontiguousarray(add.reshape(128, 512))


def _w1(w1):
    w = np.asarray(w1, np.float32).reshape(32, 64, 128).transpose(1, 0, 2).reshape(64, 32 * 128)
    return np.ascontiguousarray(np.concatenate([w, w], axis=0))


def host_inputs(inputs, nseq=2, ncores=NCORES):
    f = lambda a: np.ascontiguousarray(np.asarray(a, dtype=np.float32))
    x = f(inputs["x"])
    c = f(inputs["c"])
    common = {
        "w_ada": f(inputs["w_ada"][0]),
        "b_ada": f(inputs["b_ada"][0]).reshape(1, -1),
        "gmixT": f(f(inputs["norm_mix_g"][0]).reshape(KC, 128).T),
        "gffnT": f(f(inputs["norm_ffn_g"][0]).reshape(KC, 128).T),
        "gfin": f(inputs["norm_final_g"]).reshape(1, -1),
        "w_in": f(inputs["w_in"][0]),
        "w_proj_a": f(inputs["w_proj_a"][0]),
        "w_proj_b": f(inputs["w_proj_b"][0]),
        "w_out": f(inputs["w_out"][0]),
        "w_ffn_gate": f(inputs["w_ffn_gate"][0]),
        "w_ffn_up": f(inputs["w_ffn_up"][0]),
        "w_ffn_down": f(inputs["w_ffn_down"][0]),
        "ident": np.eye(128, dtype=np.float32),
        "sel2": f(np.concatenate([np.repeat(np.array([[1.0], [0.0]]), 128, axis=1),
                                  np.repeat(np.array([[0.0], [1.0]]), 128, axis=1)], axis=1)),
        "i2": np.eye(2, dtype=np.float32),
        "tri_lt": f(np.tril(np.ones((128, 128)), -1)),
        "tsel": _tsel(), "twin": _twin(), "tcmp": _tcmp(), "selmul": _selt()[0], "seladd": _selt()[1],
        "rv0": f((np.arange(128) >= 31).astype(np.float32).reshape(128, 1)),
        "w1k": _w1(inputs["cmp_w1_k"][0]), "w1v": _w1(inputs["cmp_w1_v"][0]),
        "posk": f(np.tile(f(inputs["cmp_pos_k"][0]).T, (2, 1))), "posv": f(np.tile(f(inputs["cmp_pos_v"][0]).T, (2, 1))),
        "w2k": f(inputs["cmp_w2_k"][0]), "w2v": f(inputs["cmp_w2_v"][0]),
        "tri_ge": f(np.triu(np.ones((128, 128)), 0)),
    }
    maps = []
    for k in range(ncores):
        xs = x[k * nseq:(k + 1) * nseq]
        cb = c[k * nseq:(k + 1) * nseq]
        csT = np.zeros((128, 16), np.float32)
        ct = cb.reshape(nseq, KC, 128).transpose(2, 1, 0)
        csT.reshape(128, KC, 2)[:, :, :nseq] = ct
        m = dict(common)
        m["x"] = f(xs)
        m["csT"] = csT
        maps.append(m)
    return maps


def kernel(**inputs):
    nseq = 2
    nc, _ = build(nseq=nseq)
    maps = host_inputs(inputs, nseq=nseq, ncores=NCORES)
    res = run_bass_kernel_spmd(nc, maps, core_ids=list(range(NCORES)))
    out = np.concatenate([r["out"] for r in res.results], axis=0)
    return out.astype(np.float32)
```

```python
import numpy as np
from contextlib import ExitStack
import concourse.bass as bass
import concourse.mybir as mybir
from concourse.bass_utils import run_bass_kernel_spmd

F32 = mybir.dt.float32
BF16 = mybir.dt.bfloat16
AF = mybir.ActivationFunctionType
ALU = mybir.AluOpType
AX = mybir.AxisListType

S = 2048
D = 1024
KC = 8
NT = 16
TB = 512
NB = S // TB
DIN = 4888
DFF = 2816
NFF = 22
EPS = 1e-6
NCORES = 8
C_QA = 0
C_KV = 512
C_GA = 1280
C_SB = 1304
C_MG = 2840
BIG = 1.0e9
SUB_ENG = "dve"
import os
SKIP1 = bool(int(os.environ.get("SKIP1", "0")))
SKIP2 = bool(int(os.environ.get("SKIP2", "0")))


class Builder:
    def __init__(self, nc, es, n_dma_sp=20, n_dma_pool=20):
        self.nc = nc
        self.es = es
        self.engs = {"pe": nc.tensor, "act": nc.scalar, "dve": nc.vector, "pool": nc.gpsimd, "sp": nc.sync}
        self.sem = {}
        self.cnt = {}
        for e in ("pe", "act", "dve", "pool"):
            self.sem[e] = es.enter_context(nc.semaphore("c_" + e))
            self.cnt[e] = 0
        self.dsems = {"sp": [], "pool": []}
        for q, n in (("sp", n_dma_sp), ("pool", n_dma_pool)):
            for i in range(n):
                name = "d_%s%d" % (q, i)
                self.sem[name] = es.enter_context(nc.semaphore(name))
                self.cnt[name] = 0
                self.dsems[q].append(name)
        self.drr = {"sp": 0, "pool": 0}
        self.seen = {e: {} for e in self.engs}
        self.res = {}
        self.nwaits = 0
        self.ninst = 0

    def _need(self, eng, reads, writes):
        need = {}

        def add(ev, same_ok):
            if ev is None:
                return
            s, v = ev
            if same_ok and s == eng:
                return
            if need.get(s, 0) < v:
                need[s] = v

        for k in reads:
            r = self.res.get(k)
            if r is not None:
                add(r[0], False)
        for k in writes:
            r = self.res.get(k)
            if r is not None:
                add(r[0], True)
                for s, v in r[1].items():
                    add((s, v), True)
        out = []
        seen = self.seen[eng]
        for s, v in need.items():
            if seen.get(s, 0) < v:
                out.append((s, v))
                seen[s] = v
        return out

    def _emit_waits(self, eng, waits):
        e = self.engs[eng]
        for s, v in waits:
            e.wait_ge(self.sem[s], v)
            self.nwaits += 1

    def _post(self, ev, reads, writes):
        for k in reads:
            r = self.res.get(k)
            if r is None:
                r = [None, {}]
                self.res[k] = r
            if r[1].get(ev[0], 0) < ev[1]:
                r[1][ev[0]] = ev[1]
        for k in writes:
            self.res[k] = [ev, {}]

    def last(self, key):
        r = self.res.get(key)
        return None if r is None else r[0]

    def op(self, eng, fn, reads=(), writes=()):
        waits = self._need(eng, reads, writes)
        self._emit_waits(eng, waits)
        ins = fn(self.engs[eng])
        self.cnt[eng] += 1
        ins.then_inc(self.sem[eng], 1)
        self._post((eng, self.cnt[eng]), reads, writes)
        self.ninst += 1
        return ins

    def dma(self, q, pairs, reads=(), writes=()):
        sems = self.dsems[q]
        i = self.drr[q]
        self.drr[q] = (i + 1) % len(sems)
        name = sems[i]
        waits = self._need(q, reads, writes)
        prev = self.cnt[name]
        if self.seen[q].get(name, 0) < prev:
            waits.append((name, prev))
            self.seen[q][name] = prev
        self._emit_waits(q, waits)
        e = self.engs[q]
        for o, i_ in pairs:
            e.dma_start(out=o, in_=i_).then_inc(self.sem[name], 16)
            self.ninst += 1
        self.cnt[name] = prev + 16 * len(pairs)
        self._post((name, self.cnt[name]), reads, writes)

    def finish(self, eng="sp"):
        e = self.engs[eng]
        for name in self.dsems["sp"] + self.dsems["pool"]:
            if self.cnt[name] > 0:
                e.wait_ge(self.sem[name], self.cnt[name])
        for c in ("pe", "act", "dve", "pool"):
            if self.cnt[c] > 0:
                e.wait_ge(self.sem[c], self.cnt[c])


class Rot:
    def __init__(self, items):
        self.items = items
        self.i = 0

    def next(self):
        it = self.items[self.i]
        self.i = (self.i + 1) % len(self.items)
        return it


def sb_ap(t, off, dims):
    return bass.AP(t, off, dims)


def build(nseq=2, debug=(), stop_after=None, branches=("cmp", "win", "sel")):
    nc = bass.Bass("TRN2", target_bir_lowering=False)
    dbg_outs = {}

    def din(name, shape):
        return nc.dram_tensor(name, list(shape), F32, kind="ExternalInput").ap()

    x_d = din("x", [nseq, S, D])
    csT_d = din("csT", [128, 16])
    w_ada_d = din("w_ada", [D, 6 * D])
    b_ada_d = din("b_ada", [1, 6 * D])
    gmixT_d = din("gmixT", [128, 8])
    gffnT_d = din("gffnT", [128, 8])
    gfin_d = din("gfin", [1, D])
    w_in_d = din("w_in", [D, DIN])
    w_pa_d = din("w_proj_a", [512, D])
    w_pb_d = din("w_proj_b", [512, D])
    w_out_d = din("w_out", [D, D])
    w_fg_d = din("w_ffn_gate", [D, DFF])
    w_fu_d = din("w_ffn_up", [D, DFF])
    w_fd_d = din("w_ffn_down", [DFF, D])
    ident_d = din("ident", [128, 128])
    sel2_d = din("sel2", [2, 256])
    i2_d = din("i2", [2, 2])
    tri_lt_d = din("tri_lt", [128, 128])
    tsel_d = din("tsel", [128, 2048])
    twin_d = din("twin", [128, 640])
    tcmp_d = din("tcmp", [128, 16 * 128])
    selmul_d = din("selmul", [128, 16 * 32])
    seladd_d = din("seladd", [128, 16 * 32])
    rv0_d = din("rv0", [128, 1])
    w1k_d = din("w1k", [128, 32 * 128])
    w1v_d = din("w1v", [128, 32 * 128])
    posk_d = din("posk", [128, 32])
    posv_d = din("posv", [128, 32])
    w2k_d = din("w2k", [128, 64])
    w2v_d = din("w2v", [128, 64])
    tri_ge_d = din("tri_ge", [128, 128])
    out_d = nc.dram_tensor("out", [nseq, S, D], F32, kind="ExternalOutput").ap()

    with ExitStack() as es:
        bld = Builder(nc, es)
        op = bld.op
        dma = bld.dma

        uid = [0]

        def SB(name, shape, dt=F32, stack=es):
            uid[0] += 1
            return stack.enter_context(nc.sbuf_tensor("s%d_%s" % (uid[0], name), list(shape), dt))

        def dbg(name, ap, shape, reads):
            if name not in debug:
                return
            o = nc.dram_tensor("dbg_" + name, list(shape), ap.dtype, kind="ExternalOutput").ap()
            dbg_outs[name] = o
            dma("sp", [(o, ap)], reads=reads, writes=())

        psum = [es.enter_context(nc.psum_tensor("ps%d" % i, [128, 512], F32)) for i in range(8)]

        def PSF(i):
            return psum[i][:]

        def PSB(i):
            return psum[i][:].bitcast(BF16)

        def pk(i):
            return "ps%d" % i

        identb = SB("identb", [128, 128], BF16)
        identf = SB("identf", [128, 128], F32)
        modT = SB("modT", [128, 64], F32)
        aT = SB("aT", [128, 32], F32)
        shT = SB("shT", [128, 32], F32)
        gbc = SB("gbc", [128, 4 * D], F32)
        gfin = SB("gfin_bc", [128, D], F32)
        tri_lt = SB("tri_lt", [128, 128], F32)
        tri_ge = SB("tri_ge", [128, 128], F32)
        ones512 = SB("ones512", [128, 512], F32)
        w_in_v = w_in_d.rearrange("(c p) n -> p c n", p=128)
        dma("sp", [(tri_lt[:], tri_lt_d), (tri_ge[:], tri_ge_d)], writes=["tri"])
        op("dve", lambda e: e.memset(ones512[:], 1.0), writes=["ones512"])

        dma("sp", [(identf[:], ident_d)], writes=["identf"])
        dma("pool", [(identb[:], ident_d)], writes=["identb"])
        dma("sp", [(gfin[:], bass.AP(gfin_d.tensor, 0, [[0, 128], [1, D]]))], writes=["gfin"])

        with ExitStack() as p0:
            cs = SB("cs", [128, 16], F32, p0)
            gT = SB("gT", [128, 16], F32, p0)
            sel2 = SB("sel2", [2, 256], F32, p0)
            i2 = SB("i2", [2, 2], F32, p0)
            ones12 = SB("ones12", [1, 2], F32, p0)
            modsb = SB("modsb", [2, 6 * D], F32, p0)
            wa = [SB("wa%d" % i, [128, KC * 512], F32, p0) for i in range(2)]
            ba = [SB("ba%d" % i, [1, 512], F32, p0) for i in range(2)]
            dma("sp", [(cs[:], csT_d)], writes=["cs"])
            dma("sp", [(gT[:, 0:8], gmixT_d), (gT[:, 8:16], gffnT_d)], writes=["gT"])
            dma("sp", [(sel2[:], sel2_d), (i2[:], i2_d)], writes=["sel2", "i2"])
            op("dve", lambda e: e.memset(ones12[:], 1.0), writes=["ones12"])
            op("act", lambda e: e.activation(out=cs[:], in_=cs[:], func=AF.Silu), reads=["cs"], writes=["cs"])
            cs3 = cs[:].rearrange("p (c b) -> p c b", b=2)
            w_ada_v = w_ada_d.rearrange("(c p) n -> p c n", p=128)
            for nb in range(12):
                wt = wa[nb % 2]
                bt = ba[nb % 2]
                wk = "wa%d" % (nb % 2)
                wt3 = wt[:].rearrange("p (c n) -> p c n", c=KC)
                dma("sp", [(wt3[:, c, :], w_ada_v[:, c, nb * 512:(nb + 1) * 512]) for c in range(KC)]
                    + [(bt[:], b_ada_d[:, nb * 512:(nb + 1) * 512])], writes=[wk])
                pb = 4 + (nb % 2)
                for c in range(KC):
                    op("pe", lambda e, c=c: e.matmul(psum[pb][0:2, :], lhsT=cs3[:, c, :], rhs=wt3[:, c, :],
                                                     start=(c == 0), stop=False),
                       reads=["cs", wk], writes=[pk(pb)])
                op("pe", lambda e: e.matmul(psum[pb][0:2, :], lhsT=ones12[:], rhs=bt[:], start=False, stop=True),
                   reads=["ones12", wk], writes=[pk(pb)])
                op("act", lambda e: e.copy(out=modsb[:, nb * 512:(nb + 1) * 512], in_=psum[pb][0:2, :]),
                   reads=[], writes=[pk(pb), "modsb"])
            for b in range(nseq):
                for w in range(2):
                    for hh in range(2):
                        col0 = (2048 if w == 0 else 5120) + hh * 512
                        pb = 4 + (hh % 2)
                        op("pe", lambda e, b=b, col0=col0, pb=pb: e.matmul(
                            psum[pb][:], lhsT=sel2[:, b * 128:(b + 1) * 128], rhs=modsb[:, col0:col0 + 512],
                            start=True, stop=True), reads=["sel2", "modsb"], writes=[pk(pb)])
                        o0 = (b * 2 + w) * D + hh * 512
                        op("act", lambda e, pb=pb, o0=o0: e.copy(out=gbc[:, o0:o0 + 512], in_=psum[pb][:]),
                           writes=[pk(pb), "gbc"])
            for j in range(32):
                col0 = [0, 1024, 3072, 4096][j // 8] + (j % 8) * 128
                op("pe", lambda e, j=j, col0=col0: e.matmul(psum[6][:, j * 2:j * 2 + 2], lhsT=modsb[:, col0:col0 + 128],
                                                            rhs=i2[:], start=True, stop=True),
                   reads=["modsb", "i2"], writes=[pk(6)])
            op("dve", lambda e: e.tensor_copy(out=modT[:], in_=psum[6][:, 0:64]), writes=[pk(6), "modT"])
            for w in range(2):
                for b in range(nseq):
                    i0 = (w * 2 + b) * 8
                    src_sh = bass.AP(modT, (w * 16) * 2 + b, [[64, 128], [2, 8]])
                    src_sc = bass.AP(modT, (w * 16 + 8) * 2 + b, [[64, 128], [2, 8]])
                    op("dve", lambda e, i0=i0, src_sh=src_sh: e.tensor_copy(out=shT[:, i0:i0 + 8], in_=src_sh),
                       reads=["modT"], writes=["shT"])
                    op("dve", lambda e, i0=i0, src_sc=src_sc, w=w: e.scalar_tensor_tensor(
                        out=aT[:, i0:i0 + 8], in0=src_sc, scalar=1.0, in1=gT[:, w * 8:(w + 1) * 8],
                        op0=ALU.add, op1=ALU.mult), reads=["modT", "gT"], writes=["aT"])
            dbg("modT", modT[:], [128, 64], ["modT"])
            dbg("aT", aT[:], [128, 32], ["aT"])
            dbg("shT", shT[:], [128, 32], ["shT"])
            dbg("gbc", gbc[:], [128, 4 * D], ["gbc"])
            for e in ("pe", "act", "dve", "pool", "sp"):
                bld.finish(e)

        def norm_block_to_T(xtiles, xkeys, widx, b, dstT3, dcol0, dkey, xn_rot, small_rot, tp_banks):
            xns = []
            for j in range(4):
                xn, xnk = xn_rot.next()
                sm, smk = small_rot.next()
                xt = xtiles[j]
                op("act", lambda e, xt=xt, xn=xn, sm=sm: e.activation(out=xn[:], in_=xt, func=AF.Square,
                                                                      accum_out=sm[:, 0:1]),
                   reads=[xkeys[j]], writes=[xnk, smk])
                op("act", lambda e, sm=sm: e.activation(out=sm[:, 1:2], in_=sm[:, 0:1], func=AF.Sqrt,
                                                        scale=1.0 / D, bias=epsb[:, 0:1]),
                   reads=[smk, "epsb"], writes=[smk])
                op("dve", lambda e, sm=sm: e.reciprocal(out=sm[:, 2:3], in_=sm[:, 1:2]), reads=[smk], writes=[smk])
                op("dve", lambda e, xt=xt, xn=xn, sm=sm: e.tensor_scalar(out=xn[:], in0=xt, scalar1=sm[:, 2:3],
                                                                         scalar2=None, op0=ALU.mult),
                   reads=[xkeys[j], smk], writes=[xnk])
                xns.append((xn, xnk))
            for c in range(KC):
                pb = tp_banks[c // 2]
                for j in range(4):
                    xn, xnk = xns[j]
                    o = PSB(pb)[:, (c % 2) * 512 + j * 128:(c % 2) * 512 + (j + 1) * 128]
                    op("pe", lambda e, o=o, xn=xn, c=c: e.transpose(o, xn[:, c * 128:(c + 1) * 128], identb[:]),
                       reads=[xnk, "identb"], writes=[pk(pb)])
            i0 = (widx * 2 + b) * 8
            for c in range(KC):
                pb = tp_banks[c // 2]
                src = PSB(pb)[:, (c % 2) * 512:(c % 2) * 512 + 512]
                dst = dstT3[:, c, dcol0:dcol0 + 512]
                if c % 2 == 0:
                    op("act", lambda e, src=src, dst=dst, c=c: e.activation(
                        out=dst, in_=src, func=AF.Identity, scale=aT[:, i0 + c:i0 + c + 1],
                        bias=shT[:, i0 + c:i0 + c + 1]), reads=["aT", "shT"], writes=[pk(pb), dkey])
                else:
                    op("dve", lambda e, src=src, dst=dst, c=c: e.tensor_scalar(
                        out=dst, in0=src, scalar1=aT[:, i0 + c:i0 + c + 1], scalar2=shT[:, i0 + c:i0 + c + 1],
                        op0=ALU.mult, op1=ALU.add), reads=["aT", "shT"], writes=[pk(pb), dkey])


        def pipeline(units, stages, lags):
            n = len(units)
            maxlag = max(lags)
            for k in range(n + maxlag):
                for st, lg in zip(stages, lags):
                    u = k - lg
                    if 0 <= u < n:
                        st(units[u])

        def phase_B(s):
            with ExitStack() as pb_:
                wsb = SB("wsb", [128, KC * 1536], BF16, pb_)
                wsb3 = wsb[:].rearrange("p (c n) -> p c n", c=KC)
                v_all = SB("v_all", [128, NT * 512], BF16, pb_)
                qTs = [SB("qTb%d" % i, [128, S], BF16, pb_) for i in range(2)]
                kTs = [SB("kTb%d" % i, [128, S], BF16, pb_) for i in range(2)]
                om_rot = Rot([(SB("om%d" % i, [128, 512], F32, pb_), "om%d" % i) for i in range(4)])
                P_rot = Rot([(SB("P%d" % i, [128, 514], F32, pb_), "P%d" % i) for i in range(6)])
                a_rot = Rot([(SB("a%d" % i, [128, 512], BF16, pb_), "a%d" % i) for i in range(5)])
                aT_rot = Rot([(SB("aT%d" % i, [128, 512], BF16, pb_), "aTs%d" % i) for i in range(4)])
                z_rot = Rot([0, 1, 2])
                t_rot = Rot([3, 4])
                o_rot = Rot([5, 6])
                dma("pool", [(wsb3[:, c, :], w_in_v[:, c, C_SB:C_SB + 1536]) for c in range(KC)], writes=["wsb"])
                hkeys = ["hT%d" % i for i in range(NB)]
                pr_rot = Rot([0, 1, 2, 3])
                for tt in range(NT):
                    pb = pr_rot.next()
                    for c in range(KC):
                        op("pe", lambda e, c=c, pb=pb, tt=tt: e.matmul(
                            PSF(pb), lhsT=hT3[:, c, tt * 128:(tt + 1) * 128], rhs=wsb3[:, c, 1024:1536],
                            start=(c == 0), stop=(c == KC - 1)), reads=[hkeys[tt // 4], "wsb"], writes=[pk(pb)])
                    if tt % 2 == 0:
                        op("act", lambda e, pb=pb, tt=tt: e.copy(out=v_all[:, tt * 512:(tt + 1) * 512], in_=PSF(pb)),
                           writes=[pk(pb), "v_all"])
                    else:
                        op("dve", lambda e, pb=pb, tt=tt: e.tensor_copy(out=v_all[:, tt * 512:(tt + 1) * 512], in_=PSF(pb)),
                           writes=[pk(pb), "v_all"])
                for hp in range(4):
                    qT = qTs[hp % 2]
                    kT = kTs[hp % 2]
                    qk = "qTb%d" % (hp % 2)
                    kk = "kTb%d" % (hp % 2)
                    for blk in range(NB):
                        for which in range(2):
                            pb = pr_rot.next()
                            col0 = which * 512 + hp * 128
                            for c in range(KC):
                                op("pe", lambda e, c=c, pb=pb, col0=col0, blk=blk: e.matmul(
                                    PSF(pb), lhsT=wsb3[:, c, col0:col0 + 128], rhs=hT3[:, c, blk * 512:(blk + 1) * 512],
                                    start=(c == 0), stop=(c == KC - 1)), reads=[hkeys[blk], "wsb"], writes=[pk(pb)])
                            if which == 0:
                                op("act", lambda e, pb=pb, blk=blk, qT=qT: e.activation(
                                    out=qT[:, blk * 512:(blk + 1) * 512], in_=PSF(pb), func=AF.Copy, scale=0.125),
                                   writes=[pk(pb), qk])
                            else:
                                op("dve", lambda e, pb=pb, blk=blk, kT=kT: e.tensor_copy(
                                    out=kT[:, blk * 512:(blk + 1) * 512], in_=PSF(pb)), writes=[pk(pb), kk])
                    units = []
                    for i in range(NT):
                        jmax = i // 4
                        ob = o_rot.next()
                        nun = 2 * (jmax + 1)
                        cnt = 0
                        for j in range(jmax, -1, -1):
                            for a in range(2):
                                w = min(512, 128 * (i + 1) - 512 * j)
                                cnt += 1
                                units.append(dict(i=i, j=j, a=a, w=w, diag=(j == jmax), ob=ob,
                                                  first=(j == jmax), last=(j == 0), final=(cnt == nun)))
                    lastP = {}

                    def s1(u):
                        a, i, j, w = u["a"], u["i"], u["j"], u["w"]
                        zb = z_rot.next()
                        u["zb"] = zb
                        op("pe", lambda e: e.matmul(PSF(zb)[:, 0:w], lhsT=qT[a * 64:(a + 1) * 64, i * 128:(i + 1) * 128],
                                                    rhs=kT[a * 64:(a + 1) * 64, 512 * j:512 * j + w], start=True, stop=True),
                           reads=[qk, kk], writes=[pk(zb)])

                    def sA(u):
                        a, i, j, w, zb = u["a"], u["i"], u["j"], u["w"], u["zb"]
                        om, omk = om_rot.next()
                        u["om"], u["omk"] = om, omk
                        op("act", lambda e: e.activation(out=om[:, 0:w], in_=PSF(zb)[:, 0:w], func=AF.Sigmoid, scale=-1.0),
                           writes=[pk(zb), omk])
                        u["omev"] = bld.last(omk)

                    def s2(u):
                        a, i, j, w, zb = u["a"], u["i"], u["j"], u["w"], u["zb"]
                        om, omk = u["om"], u["omk"]
                        assert bld.last(omk) == u["omev"], "rotation depth too small (om)"
                        P, Pk = P_rot.next()
                        at, ak = a_rot.next()
                        u["at"], u["ak"] = at, ak
                        if u["diag"]:
                            op("dve", lambda e: e.tensor_tensor(out=om[:, w - 128:w], in0=om[:, w - 128:w], in1=tri_ge[:],
                                                                op=ALU.max), reads=["tri"], writes=[omk])
                            op("pool", lambda e: e.memset(P[:, w:w + 1], 1.0), writes=[Pk + "c"])
                            init = 1.0
                            rds = [omk, "ones512"]
                        else:
                            Pp, Ppk = lastP[a]
                            op("pool", lambda e: e.tensor_copy(out=P[:, w:w + 1], in_=Pp[:, 0:1]), reads=[Ppk], writes=[Pk + "c"])
                            init = Pp[:, 0:1]
                            rds = [omk, "ones512", Ppk]
                        rin = bass.AP(om, w - 1, [[512, 128], [-1, w]])
                        rout = bass.AP(P, w - 1, [[514, 128], [-1, w]])
                        op("dve", lambda e: e.tensor_tensor_scan(out=rout, data0=rin, data1=ones512[:, 0:w], initial=init,
                                                                 op0=ALU.mult, op1=ALU.mult), reads=rds, writes=[Pk])
                        op("dve", lambda e: e.tensor_tensor(out=at[:, 0:w], in0=P[:, 1:w + 1], in1=P[:, 0:w], op=ALU.subtract),
                           reads=[Pk, Pk + "c"], writes=[ak])
                        lastP[a] = (P, Pk)
                        u["aev"] = bld.last(ak)

                    def s3(u):
                        w, at, ak = u["w"], u["at"], u["ak"]
                        assert bld.last(ak) == u["aev"], "rotation depth too small"
                        tb = t_rot.next()
                        u["tb"] = tb
                        for kb in range(w // 128):
                            op("pe", lambda e, kb=kb: e.transpose(PSB(tb)[:, kb * 128:(kb + 1) * 128],
                                                                  at[:, kb * 128:(kb + 1) * 128], identb[:]),
                               reads=[ak, "identb"], writes=[pk(tb)])

                    def s4(u):
                        w, tb = u["w"], u["tb"]
                        aTt, aTk = aT_rot.next()
                        u["aT"], u["aTk"] = aTt, aTk
                        op("act", lambda e: e.copy(out=aTt[:, 0:w], in_=PSB(tb)[:, 0:w]), writes=[pk(tb), aTk])
                        u["aTev"] = bld.last(aTk)

                    def s5(u):
                        a, i, j, w, ob = u["a"], u["i"], u["j"], u["w"], u["ob"]
                        aTt, aTk = u["aT"], u["aTk"]
                        assert bld.last(aTk) == u["aTev"], "rotation depth too small (aT)"
                        nkb = w // 128
                        for kb in range(nkb):
                            kt = 4 * j + kb
                            vc0 = kt * 512 + (2 * hp + a) * 64
                            op("pe", lambda e, kb=kb, vc0=vc0: e.matmul(
                                PSF(ob)[a * 64:(a + 1) * 64, 0:128], lhsT=v_all[:, vc0:vc0 + 64],
                                rhs=aTt[:, kb * 128:(kb + 1) * 128],
                                start=(u["first"] and kb == 0), stop=(u["last"] and kb == nkb - 1)),
                               reads=[aTk, "v_all"], writes=[pk(ob)])
                        if u["final"]:
                            op("dve", lambda e: e.tensor_copy(out=obT[:, hp * S + i * 128:hp * S + (i + 1) * 128],
                                                              in_=PSF(ob)[:, 0:128]), writes=[pk(ob), "obT%d" % (i // 4)])

                    pipeline(units, [s5, s3, s4, s2, sA, s1], [4, 3, 3, 2, 1, 0])
                if s == 0:
                    dbg("obT", obT[:], [128, 4 * S], ["obT%d" % i for i in range(NB)])
                for e in ("pe", "act", "dve", "pool", "sp"):
                    bld.finish(e)

        def phase_C(s):
            hkeys = ["hT%d" % i for i in range(NB)]
            with ExitStack() as pc_:
                qA = [SB("qA%d" % r, [128, S], BF16, pc_) for r in range(4)]
                kselT = SB("kselT", [128, S], BF16, pc_)
                kwinT = SB("kwinT", [128, S], BF16, pc_)
                vsel = SB("vsel", [128, NT * 128], BF16, pc_)
                vwin = SB("vwin", [128, NT * 128], BF16, pc_)
                gates = SB("gates", [128, NT * 24], F32, pc_)
                kcT2 = SB("kcT2", [128, 128], BF16, pc_)
                vcs = [SB("vcs%d" % g, [128, 64], BF16, pc_) for g in range(2)]
                negb = SB("negb", [128, 384], F32, pc_)
                pr_rot = Rot([0, 1, 2, 3])
                with ExitStack() as c1:
                    wns = SB("wns", [128, KC * 1304], BF16, c1)
                    wns3 = wns[:].rearrange("p (c n) -> p c n", c=KC)
                    kcmpT = SB("kcmpT", [128, S], BF16, c1)
                    vcmpT = SB("vcmpT", [128, S], BF16, c1)
                    w1k = SB("w1k", [128, 32 * 128], BF16, c1)
                    w1v = SB("w1v", [128, 32 * 128], BF16, c1)
                    posk = SB("posk", [128, 32], BF16, c1)
                    posv = SB("posv", [128, 32], BF16, c1)
                    w2k = SB("w2k", [128, 64], BF16, c1)
                    w2v = SB("w2v", [128, 64], BF16, c1)
                    dma("pool", [(wns3[:, c, :], w_in_v[:, c, 0:1304]) for c in range(KC)], writes=["wns"])
                    dma("pool", [(w1k[:], w1k_d), (w1v[:], w1v_d), (posk[:], posk_d), (posv[:], posv_d),
                                 (w2k[:], w2k_d), (w2v[:], w2v_d)], writes=["cmpw"])
                    for tt in range(NT):
                        pb = pr_rot.next()
                        for c in range(KC):
                            op("pe", lambda e, c=c, pb=pb, tt=tt: e.matmul(
                                PSF(pb)[:, 0:408], lhsT=hT3[:, c, tt * 128:(tt + 1) * 128], rhs=wns3[:, c, 896:1304],
                                start=(c == 0), stop=(c == KC - 1)), reads=[hkeys[tt // 4], "wns"], writes=[pk(pb)])
                        op("act", lambda e, pb=pb, tt=tt: e.copy(out=vsel[:, tt * 128:(tt + 1) * 128], in_=PSF(pb)[:, 0:128]),
                           writes=[pk(pb), "vsel"])
                        op("dve", lambda e, pb=pb, tt=tt: e.tensor_copy(out=vwin[:, tt * 128:(tt + 1) * 128],
                                                                        in_=PSF(pb)[:, 256:384]), writes=[pk(pb), "vwin"])
                        op("act", lambda e, pb=pb, tt=tt: e.activation(out=gates[:, tt * 24:(tt + 1) * 24],
                                                                       in_=PSF(pb)[:, 384:408], func=AF.Sigmoid),
                           writes=[pk(pb), "gates"])
                    for (dst, dk, col0) in ((kcmpT, "kcmpT", 512), (vcmpT, "vcmpT", 640), (kselT, "kselT", 768),
                                            (kwinT, "kwinT", 1024)):
                        for blk in range(NB):
                            pb = pr_rot.next()
                            for c in range(KC):
                                op("pe", lambda e, c=c, pb=pb, blk=blk, col0=col0: e.matmul(
                                    PSF(pb), lhsT=wns3[:, c, col0:col0 + 128], rhs=hT3[:, c, blk * 512:(blk + 1) * 512],
                                    start=(c == 0), stop=(c == KC - 1)), reads=[hkeys[blk], "wns"], writes=[pk(pb)])
                            if blk % 2 == 0:
                                op("act", lambda e, pb=pb, blk=blk, dst=dst: e.copy(out=dst[:, blk * 512:(blk + 1) * 512],
                                                                                   in_=PSF(pb)), writes=[pk(pb), dk])
                            else:
                                op("dve", lambda e, pb=pb, blk=blk, dst=dst: e.tensor_copy(
                                    out=dst[:, blk * 512:(blk + 1) * 512], in_=PSF(pb)), writes=[pk(pb), dk])
                    qsq_rot = Rot([(SB("qsq%d" % i, [128, 512], BF16, c1), "qsq%d" % i) for i in range(2)])
                    ones_bf = SB("ones_bf", [128, 128], BF16, c1)
                    op("dve", lambda e: e.memset(ones_bf[:], 1.0), writes=["ones_bf"])
                    for r in range(4):
                        for blk in range(NB):
                            pb = pr_rot.next()
                            for g in range(2):
                                col0 = (g * 4 + r) * 64
                                for c in range(KC):
                                    op("pe", lambda e, c=c, pb=pb, blk=blk, col0=col0, g=g: e.matmul(
                                        PSF(pb)[g * 64:(g + 1) * 64, :], lhsT=wns3[:, c, col0:col0 + 64],
                                        rhs=hT3[:, c, blk * 512:(blk + 1) * 512], start=(c == 0), stop=(c == KC - 1)),
                                       reads=[hkeys[blk], "wns"], writes=[pk(pb)])
                            op("act", lambda e, pb=pb, blk=blk, r=r: e.activation(
                                out=qA[r][:, blk * 512:(blk + 1) * 512], in_=PSF(pb), func=AF.Copy, scale=0.125),
                               writes=[pk(pb), "qA%d" % r])
                            qs, qsk = qsq_rot.next()
                            op("act", lambda e, pb=pb, qs=qs: e.activation(out=qs[:], in_=PSF(pb), func=AF.Square),
                               writes=[pk(pb), qsk])
                            for j in range(4 if not SKIP1 else 0):
                                for g in range(2):
                                    col = (g * 4 + r) * 16 + blk * 4 + j
                                    op("pe", lambda e, j=j, g=g, col=col, qs=qs: e.matmul(
                                        PSF(7 - g)[:, col:col + 1], lhsT=qs[g * 64:(g + 1) * 64, j * 128:(j + 1) * 128],
                                        rhs=ones_bf[g * 64:(g + 1) * 64, 0:1], start=True, stop=True),
                                       reads=[qsk, "ones_bf"], writes=[pk(7 - g)])
                    cu = SB("cu", [128, 128], F32, c1)
                    ct1 = SB("ct1", [128, 128], F32, c1)
                    ct2 = SB("ct2", [128, 128], F32, c1)
                    cgel = SB("cgel", [128, 128], BF16, c1)
                    cconst = SB("cconst", [128, 1], F32, c1)
                    op("dve", lambda e: e.memset(kcT2[:], 0.0), writes=["kcT2"])
                    for g in range(2):
                        op("dve", lambda e, g=g: e.memset(vcs[g][:], 0.0), writes=["vcs%d" % g])
                    for kv in range(2):
                        xT, xk = (kcmpT, "kcmpT") if kv == 0 else (vcmpT, "vcmpT")
                        w1 = w1k if kv == 0 else w1v
                        pos = posk if kv == 0 else posv
                        w2 = w2k if kv == 0 else w2v
                        for g in range(2):
                            pb = pr_rot.next()
                            rows = slice(g * 64, (g + 1) * 64)
                            for l in range(32):
                                op("pe", lambda e, l=l: e.matmul(PSF(pb)[:, 0:127], lhsT=w1[rows, l * 128:(l + 1) * 128],
                                                                 rhs=xT[rows, l:l + 2017:16], start=(l == 0), stop=(l == 31)),
                                   reads=[xk, "cmpw"], writes=[pk(pb)])
                            for l in range(32):
                                op("pe", lambda e, l=l: e.matmul(PSF(pb)[:, 127:128], lhsT=w1[rows, l * 128:(l + 1) * 128],
                                                                 rhs=pos[rows, l:l + 1], start=(l == 0), stop=(l == 31)),
                                   reads=["cmpw"], writes=[pk(pb)])
                            op("act", lambda e: e.copy(out=cconst[:], in_=PSF(pb)[:, 127:128]), writes=[pk(pb), "cconst"])
                            op("act", lambda e: e.activation(out=cu[:, 0:127], in_=PSF(pb)[:, 0:127], func=AF.Identity,
                                                             bias=cconst[:, 0:1]), reads=["cconst"], writes=[pk(pb), "cu"])
                            op("dve", lambda e: e.tensor_tensor(out=ct1[:, 0:127], in0=cu[:, 0:127], in1=cu[:, 0:127], op=ALU.mult),
                               reads=["cu"], writes=["ct1"])
                            op("dve", lambda e: e.tensor_scalar(out=ct1[:, 0:127], in0=ct1[:, 0:127], scalar1=0.044715, scalar2=1.0,
                                                                op0=ALU.mult, op1=ALU.add), reads=["ct1"], writes=["ct1"])
                            op("dve", lambda e: e.tensor_tensor(out=ct1[:, 0:127], in0=ct1[:, 0:127], in1=cu[:, 0:127], op=ALU.mult),
                               reads=["ct1", "cu"], writes=["ct1"])
                            op("act", lambda e: e.activation(out=ct2[:, 0:127], in_=ct1[:, 0:127], func=AF.Tanh,
                                                             scale=0.7978845608028654), reads=["ct1"], writes=["ct2"])
                            op("dve", lambda e: e.scalar_tensor_tensor(out=ct2[:, 0:127], in0=ct2[:, 0:127], scalar=1.0,
                                                                       in1=cu[:, 0:127], op0=ALU.add, op1=ALU.mult),
                               reads=["ct2", "cu"], writes=["ct2"])
                            op("dve", lambda e: e.tensor_scalar(out=cgel[:, 0:127], in0=ct2[:, 0:127], scalar1=0.5, scalar2=None,
                                                                op0=ALU.mult), reads=["ct2"], writes=["cgel"])
                            pb2 = pr_rot.next()
                            if kv == 0:
                                op("pe", lambda e: e.matmul(PSF(pb2)[g * 64:(g + 1) * 64, 0:127], lhsT=w2[:, :], rhs=cgel[:, 0:127],
                                                            start=True, stop=True), reads=["cgel", "cmpw"], writes=[pk(pb2)])
                                op("act", lambda e: e.copy(out=kcT2[g * 64:(g + 1) * 64, 0:127],
                                                           in_=PSF(pb2)[g * 64:(g + 1) * 64, 0:127]), writes=[pk(pb2), "kcT2"])
                            else:
                                op("pe", lambda e: e.matmul(PSF(pb2)[0:127, 0:64], lhsT=cgel[:, 0:127], rhs=w2[:, :],
                                                            start=True, stop=True), reads=["cgel", "cmpw"], writes=[pk(pb2)])
                                op("act", lambda e: e.copy(out=vcs[g][0:127, :], in_=PSF(pb2)[0:127, 0:64]),
                                   writes=[pk(pb2), "vcs%d" % g])
                    qn_all = SB("qn_all", [128, 128], F32, c1)
                    for g in range(2):
                        op("act", lambda e, g=g: e.activation(out=qn_all[:, g * 64:(g + 1) * 64], in_=PSF(7 - g)[:, g * 64:(g + 1) * 64],
                                                              func=AF.Sqrt, scale=1.0 / 64.0), writes=[pk(7 - g), "qn_all"])
                    ksq = SB("ksq", [128, S], BF16, c1)
                    km = SB("km", [128, 24], F32, c1)
                    km2 = SB("km2", [128, 8], F32, c1)
                    op("dve", lambda e: e.memset(km[:], 0.0), writes=["km"])
                    for bi, (kT_, kk_, ncol) in enumerate(((kcT2, "kcT2", 128), (kselT, "kselT", S), (kwinT, "kwinT", S)) if not SKIP2 else ()):
                        op("act", lambda e, kT_=kT_, ncol=ncol: e.activation(out=ksq[:, 0:ncol], in_=kT_[:, 0:ncol], func=AF.Square),
                           reads=[kk_], writes=["ksq"])
                        for g in range(2):
                            for c in range((ncol + 511) // 512):
                                w = min(512, ncol - c * 512)
                                pb = pr_rot.next()
                                op("pe", lambda e, g=g, c=c, w=w, pb=pb: e.matmul(
                                    PSF(pb)[:, 0:w], lhsT=ones_bf[g * 64:(g + 1) * 64, :], rhs=ksq[g * 64:(g + 1) * 64, c * 512:c * 512 + w],
                                    start=True, stop=True), reads=["ksq", "ones_bf"], writes=[pk(pb)])
                                idx = (bi * 2 + g) * 4 + c
                                op("dve", lambda e, pb=pb, w=w, idx=idx: e.tensor_reduce(out=km[:, idx:idx + 1], in_=PSF(pb)[:, 0:w],
                                                                                       axis=AX.X, op=ALU.max), writes=[pk(pb), "km"])
                    op("dve", lambda e: e.tensor_reduce(out=km2[:, 0:6], in_=km[:].rearrange("p (a b) -> p a b", b=4), axis=AX.X, op=ALU.max),
                       reads=["km"], writes=["km2"])
                    op("act", lambda e: e.activation(out=km2[:, 0:6], in_=km2[:, 0:6], func=AF.Sqrt), reads=["km2"], writes=["km2"])
                    for bi in range(3):
                        for h in range(8):
                            kc_i = bi * 2 + h // 4
                            op("dve", lambda e, bi=bi, h=h, kc_i=kc_i: e.tensor_scalar(
                                out=negb[:, bi * 128 + h * 16:bi * 128 + (h + 1) * 16], in0=qn_all[:, h * 16:(h + 1) * 16],
                                scalar1=km2[:, kc_i:kc_i + 1], scalar2=-1.0, op0=ALU.mult, op1=ALU.mult),
                               reads=["qn_all", "km2"], writes=["negb"])
                    if s == 0:
                        dbg("negb", negb[:], [128, 384], ["negb"])
                        dbg("kcT2", kcT2[:], [128, 128], ["kcT2"])
                        dbg("vcs0", vcs[0][:], [128, 64], ["vcs0"])
                        dbg("vcs1", vcs[1][:], [128, 64], ["vcs1"])
                        dbg("gates", gates[:], [128, NT * 24], ["gates"])
                    for e in ("pe", "act", "dve", "pool", "sp"):
                        bld.finish(e)
                if stop_after == "C1":
                    return
                with ExitStack() as c2:
                    tsel = SB("tsel", [128, 2048], F32, c2)
                    twin = SB("twin", [128, 640], F32, c2)
                    tcmp = SB("tcmp", [128, 16 * 128], F32, c2)
                    selmul = SB("selmul", [128, 16 * 32], F32, c2)
                    seladd = SB("seladd", [128, 16 * 32], F32, c2)
                    rv0 = SB("rv0", [128, 1], F32, c2)
                    dma("sp", [(tsel[:], tsel_d), (twin[:], twin_d), (tcmp[:], tcmp_d), (selmul[:], selmul_d),
                               (seladd[:], seladd_d), (rv0[:], rv0_d)], writes=["tabs"])
                    pp = SB("pp", [128, 132], F32, c2)
                    op("dve", lambda e: e.memset(pp[:], 0.0), writes=["pp"])
                    W_rot = Rot([(SB("Ws%d" % i, [128, 2048], F32, c2), "Ws%d" % i) for i in range(2)])
                    Es_rot = Rot([(SB("Es%d" % i, [128, 2048], BF16, c2), "Es%d" % i) for i in range(2)])
                    Ew_rot = Rot([(SB("Ew%d" % i, [128, 640], BF16, c2), "Ew%d" % i) for i in range(3)])
                    Tm_rot = Rot([(SB("Tm%d" % i, [128, 2048], F32, c2), "Tm%d" % i) for i in range(2)])
                    Eb_rot = Rot([(SB("Eb%d" % i, [128, 128], BF16, c2), "Eb%d" % i) for i in range(4)])
                    PT_rot = Rot([(SB("PTs%d" % i, [128, 2048], BF16, c2), "PTs%d" % i) for i in range(2)])
                    Ec_rot = Rot([(SB("Ec%d" % i, [128, 128], F32, c2), "Ec%d" % i) for i in range(2)])
                    sm_rot = Rot([(SB("smc%d" % i, [128, 8], F32, c2), "smc%d" % i) for i in range(6)])
                    Dm_rot = Rot([(SB("Dm%d" % i, [128, 128], BF16, c2), "Dm%d" % i) for i in range(4)])
                    selb_rot = Rot([(SB("selb%d" % i, [128, 32], F32, c2), "selb%d" % i) for i in range(2)])
                    pr1 = SB("pr1", [128, 32], F32, c2)
                    pr2 = SB("pr2", [128, 32], F32, c2)
                    m8 = SB("m8", [128, 8], F32, c2)
                    z_rot = Rot([0, 1, 2, 3])
                    t_rot = Rot([4, 5])
                    print("[build] C2 sbuf bytes remaining/partition:", nc.sbuf_bytes_remaining)
                    units = []
                    for g in range(2):
                        for i in range(NT):
                            for r in range(4):
                                units.append(dict(g=g, i=i, r=r, br="cmp"))
                            for br_ in ("win", "sel"):
                                if br_ in branches:
                                    for r in range(4):
                                        units.append(dict(g=g, i=i, r=r, br=br_))
                    cur_selb = {}
                    nrv0 = SB("nrv0", [128, 1], F32, c2)
                    op("dve", lambda e: e.tensor_scalar(out=nrv0[:], in0=rv0[:], scalar1=-1.0, scalar2=1.0, op0=ALU.mult, op1=ALU.add),
                       reads=["tabs"], writes=["nrv0"])
                    g0v = bass.AP(gates, 0, [[NT * 24, 128], [3, 8]])
                    op("dve", lambda e: e.tensor_scalar(out=g0v, in0=g0v, scalar1=rv0[:, 0:1], scalar2=None, op0=ALU.mult),
                       reads=["gates", "tabs"], writes=["gates"])

                    def chunks_of(u):
                        i = u["i"]
                        if u["br"] == "cmp":
                            return [(0, 128, None)]
                        if u["br"] == "win":
                            nblk = min(i, 4) + 1
                            k0 = 128 * (i - nblk + 1)
                            tc0 = 640 - 128 * nblk
                            res = []
                            c = 0
                            while c < 128 * nblk:
                                w = min(512, 128 * nblk - c)
                                res.append((k0 + c, w, tc0 + c))
                                c += w
                            return res
                        res = []
                        for j in range(i // 4 + 1):
                            w = min(512, 128 * (i + 1) - 512 * j)
                            res.append((512 * j, w, 512 * j - 128 * i + 1920))
                        return res

                    def s1(u):
                        g, i, r, br = u["g"], u["i"], u["r"], u["br"]
                        rows = slice(g * 64, (g + 1) * 64)
                        kT, kk = {"cmp": (kcT2, "kcT2"), "win": (kwinT, "kwinT"), "sel": (kselT, "kselT")}[br]
                        u["zb"] = []
                        for (k0, w, tc0) in chunks_of(u):
                            zb = z_rot.next()
                            u["zb"].append(zb)
                            op("pe", lambda e, zb=zb, k0=k0, w=w: e.matmul(
                                PSF(zb)[:, 0:w], lhsT=qA[r][rows, i * 128:(i + 1) * 128], rhs=kT[rows, k0:k0 + w],
                                start=True, stop=True), reads=["qA%d" % r, kk], writes=[pk(zb)])

                    def sA(u):
                        g, i, r, br = u["g"], u["i"], u["r"], u["br"]
                        h = g * 4 + r
                        slope = 2.0 ** (-(h + 1))
                        chs = chunks_of(u)
                        if br == "cmp":
                            W, Wk = Ec_rot.next()
                            tab = tcmp[:, i * 128:(i + 1) * 128]
                            zb = u["zb"][0]
                            op("dve", lambda e: e.scalar_tensor_tensor(out=W[:, 0:128], in0=tab, scalar=slope, in1=PSF(zb)[:, 0:128],
                                                                       op0=ALU.mult, op1=ALU.add), reads=["tabs"], writes=[pk(zb), Wk])
                            width = 128
                        else:
                            W, Wk = W_rot.next()
                            off = 0
                            for (k0, w, tc0), zb in zip(chs, u["zb"]):
                                if br == "win":
                                    tab = twin[:, tc0:tc0 + w]
                                    tk = "tabs"
                                else:
                                    Tm, Tmk = cur_selb[(g, i)]
                                    tab = Tm[:, k0:k0 + w]
                                    tk = Tmk
                                op("dve", lambda e, off=off, w=w, tab=tab, zb=zb: e.scalar_tensor_tensor(
                                    out=W[:, off:off + w], in0=tab, scalar=slope, in1=PSF(zb)[:, 0:w], op0=ALU.mult, op1=ALU.add),
                                   reads=[tk], writes=[pk(zb), Wk])
                                off += w
                            width = off
                        u["width"] = width
                        u["W"], u["Wk"] = W, Wk
                        u["Wev"] = bld.last(Wk)

                    def sB(u):
                        br, width, W, Wk = u["br"], u["width"], u["W"], u["Wk"]
                        assert bld.last(Wk) == u["Wev"], "rotation depth too small (W)"
                        sm, smk = sm_rot.next()
                        u["sm"], u["smk"] = sm, smk
                        g_, i_, r_ = u["g"], u["i"], u["r"]
                        bcol = {"cmp": 0, "sel": 1, "win": 2}[br] * 128 + (g_ * 4 + r_) * 16 + i_
                        if br == "cmp":
                            E, Ek = W, Wk
                        else:
                            E, Ek = (Ew_rot if br == "win" else Es_rot).next()
                        op("act", lambda e: e.activation(out=E[:, 0:width], in_=W[:, 0:width], func=AF.Exp, bias=negb[:, bcol:bcol + 1],
                                                         accum_out=sm[:, 1:2]), reads=[Wk, "negb"], writes=[Ek, smk])
                        if br == "cmp" and i_ == 0:
                            op("dve", lambda e: e.tensor_tensor(out=sm[:, 1:2], in0=sm[:, 1:2], in1=nrv0[:, 0:1], op=ALU.add),
                               reads=[smk, "nrv0"], writes=[smk])
                        u["E32"], u["E32k"] = E, Ek
                        if br == "cmp":
                            Eb, Ebk = Eb_rot.next()
                            op("act", lambda e: e.copy(out=Eb[:, 0:128], in_=E[:, 0:128]), reads=[Ek], writes=[Ebk])
                            E, Ek = Eb, Ebk
                        u["E"], u["Ek"] = E, Ek
                        u["Eev"] = bld.last(Ek)
                        u["smev"] = bld.last(smk)

                    def sC(u):
                        g, i, r, br = u["g"], u["i"], u["r"], u["br"]
                        h = g * 4 + r
                        sm, smk = u["sm"], u["smk"]
                        assert bld.last(smk) == u["smev"], "rotation depth too small (sm)"
                        Dm, Dmk = Dm_rot.next()
                        u["Dm"], u["Dmk"] = Dm, Dmk
                        op("dve", lambda e: e.reciprocal(out=sm[:, 2:3], in_=sm[:, 1:2]), reads=[smk], writes=[smk])
                        bri = {"cmp": 0, "sel": 1, "win": 2}[br]
                        gcol = i * 24 + h * 3 + bri
                        op("pool", lambda e: e.tensor_scalar(out=Dm[:], in0=identf[:], scalar1=sm[:, 2:3], scalar2=gates[:, gcol:gcol + 1],
                                                             op0=ALU.mult, op1=ALU.mult), reads=[smk, "identf", "gates"], writes=[Dmk])
                        u["Dmev"] = bld.last(Dmk)
                        if br == "cmp":
                            E, Ek = u["E32"], u["E32k"]
                            if r == 0:
                                op("dve", lambda e: e.tensor_scalar(out=pp[:, 1:129], in0=E[:, 0:128], scalar1=sm[:, 2:3], scalar2=None,
                                                                    op0=ALU.mult), reads=[Ek, smk], writes=["pp"])
                            else:
                                op("dve", lambda e: e.scalar_tensor_tensor(out=pp[:, 1:129], in0=E[:, 0:128], scalar=sm[:, 2:3],
                                                                           in1=pp[:, 1:129], op0=ALU.mult, op1=ALU.add),
                                   reads=[Ek, smk, "pp"], writes=["pp"])
                            if r == 3:
                                selb, selbk = selb_rot.next()
                                ppv = pp[:, 0:128].rearrange("p (a b) -> p a b", b=4)
                                op("dve", lambda e: e.tensor_reduce(out=pr1[:], in_=ppv, axis=AX.X, op=ALU.add), reads=["pp"], writes=["pr1"])
                                op("dve", lambda e: e.tensor_tensor(out=pr2[:], in0=pr1[:], in1=pp[:, 4:132:4], op=ALU.add),
                                   reads=["pr1", "pp"], writes=["pr2"])
                                op("dve", lambda e: e.tensor_tensor(out=pr2[:], in0=pr2[:], in1=selmul[:, i * 32:(i + 1) * 32], op=ALU.mult),
                                   reads=["pr2", "tabs"], writes=["pr2"])
                                op("dve", lambda e: e.tensor_tensor(out=pr2[:], in0=pr2[:], in1=seladd[:, i * 32:(i + 1) * 32], op=ALU.add),
                                   reads=["pr2", "tabs"], writes=["pr2"])
                                op("dve", lambda e: e.max(out=m8[:], in_=pr2[:]), reads=["pr2"], writes=["m8"])
                                op("dve", lambda e: e.tensor_scalar(out=pr1[:], in0=pr2[:], scalar1=m8[:, 7:8], scalar2=None, op0=ALU.is_ge),
                                   reads=["pr2", "m8"], writes=["pr1"])
                                op("dve", lambda e: e.tensor_scalar(out=selb[:], in0=pr1[:], scalar1=-1.0, scalar2=BIG, op0=ALU.add,
                                                                    op1=ALU.mult), reads=["pr1"], writes=[selbk])
                                Tm, Tmk = Tm_rot.next()
                                cur_selb[(g, i)] = (Tm, Tmk)
                                nkeys = 128 * (i + 1)
                                c0 = 0
                                while c0 < nkeys:
                                    w_ = min(512, nkeys - c0)
                                    in1 = bass.AP(selb, c0 // 64, [[32, 128], [1, w_ // 64], [0, 64]])
                                    src = tsel[:, c0 - 128 * i + 1920:c0 - 128 * i + 1920 + w_].rearrange("p (a b) -> p a b", b=64)
                                    dst = Tm[:, c0:c0 + w_].rearrange("p (a b) -> p a b", b=64)
                                    op("pool", lambda e, dst=dst, src=src, in1=in1: e.tensor_tensor(out=dst, in0=src, in1=in1, op=ALU.add),
                                       reads=[selbk, "tabs"], writes=[Tmk])
                                    c0 += w_

                    def sD(u):
                        E, Ek, Dm, Dmk, width = u["E"], u["Ek"], u["Dm"], u["Dmk"], u["width"]
                        assert bld.last(Ek) == u["Eev"] and bld.last(Dmk) == u["Dmev"], "rotation depth too small"
                        PT, PTk = PT_rot.next()
                        u["PT"], u["PTk"] = PT, PTk
                        nkb = width // 128
                        kb = 0
                        while kb < nkb:
                            n = min(4, nkb - kb)
                            tb = t_rot.next()
                            for q in range(n):
                                op("pe", lambda e, q=q, kb=kb, tb=tb: e.matmul(
                                    PSF(tb)[:, q * 128:(q + 1) * 128], lhsT=E[:, (kb + q) * 128:(kb + q + 1) * 128], rhs=Dm[:],
                                    start=True, stop=True), reads=[Ek, Dmk], writes=[pk(tb)])
                            op("act", lambda e, kb=kb, n=n, tb=tb: e.copy(out=PT[:, kb * 128:(kb + n) * 128], in_=PSF(tb)[:, 0:n * 128]),
                               writes=[pk(tb), PTk])
                            kb += n
                        u["PTev"] = bld.last(PTk)

                    def sE(u):
                        g, i, r, br = u["g"], u["i"], u["r"], u["br"]
                        PT, PTk, width = u["PT"], u["PTk"], u["width"]
                        assert bld.last(PTk) == u["PTev"], "rotation depth too small (PT)"
                        nkb = width // 128
                        ob = 6 + r // 2
                        orow = slice((r % 2) * 64, (r % 2) * 64 + 64)
                        chs = chunks_of(u)
                        for kb in range(nkb):
                            if br == "cmp":
                                lhsT = vcs[g][:, :]
                                vk = "vcs%d" % g
                            else:
                                key0 = chs[0][0] + kb * 128
                                vt = vwin if br == "win" else vsel
                                vk = "vwin" if br == "win" else "vsel"
                                lhsT = vt[:, (key0 // 128) * 128 + g * 64:(key0 // 128) * 128 + g * 64 + 64]
                            op("pe", lambda e, kb=kb, lhsT=lhsT: e.matmul(
                                PSF(ob)[orow, 0:128], lhsT=lhsT, rhs=PT[:, kb * 128:(kb + 1) * 128],
                                start=(br == "cmp" and kb == 0), stop=(br == branches[-1] and kb == nkb - 1)), reads=[PTk, vk], writes=[pk(ob)])
                        if br == branches[-1]:
                            kc_ = 2 * g + r // 2
                            op("dve", lambda e: e.tensor_copy(out=oaT[orow, kc_ * S + i * 128:kc_ * S + (i + 1) * 128],
                                                              in_=PSF(ob)[orow, 0:128]), writes=[pk(ob), "oaT%d" % (i // 4)])

                    pipeline(units, [sE, sD, sC, sB, sA, s1], [5, 4, 3, 2, 1, 0])
                    if s == 0:
                        dbg("oaT", oaT[:], [128, 4 * S], ["oaT%d" % i for i in range(NB)])
                    for e in ("pe", "act", "dve", "pool", "sp"):
                        bld.finish(e)

        def phase_D(s):
            hkeys = ["hT%d" % i for i in range(NB)]
            oaT3 = oaT[:].rearrange("p (c t) -> p c t", c=4)
            obT3 = obT[:].rearrange("p (c t) -> p c t", c=4)
            g1 = gbc[:, (s * 2 + 0) * D:(s * 2 + 1) * D]
            with ExitStack() as pd_:
                wpa = SB("wpa", [128, 4 * D], BF16, pd_)
                wpb = SB("wpb", [128, 4 * D], BF16, pd_)
                wout = SB("wout", [128, KC * D], BF16, pd_)
                wmg = SB("wmg", [128, KC * 2048], BF16, pd_)
                wpa3 = wpa[:].rearrange("p (c n) -> p c n", c=4)
                wpb3 = wpb[:].rearrange("p (c n) -> p c n", c=4)
                wout3 = wout[:].rearrange("p (c n) -> p c n", c=KC)
                wmg3 = wmg[:].rearrange("p (c n) -> p c n", c=KC)
                w_pa_v = w_pa_d.rearrange("(c p) n -> p c n", p=128)
                w_pb_v = w_pb_d.rearrange("(c p) n -> p c n", p=128)
                w_out_v = w_out_d.rearrange("(c p) n -> p c n", p=128)
                dma("pool", [(wpa3[:, c, :], w_pa_v[:, c, :]) for c in range(4)], writes=["wpa"])
                dma("pool", [(wpb3[:, c, :], w_pb_v[:, c, :]) for c in range(4)], writes=["wpb"])
                for hh in range(4):
                    dma("pool", [(wmg3[:, c, hh * 512:(hh + 1) * 512], w_in_v[:, c, C_MG + hh * 512:C_MG + (hh + 1) * 512])
                                 for c in range(KC)], writes=["wmg%d" % hh])
                dma("pool", [(wout3[:, c, :], w_out_v[:, c, :]) for c in range(KC)], writes=["wout"])
                x_rot = Rot([(SB("xd%d" % i, [128, D], F32, pd_), "xd%d" % i) for i in range(6)])
                yT = SB("yT", [128, KC * TB], BF16, pd_)
                yT3 = yT[:].rearrange("p (c t) -> p c t", c=KC)
                s_rot = Rot([(SB("sg%d" % i, [128, 512], F32, pd_), "sg%d" % i) for i in range(4)])
                tmp_rot = Rot([(SB("tmpd%d" % i, [128, 512], F32, pd_), "tmpd%d" % i) for i in range(2)])
                bank_sets = Rot([[0, 1, 2, 3], [4, 5, 6, 7]])
                o_rot = Rot([0, 1, 2, 3, 4, 5, 6, 7])
                print("[build] D sbuf bytes remaining:", nc.sbuf_bytes_remaining)
                for blk in range(NB):
                    t0 = blk * TB
                    xt = []
                    for j in range(4):
                        x_, xk = x_rot.next()
                        tt = blk * 4 + j
                        dma("sp", [(x_[:], x_d[s, tt * 128:(tt + 1) * 128, :])], writes=[xk])
                        xt.append((x_, xk))
                    for fc in range(KC):
                        bA, bB, bC, bD = bank_sets.next()
                        for kc in range(4):
                            op("pe", lambda e, kc=kc: e.matmul(PSF(bA), lhsT=wpa3[:, kc, fc * 128:(fc + 1) * 128],
                                                               rhs=oaT3[:, kc, t0:t0 + TB], start=(kc == 0), stop=(kc == 3)),
                               reads=["wpa", "oaT%d" % blk], writes=[pk(bA)])
                        for kc in range(4):
                            op("pe", lambda e, kc=kc: e.matmul(PSF(bB), lhsT=wpb3[:, kc, fc * 128:(fc + 1) * 128],
                                                               rhs=obT3[:, kc, t0:t0 + TB], start=(kc == 0), stop=(kc == 3)),
                               reads=["wpb", "obT%d" % blk], writes=[pk(bB)])
                        for (bb, c0) in ((bC, fc * 128), (bD, 1024 + fc * 128)):
                            for kc in range(KC):
                                op("pe", lambda e, kc=kc, bb=bb, c0=c0: e.matmul(
                                    PSF(bb), lhsT=wmg3[:, kc, c0:c0 + 128], rhs=hT3[:, kc, t0:t0 + TB],
                                    start=(kc == 0), stop=(kc == KC - 1)), reads=["wmg%d" % (c0 // 512), hkeys[blk]], writes=[pk(bb)])
                        s0, s0k = s_rot.next()
                        s1_, s1k = s_rot.next()
                        op("act", lambda e: e.activation(out=s0[:], in_=PSF(bC), func=AF.Sigmoid), writes=[pk(bC), s0k])
                        op("act", lambda e: e.activation(out=s1_[:], in_=PSF(bD), func=AF.Sigmoid), writes=[pk(bD), s1k])
                        op("dve", lambda e: e.tensor_tensor(out=s0[:], in0=PSF(bA), in1=s0[:], op=ALU.mult), reads=[s0k], writes=[pk(bA), s0k])
                        op("dve", lambda e: e.tensor_tensor(out=s1_[:], in0=PSF(bB), in1=s1_[:], op=ALU.mult), reads=[s1k], writes=[pk(bB), s1k])
                        op("dve", lambda e: e.tensor_tensor(out=yT3[:, fc, :], in0=s0[:], in1=s1_[:], op=ALU.add),
                           reads=[s0k, s1k], writes=["yT"])
                    if s == 0 and blk == 0:
                        dbg("yT", yT[:], [128, KC * TB], ["yT"])
                    for j in range(4):
                        x_, xk = xt[j]
                        for hh in range(2):
                            ob = o_rot.next()
                            for fc in range(KC):
                                op("pe", lambda e, fc=fc: e.matmul(PSF(ob), lhsT=yT3[:, fc, j * 128:(j + 1) * 128],
                                                                   rhs=wout3[:, fc, hh * 512:(hh + 1) * 512],
                                                                   start=(fc == 0), stop=(fc == KC - 1)),
                                   reads=["yT", "wout"], writes=[pk(ob)])
                            tm, tmk = tmp_rot.next()
                            op("dve", lambda e: e.tensor_tensor(out=tm[:], in0=PSF(ob), in1=g1[:, hh * 512:(hh + 1) * 512], op=ALU.mult),
                               reads=["gbc"], writes=[pk(ob), tmk])
                            op("dve", lambda e: e.tensor_tensor(out=x_[:, hh * 512:(hh + 1) * 512], in0=x_[:, hh * 512:(hh + 1) * 512],
                                                                in1=tm[:], op=ALU.add), reads=[tmk, xk], writes=[xk])
                        tt = blk * 4 + j
                        dma("sp", [(out_d[s, tt * 128:(tt + 1) * 128, :], x_[:])], reads=[xk], writes=["xm%d" % tt])
                for e in ("pe", "act", "dve", "pool", "sp"):
                    bld.finish(e)

        def phase_E(s, TBE=1024):
            nsub = TBE // 512
            ntile = TBE // 128
            g2 = gbc[:, (s * 2 + 1) * D:(s * 2 + 2) * D]
            with ExitStack() as pe_:
                wd = SB("wd", [128, NFF * D], BF16, pe_)
                wd3 = wd[:].rearrange("p (c n) -> p c n", c=NFF)
                w_fd_v = w_fd_d.rearrange("(c p) n -> p c n", p=128)
                for c0 in range(0, NFF, 4):
                    n = min(4, NFF - c0)
                    dma("pool", [(wd3[:, c, :], w_fd_v[:, c, :]) for c in range(c0, c0 + n)], writes=["wd%d" % c0])
                wdkeys = ["wd%d" % c0 for c0 in range(0, NFF, 4)]
                w_fg_v = w_fg_d.rearrange("(c p) n -> p c n", p=128)
                w_fu_v = w_fu_d.rearrange("(c p) n -> p c n", p=128)
                wg_rot = Rot([(SB("wg%d" % i, [128, KC * 256], BF16, pe_), "wg%d" % i) for i in range(2)])
                wu_rot = Rot([(SB("wu%d" % i, [128, KC * 256], BF16, pe_), "wu%d" % i) for i in range(2)])
                x_rot = Rot([(SB("xe%d" % i, [128, D], F32, pe_), "xe%d" % i) for i in range(ntile + 2)])
                xn_rot = Rot([(SB("xne%d" % i, [128, D], BF16, pe_), "xne%d" % i) for i in range(5)])
                sm_rot = Rot([(SB("sme%d" % i, [128, 4], F32, pe_), "sme%d" % i) for i in range(6)])
                h2T = SB("h2T", [128, KC * TBE], BF16, pe_)
                h2T3 = h2T[:].rearrange("p (c t) -> p c t", c=KC)
                actT = SB("actT", [128, NFF * TBE], BF16, pe_)
                actT3 = actT[:].rearrange("p (c t) -> p c t", c=NFF)
                sg_rot = Rot([(SB("sge%d" % i, [128, 512], F32, pe_), "sge%d" % i) for i in range(3)])
                tmp_rot = Rot([(SB("tmpe%d" % i, [128, 512], F32, pe_), "tmpe%d" % i) for i in range(2)])
                gu_rot = Rot([0, 1, 2, 3])
                o_rot = Rot([4, 5, 6, 7])
                print("[build] E sbuf bytes remaining:", nc.sbuf_bytes_remaining)
                for blk in range(S // TBE):
                    xt = []
                    for j in range(ntile):
                        x_, xk = x_rot.next()
                        tt = blk * ntile + j
                        dma("sp", [(x_[:], out_d[s, tt * 128:(tt + 1) * 128, :])], reads=["xm%d" % tt], writes=[xk])
                        xt.append((x_, xk))
                    for sub in range(nsub):
                        norm_block_to_T([xt[sub * 4 + j][0][:] for j in range(4)], [xt[sub * 4 + j][1] for j in range(4)],
                                        1, s, h2T3, sub * 512, "h2T%d" % sub, xn_rot, sm_rot, [4, 5, 6, 7])
                    if s == 0 and blk == 0:
                        dbg("h2T", h2T[:], [128, KC * TBE], ["h2T%d" % i for i in range(nsub)])
                    for grp in range(NFF // 2):
                        wg, wgk = wg_rot.next()
                        wu, wuk = wu_rot.next()
                        wg3 = wg[:].rearrange("p (c n) -> p c n", c=KC)
                        wu3 = wu[:].rearrange("p (c n) -> p c n", c=KC)
                        dma("pool", [(wg3[:, c, :], w_fg_v[:, c, grp * 256:(grp + 1) * 256]) for c in range(KC)], writes=[wgk])
                        dma("pool", [(wu3[:, c, :], w_fu_v[:, c, grp * 256:(grp + 1) * 256]) for c in range(KC)], writes=[wuk])
                        for f2 in range(2):
                            ffc = grp * 2 + f2
                            for sub in range(nsub):
                                bG = gu_rot.next()
                                bU = gu_rot.next()
                                for (bb, w3, wk_) in ((bG, wg3, wgk), (bU, wu3, wuk)):
                                    for kc in range(KC):
                                        op("pe", lambda e, kc=kc, bb=bb, w3=w3: e.matmul(
                                            PSF(bb), lhsT=w3[:, kc, f2 * 128:(f2 + 1) * 128], rhs=h2T3[:, kc, sub * 512:(sub + 1) * 512],
                                            start=(kc == 0), stop=(kc == KC - 1)), reads=[wk_, "h2T%d" % sub], writes=[pk(bb)])
                                sg, sgk = sg_rot.next()
                                op("act", lambda e: e.activation(out=sg[:], in_=PSF(bG), func=AF.Silu), writes=[pk(bG), sgk])
                                op("dve", lambda e: e.tensor_tensor(out=actT3[:, ffc, sub * 512:(sub + 1) * 512], in0=PSF(bU), in1=sg[:],
                                                                    op=ALU.mult), reads=[sgk], writes=[pk(bU), "actT%d" % sub])
                    for j in range(ntile):
                        x_, xk = xt[j]
                        sm, smk = sm_rot.next()
                        for hh in range(2):
                            ob = o_rot.next()
                            for ffc in range(NFF):
                                op("pe", lambda e, ffc=ffc: e.matmul(PSF(ob), lhsT=actT3[:, ffc, j * 128:(j + 1) * 128],
                                                                     rhs=wd3[:, ffc, hh * 512:(hh + 1) * 512],
                                                                     start=(ffc == 0), stop=(ffc == NFF - 1)),
                                   reads=["actT%d" % (j // 4), wdkeys[ffc // 4]], writes=[pk(ob)])
                            tm, tmk = tmp_rot.next()
                            op("dve", lambda e: e.tensor_tensor(out=tm[:], in0=PSF(ob), in1=g2[:, hh * 512:(hh + 1) * 512], op=ALU.mult),
                               reads=["gbc"], writes=[pk(ob), tmk])
                            op("dve", lambda e: e.tensor_tensor(out=x_[:, hh * 512:(hh + 1) * 512], in0=x_[:, hh * 512:(hh + 1) * 512],
                                                                in1=tm[:], op=ALU.add), reads=[tmk, xk], writes=[xk])
                        xn, xnk = xn_rot.next()
                        op("act", lambda e: e.activation(out=xn[:], in_=x_[:], func=AF.Square, accum_out=sm[:, 0:1]),
                           reads=[xk], writes=[xnk, smk])
                        op("act", lambda e: e.activation(out=sm[:, 1:2], in_=sm[:, 0:1], func=AF.Sqrt, scale=1.0 / D, bias=epsb[:, 0:1]),
                           reads=[smk, "epsb"], writes=[smk])
                        op("dve", lambda e: e.reciprocal(out=sm[:, 2:3], in_=sm[:, 1:2]), reads=[smk], writes=[smk])
                        op("dve", lambda e: e.scalar_tensor_tensor(out=x_[:], in0=x_[:], scalar=sm[:, 2:3], in1=gfin[:], op0=ALU.mult,
                                                                   op1=ALU.mult), reads=[xk, smk, "gfin"], writes=[xk])
                        tt = blk * ntile + j
                        dma("sp", [(out_d[s, tt * 128:(tt + 1) * 128, :], x_[:])], reads=[xk], writes=["xo%d" % tt])
                for e in ("pe", "act", "dve", "pool", "sp"):
                    bld.finish(e)

        epsb = SB("epsb", [128, 1], F32)
        op("dve", lambda e: e.memset(epsb[:], EPS), writes=["epsb"])

        for s in range(nseq):
          with ExitStack() as seqscope:
            hT = SB("hT", [128, KC * S], BF16, seqscope)
            hT3 = hT[:].rearrange("p (c t) -> p c t", c=KC)
            obT = SB("obT", [128, 4 * S], BF16, seqscope)
            oaT = SB("oaT", [128, 4 * S], BF16, seqscope)
            with ExitStack() as pa:
                xb = [SB("xa%d" % i, [128, D], F32, pa) for i in range(6)]
                xnb = [SB("xna%d" % i, [128, D], BF16, pa) for i in range(6)]
                smb = [SB("sma%d" % i, [128, 4], F32, pa) for i in range(6)]
                x_rot = Rot([(t, "xa%d" % i) for i, t in enumerate(xb)])
                xn_rot = Rot([(t, "xna%d" % i) for i, t in enumerate(xnb)])
                sm_rot = Rot([(t, "sma%d" % i) for i, t in enumerate(smb)])
                for blk in range(NB):
                    tiles = []
                    keys = []
                    for j in range(4):
                        xt, xk = x_rot.next()
                        tt = blk * 4 + j
                        dma("sp", [(xt[:], x_d[s, tt * 128:(tt + 1) * 128, :])], writes=[xk])
                        tiles.append(xt[:])
                        keys.append(xk)
                    norm_block_to_T(tiles, keys, 0, s, hT3, blk * TB, "hT%d" % blk, xn_rot, sm_rot, [0, 1, 2, 3])
                if s == 0:
                    dbg("hT", hT[:], [128, KC * S], ["hT%d" % i for i in range(NB)])
                for e in ("pe", "act", "dve", "pool", "sp"):
                    bld.finish(e)
            if stop_after == "A":
                break
            phase_B(s)
            if stop_after == "B":
                break
            phase_C(s)
            if stop_after in ("C", "C1"):
                break
            phase_D(s)
            if stop_after == "D":
                break
          phase_E(s)

        for e in ("sp",):
            bld.finish(e)
    print("[build] instructions=%d waits=%d" % (bld.ninst, bld.nwaits))
    return nc, dbg_outs


NEGD = -1.0e9


def _tsel():
    p = np.arange(128)[:, None].astype(np.float64)
    u = np.arange(2048)[None, :].astype(np.float64)
    rel = u - 1920.0
    t = np.where(rel <= p, rel - p, NEGD)
    return np.ascontiguousarray(t.astype(np.float32))


def _twin():
    p = np.arange(128)[:, None].astype(np.float64)
    c = np.arange(640)[None, :].astype(np.float64)
    dist = p + 512.0 - c
    t = np.where((dist >= 0) & (dist < 512), -dist, NEGD)
    return np.ascontiguousarray(t.astype(np.float32))


def _tcmp():
    out = np.full((128, 16, 128), NEGD, np.float64)
    p = np.arange(128)[:, None]
    n = np.arange(127)[None, :]
    for i in range(16):
        t = 128 * i + p
        dist = t - (16 * n + 31)
        out[:, i, :127] = np.where(dist >= 0, -dist, NEGD)
    return np.ascontiguousarray(out.reshape(128, 16 * 128).astype(np.float32))


def _selt():
    mul = np.zeros((128, 16, 32), np.float32)
    add = np.zeros((128, 16, 32), np.float32)
    j = np.arange(32)[None, :]
    for i in range(16):
        cur = (128 * i + np.arange(128)[:, None]) // 64
        valid = j <= cur
        forced = (j == 0) | (j == cur) | (j == cur - 1)
        mul[:, i, :] = (valid & ~forced).astype(np.float32)
        add[:, i, :] = np.where(valid, np.where(forced, 1.0e4, 0.0), -1.0e30).astype(np.float32)
    return np.ascontiguousarray(mul.reshape(128, 512)), np.ascontiguousarray(add.reshape(128, 512))


def _w1(w1):
    w = np.asarray(w1, np.float32).reshape(32, 64, 128).transpose(1, 0, 2).reshape(64, 32 * 128)
    return np.ascontiguousarray(np.concatenate([w, w], axis=0))


def host_inputs(inputs, nseq=2, ncores=NCORES):
    f = lambda a: np.ascontiguousarray(np.asarray(a, dtype=np.float32))
    x = f(inputs["x"])
    c = f(inputs["c"])
    common = {
        "w_ada": f(inputs["w_ada"][0]),
        "b_ada": f(inputs["b_ada"][0]).reshape(1, -1),
        "gmixT": f(f(inputs["norm_mix_g"][0]).reshape(KC, 128).T),
        "gffnT": f(f(inputs["norm_ffn_g"][0]).reshape(KC, 128).T),
        "gfin": f(inputs["norm_final_g"]).reshape(1, -1),
        "w_in": f(inputs["w_in"][0]),
        "w_proj_a": f(inputs["w_proj_a"][0]),
        "w_proj_b": f(inputs["w_proj_b"][0]),
        "w_out": f(inputs["w_out"][0]),
        "w_ffn_gate": f(inputs["w_ffn_gate"][0]),
        "w_ffn_up": f(inputs["w_ffn_up"][0]),
        "w_ffn_down": f(inputs["w_ffn_down"][0]),
        "ident": np.eye(128, dtype=np.float32),
        "sel2": f(np.concatenate([np.repeat(np.array([[1.0], [0.0]]), 128, axis=1),
                                  np.repeat(np.array([[0.0], [1.0]]), 128, axis=1)], axis=1)),
        "i2": np.eye(2, dtype=np.float32),
        "tri_lt": f(np.tril(np.ones((128, 128)), -1)),
        "tsel": _tsel(), "twin": _twin(), "tcmp": _tcmp(), "selmul": _selt()[0], "seladd": _selt()[1],
        "rv0": f((np.arange(128) >= 31).astype(np.float32).reshape(128, 1)),
        "w1k": _w1(inputs["cmp_w1_k"][0]), "w1v": _w1(inputs["cmp_w1_v"][0]),
        "posk": f(np.tile(f(inputs["cmp_pos_k"][0]).T, (2, 1))), "posv": f(np.tile(f(inputs["cmp_pos_v"][0]).T, (2, 1))),
        "w2k": f(inputs["cmp_w2_k"][0]), "w2v": f(inputs["cmp_w2_v"][0]),
        "tri_ge": f(np.triu(np.ones((128, 128)), 0)),
    }
    maps = []
    for k in range(ncores):
        xs = x[k * nseq:(k + 1) * nseq]
        cb = c[k * nseq:(k + 1) * nseq]
        csT = np.zeros((128, 16), np.float32)
        ct = cb.reshape(nseq, KC, 128).transpose(2, 1, 0)
        csT.reshape(128, KC, 2)[:, :, :nseq] = ct
        m = dict(common)
        m["x"] = f(xs)
        m["csT"] = csT
        maps.append(m)
    return maps


def kernel(**inputs):
    nseq = 2
    nc, _ = build(nseq=nseq)
    maps = host_inputs(inputs, nseq=nseq, ncores=NCORES)
    res = run_bass_kernel_spmd(nc, maps, core_ids=list(range(NCORES)))
    out = np.concatenate([r["out"] for r in res.results], axis=0)
    return out.astype(np.float32)
```
